# Optimizing a Trainium2 kernel written in Bass

```python
import jax, jax.numpy as jnp
from jax import lax
import numpy as np

D_MODEL = 1024
BATCH = 8
SEQ = 2048
DEPTH = 2

GRID_W = 64
CTX_LEN = 256
EPS = 1e-6

HEAD_DIM = 64
N_HEADS = D_MODEL // (2 * HEAD_DIM)
N_KV_HEADS = max(1, N_HEADS // 4)
Q_PER_KV = N_HEADS // N_KV_HEADS
ATTN_W = N_HEADS * HEAD_DIM
KV_W = N_KV_HEADS * HEAD_DIM
Q_BLOCK = 128
ROPE_THETA = 10000.0
ROPE_PAIRS_PER_AXIS = HEAD_DIM // 4
ATTN_SCALE = HEAD_DIM ** -0.5

CHUNK = 128
SGU_W = D_MODEL // 4
SGU_GROUPS = 4
SGU_GROUP_W = SGU_W // SGU_GROUPS

POOL_WINDOWS = (2, 4, 8, 16)
POOL_W = D_MODEL // 4
POOL_GROUPS = len(POOL_WINDOWS)
POOL_GROUP_W = POOL_W // POOL_GROUPS

MIX_W = ATTN_W + SGU_W + POOL_W
O_Q = 0
O_K = O_Q + ATTN_W
O_V = O_K + KV_W
O_G = O_V + KV_W
O_P = O_G + 2 * SGU_W
IN_W = O_P + POOL_W

D_FF = -(-8 * D_MODEL // (3 * 256)) * 256

kernel_name = "hybrid_parallel_attn_sgu_pool_dit"


def rms_norm(x, g):
    xf = x.astype(jnp.float32)
    y = xf * lax.rsqrt(jnp.mean(xf * xf, axis=-1, keepdims=True) + EPS)
    return (y * g.astype(jnp.float32)).astype(x.dtype)


def modulate(x, g, shift, scale):
    return rms_norm(x, g) * (1 + scale[:, None, :]) + shift[:, None, :]


def axial_rope(n_tokens):
    rows_count = n_tokens // GRID_W
    rows = jnp.repeat(jnp.arange(rows_count, dtype=jnp.float32), GRID_W)
    cols = jnp.tile(jnp.arange(GRID_W, dtype=jnp.float32), rows_count)
    inv = ROPE_THETA ** (-jnp.arange(ROPE_PAIRS_PER_AXIS, dtype=jnp.float32) / ROPE_PAIRS_PER_AXIS)
    ang = jnp.concatenate([rows[:, None] * inv, cols[:, None] * inv], axis=-1)
    return jnp.cos(ang), jnp.sin(ang)


def apply_rope(x, cos, sin):
    xf = x.astype(jnp.float32)
    x1, x2 = xf[..., : HEAD_DIM // 2], xf[..., HEAD_DIM // 2:]
    c = cos[None, :, None, :]
    s = sin[None, :, None, :]
    return jnp.concatenate([x1 * c - x2 * s, x2 * c + x1 * s], axis=-1).astype(x.dtype)


def q_heads(zq, q_norm):
    B, N = zq.shape[:2]
    return rms_norm(zq.reshape(B, N, N_HEADS, HEAD_DIM), q_norm)


def kv_heads(zk, zv, k_norm):
    B, N = zk.shape[:2]
    k = rms_norm(zk.reshape(B, N, N_KV_HEADS, HEAD_DIM), k_norm)
    v = zv.reshape(B, N, N_KV_HEADS, HEAD_DIM)
    return k, v


def latent_attention(q, k, v, k_ctx, v_ctx):
    B, S = q.shape[:2]
    L = k_ctx.shape[1]
    nb = S // Q_BLOCK
    qb = q.reshape(B, nb, Q_BLOCK, N_KV_HEADS, Q_PER_KV, HEAD_DIM).transpose(1, 0, 2, 3, 4, 5)

    def one_block(qblk):
        s_ctx = jnp.einsum('bqkgd,bskd->bkgqs', qblk, k_ctx).astype(jnp.float32)
        s_lat = jnp.einsum('bqkgd,bskd->bkgqs', qblk, k).astype(jnp.float32)
        s = jnp.concatenate([s_ctx, s_lat], axis=-1) * ATTN_SCALE
        p = jax.nn.softmax(s, axis=-1).astype(v.dtype)
        return (jnp.einsum('bkgqs,bskd->bqkgd', p[..., :L], v_ctx)
                + jnp.einsum('bkgqs,bskd->bqkgd', p[..., L:], v))

    o = lax.map(one_block, qb)
    return o.transpose(1, 0, 2, 3, 4, 5).reshape(B, S, ATTN_W)


def context_attention(q, k, v):
    B, L = q.shape[:2]
    qg = q.reshape(B, L, N_KV_HEADS, Q_PER_KV, HEAD_DIM)
    s = jnp.einsum('bqkgd,bskd->bkgqs', qg, k).astype(jnp.float32) * ATTN_SCALE
    p = jax.nn.softmax(s, axis=-1).astype(v.dtype)
    return jnp.einsum('bkgqs,bskd->bqkgd', p, v).reshape(B, L, ATTN_W)


def spatial_gating(zg, sgu_norm, w_s, b_s):
    B, N = zg.shape[:2]
    z = jax.nn.gelu(zg)
    u, v = z[..., :SGU_W], z[..., SGU_W:]
    v = rms_norm(v, sgu_norm).reshape(B, N // CHUNK, CHUNK, SGU_GROUPS, SGU_GROUP_W)
    mixed = jnp.einsum('hpq,bnqhc->bnphc', w_s, v) + b_s.T[:, :, None]
    return u * mixed.reshape(B, N, SGU_W)


def multiscale_pool(p, pool_w, pool_scale):
    B, N = p.shape[:2]
    pf = p.reshape(B, N, POOL_GROUPS, POOL_GROUP_W).astype(jnp.float32)
    cs = jnp.concatenate([jnp.zeros((B, 1, POOL_GROUPS, POOL_GROUP_W), jnp.float32),
                          jnp.cumsum(pf, axis=1)], axis=1)
    t = jnp.arange(N)
    means = []
    for g, w in enumerate(POOL_WINDOWS):
        lo = jnp.clip(t - w // 2, 0, N)
        hi = jnp.clip(t - w // 2 + w, 0, N)
        win_sum = cs[:, hi, g] - cs[:, lo, g]
        means.append(win_sum / (hi - lo).astype(jnp.float32)[None, :, None])
    d = (jnp.stack(means, axis=2) - pf).astype(p.dtype)
    out = jnp.einsum('bngc,gcd->bngd', d, pool_w).reshape(B, N, POOL_W)
    return out * pool_scale


def swiglu(h, w_gate, w_up, w_down):
    return (jax.nn.silu(h @ w_gate) * (h @ w_up)) @ w_down


def setup_inputs(seed: int = 0) -> dict:
    key = jax.random.key(seed)
    ks = jax.random.split(key, 24)
    f32 = jnp.float32

    def nrm(k, shape, s):
        return jax.random.normal(k, shape, f32) * s

    return {
        "x": nrm(ks[0], (BATCH, SEQ, D_MODEL), 1.0),
        "c": nrm(ks[1], (BATCH, D_MODEL), 1.0),
        "ctx": nrm(ks[2], (BATCH, CTX_LEN, D_MODEL), 1.0),
        "c_ctx": nrm(ks[3], (D_MODEL,), 1.0),
        "w_mod": nrm(ks[4], (DEPTH, D_MODEL, 6 * D_MODEL), 0.5 * D_MODEL ** -0.5),
        "b_mod": nrm(ks[5], (DEPTH, 6 * D_MODEL), 0.02),
        "norm1": 1.0 + nrm(ks[6], (DEPTH, D_MODEL), 0.02),
        "norm2": 1.0 + nrm(ks[7], (DEPTH, D_MODEL), 0.02),
        "w_in": nrm(ks[8], (DEPTH, D_MODEL, IN_W), D_MODEL ** -0.5),
        "q_norm": 1.0 + nrm(ks[9], (DEPTH, HEAD_DIM), 0.02),
        "k_norm": 1.0 + nrm(ks[10], (DEPTH, HEAD_DIM), 0.02),
        "sgu_norm": 1.0 + nrm(ks[11], (DEPTH, SGU_W), 0.02),
        "w_s": nrm(ks[12], (DEPTH, SGU_GROUPS, CHUNK, CHUNK), CHUNK ** -0.5),
        "b_s": 1.0 + nrm(ks[13], (DEPTH, SGU_GROUPS, CHUNK), 0.02),
        "pool_w": nrm(ks[14], (DEPTH, POOL_GROUPS, POOL_GROUP_W, POOL_GROUP_W), POOL_GROUP_W ** -0.5),
        "pool_scale": 1.0 + nrm(ks[15], (DEPTH, POOL_W), 0.02),
        "w_out": nrm(ks[16], (DEPTH, MIX_W, D_MODEL), MIX_W ** -0.5),
        "w_gate": nrm(ks[17], (DEPTH, D_MODEL, D_FF), D_MODEL ** -0.5),
        "w_up": nrm(ks[18], (DEPTH, D_MODEL, D_FF), D_MODEL ** -0.5),
        "w_down": nrm(ks[19], (DEPTH, D_FF, D_MODEL), D_FF ** -0.5),
        "final_norm": 1.0 + nrm(ks[20], (D_MODEL,), 0.02),
    }


def reference(x, c, ctx, c_ctx, w_mod, b_mod, norm1, norm2, w_in, q_norm, k_norm,
              sgu_norm, w_s, b_s, pool_w, pool_scale, w_out, w_gate, w_up, w_down,
              final_norm):
    cos, sin = axial_rope(x.shape[1])
    for l in range(DEPTH):
        last = l == DEPTH - 1
        mod = jax.nn.silu(c) @ w_mod[l] + b_mod[l]
        mod_c = jax.nn.silu(c_ctx)[None, :] @ w_mod[l] + b_mod[l]
        sh1, sc1, gt1, sh2, sc2, gt2 = jnp.split(mod, 6, axis=-1)
        csh1, csc1, cgt1, csh2, csc2, cgt2 = jnp.split(mod_c, 6, axis=-1)

        h = modulate(x, norm1[l], sh1, sc1)
        hc = modulate(ctx, norm1[l], csh1, csc1)
        z = h @ w_in[l]
        q = apply_rope(q_heads(z[..., O_Q:O_K], q_norm[l]), cos, sin)
        k, v = kv_heads(z[..., O_K:O_V], z[..., O_V:O_G], k_norm[l])
        k = apply_rope(k, cos, sin)

        if last:
            zc_kv = hc @ w_in[l][:, O_K:O_G]
            k_c, v_c = kv_heads(zc_kv[..., :KV_W], zc_kv[..., KV_W:], k_norm[l])
        else:
            zc = hc @ w_in[l]
            k_c, v_c = kv_heads(zc[..., O_K:O_V], zc[..., O_V:O_G], k_norm[l])

        attn = latent_attention(q, k, v, k_c, v_c)
        sgu = spatial_gating(z[..., O_G:O_P], sgu_norm[l], w_s[l], b_s[l])
        pool = multiscale_pool(z[..., O_P:], pool_w[l], pool_scale[l])
        mix = jnp.concatenate([attn, sgu, pool], axis=-1) @ w_out[l]

        if not last:
            q_c = q_heads(zc[..., O_Q:O_K], q_norm[l])
            attn_c = context_attention(q_c, k_c, v_c)
            sgu_c = spatial_gating(zc[..., O_G:O_P], sgu_norm[l], w_s[l], b_s[l])
            pool_c = multiscale_pool(zc[..., O_P:], pool_w[l], pool_scale[l])
            mix_c = jnp.concatenate([attn_c, sgu_c, pool_c], axis=-1) @ w_out[l]
            ctx = ctx + cgt1[:, None, :] * mix_c
            hc2 = modulate(ctx, norm2[l], csh2, csc2)
            ctx = ctx + cgt2[:, None, :] * swiglu(hc2, w_gate[l], w_up[l], w_down[l])

        x = x + gt1[:, None, :] * mix
        h2 = modulate(x, norm2[l], sh2, sc2)
        x = x + gt2[:, None, :] * swiglu(h2, w_gate[l], w_up[l], w_down[l])

    return rms_norm(x, final_norm)
```

```python
import contextlib
import numpy as np
import concourse.bass as bass
import concourse.mybir as mybir
from concourse.bass_utils import run_bass_kernel_spmd

F32 = mybir.dt.float32
BF16 = mybir.dt.bfloat16
ALU = mybir.AluOpType
AF = mybir.ActivationFunctionType
AX = mybir.AxisListType

EPS = 1e-6
DEPTH = 2
NT = 2304
TILES = [(0, 512), (512, 512), (1024, 512), (1536, 512), (2048, 256)]
GELU_C = 0.7978845608028654
TM_SKEW = 24
ZIP = 1


class Sched:
    ENG = ("pe", "act", "dve", "pool", "sp")

    def __init__(self, nc, same_engine_sync=True):
        self.nc = nc
        self.ops = {e: [] for e in self.ENG}
        self.ncomp = {e: 0 for e in self.ENG}
        self.waited = {e: {} for e in self.ENG}
        self.last_w = {}
        self.readers = {}
        self.dma_cnt = {}
        self.same = same_engine_sync
        self.semkeys = []

    def _semkey(self, k):
        if k not in self.semkeys:
            self.semkeys.append(k)
        return k

    def op(self, eng, emit, reads=(), writes=(), dma=None):
        deps = []
        for r in reads:
            t = self.last_w.get(r)
            if t is not None:
                deps.append((t, "raw"))
        for w in writes:
            t = self.last_w.get(w)
            if t is not None:
                deps.append((t, "waw"))
            for t in self.readers.get(w, ()):
                deps.append((t, "war"))
        need = {}
        for (key, val, teng), kind in deps:
            if teng == eng:
                if eng == "pe" or not self.same or kind == "war":
                    continue
            if self.waited[eng].get(key, 0) >= val:
                continue
            if need.get(key, 0) < val:
                need[key] = val
        for key, val in need.items():
            self.waited[eng][key] = val
        if dma is not None:
            key = self._semkey("D:" + dma)
            self.dma_cnt[key] = self.dma_cnt.get(key, 0) + 1
            tok = (key, 16 * self.dma_cnt[key], None)
        elif emit is not None:
            key = self._semkey("E:" + eng)
            self.ncomp[eng] += 1
            tok = (key, self.ncomp[eng], eng)
        else:
            tok = None
        self.ops[eng].append((emit, sorted(need.items()), tok))
        if tok is not None:
            for w in writes:
                self.last_w[w] = tok
                self.readers[w] = []
            for r in reads:
                self.readers.setdefault(r, []).append(tok)
        return tok

    def barrier(self):
        state = {}
        for e in self.ENG:
            if self.ncomp[e]:
                state["E:" + e] = self.ncomp[e]
        for k, c in self.dma_cnt.items():
            state[k] = 16 * c
        for eng in self.ENG:
            need = {}
            for key, val in state.items():
                if self.waited[eng].get(key, 0) >= val:
                    continue
                need[key] = val
                self.waited[eng][key] = val
            if need:
                self.ops[eng].append((None, sorted(need.items()), None))

    def build(self):
        nc = self.nc
        with contextlib.ExitStack() as st:
            sems = {}
            for i, k in enumerate(self.semkeys):
                sems[k] = st.enter_context(nc.semaphore("s%d" % i))
            block = st.enter_context(nc.Block())

            def run(name):
                def f(e):
                    for emit, waits, tok in self.ops[name]:
                        for key, val in waits:
                            e.wait_ge(sems[key], val)
                        if emit is None:
                            continue
                        ins = emit(e)
                        if tok is not None:
                            ins.then_inc(sems[tok[0]], 16 if tok[2] is None else 1)
                return f

            block.tensor(run("pe"))
            block.scalar(run("act"))
            block.vector(run("dve"))
            block.gpsimd(run("pool"))
            block.sync(run("sp"))


def build_program(dumps=()):
    nc = bass.Bass("TRN2", target_bir_lowering=False)
    S = Sched(nc)

    def din(name, shape):
        return nc.dram_tensor(name, list(shape), F32, kind="ExternalInput").ap()

    x_d = din("x", [2048, 1024])
    ctx_d = din("ctx", [256, 1024])
    cvec_d = din("cvec", [16, 128])
    wmod_d = din("w_mod", [DEPTH, 1024, 6144])
    vecs_d = din("vecs", [128, 128])
    pscale_d = din("pool_scale", [4, 128])
    win_d = din("w_in", [DEPTH, 1024, 1536])
    qn_d = din("q_norm", [DEPTH, 64])
    kn_d = din("k_norm", [DEPTH, 64])
    sgn_d = din("sgu_norm", [DEPTH, 256])
    ws_d = din("w_s", [DEPTH, 4, 128, 128])
    bs_d = din("b_s", [DEPTH, 4, 128])
    pw_d = din("pool_w", [DEPTH, 4, 64, 64])
    wout_d = din("w_out", [DEPTH, 1024, 1024])
    wg_d = din("w_gate", [DEPTH, 1024, 2816])
    wu_d = din("w_up", [DEPTH, 1024, 2816])
    wd_d = din("w_down", [DEPTH, 2816, 1024])
    fn_d = din("final_norm", [1024])
    cos_d = din("cosT", [128, 16, 32])
    sin_d = din("sinT", [128, 16, 32])
    rcb_d = din("rcb", [128, 2, 2, 8])
    out_d = nc.dram_tensor("out", [2048, 1024], F32, kind="ExternalOutput").ap()

    KB_ = 1024
    uid = [0]

    def nm(s_):
        uid[0] += 1
        return "%s_%d" % (s_, uid[0])

    SB_BASE = 16512
    SB_END = 229376
    off = {"v": SB_BASE}

    def sb(name, shape, dt, at=None):
        nbytes = int(np.prod(shape[1:])) * (4 if dt == F32 else 2)
        if at is None:
            at = off["v"]
            off["v"] = (at + nbytes + 31) // 32 * 32
        return nc.alloc_sbuf_tensor_at(name, list(shape), dt, offset=at).ap(), at + nbytes

    xT, _ = sb("xT", [128, 8, NT], F32)
    MIX0 = off["v"]
    MIX67 = MIX0 + 6 * NT * 2
    mixT, _ = sb("mixT", [128, 8, NT], BF16)
    KA, _ = sb("KA", [128, NT], BF16)
    Vt, _ = sb("Vt", [128, 18, 2, 64], BF16)
    pT, _ = sb("pT", [128, 2, NT], BF16)
    LOC0 = off["v"]
    CONST_BYTES = 15616 + 256 + 640
    TOTAL = SB_END
    C0 = TOTAL - CONST_BYTES
    off["v"] = C0
    ident, _ = sb("ident", [128, 128], BF16)
    identf, _ = sb("identf", [128, 128], F32)
    ones_bf, _ = sb("ones_bf", [128, 128], BF16)
    ones_f, _ = sb("ones_f", [128, 128], F32)
    mhalf, _ = sb("mhalf", [128, 16], F32)
    vecT, _ = sb("vecT", [128, 128], F32)
    cT, _ = sb("cT", [128, 16], F32)
    pscT, _ = sb("pscT", [128, 4], F32)
    silT, _ = sb("silT", [128, 8, 2], BF16)
    modvs = [sb("modv%d" % i, [128, 48, 2], F32)[0] for i in range(DEPTH)]
    a1s = [sb("a1_%d" % i, [128, 8, 2], F32)[0] for i in range(DEPTH)]
    a2s = [sb("a2_%d" % i, [128, 8, 2], F32)[0] for i in range(DEPTH)]
    CUR_L = [0]
    WO = {}
    qg_bc, _ = sb("qg_bc", [128, DEPTH, 64], F32)
    kg_bc, _ = sb("kg_bc", [128, DEPTH, 64], F32)
    sgn_bc, _ = sb("sgn_bc", [128, DEPTH, 256], F32)
    bs_bc, _ = sb("bs_bc", [128, DEPTH, 2, 128], F32)
    wsT, _ = sb("wsT", [128, DEPTH, 4, 128], BF16)
    poolbd, _ = sb("poolbd", [128, DEPTH, 2, 128], BF16)
    invw, _ = sb("invw", [128, 2], F32)
    rcb, _ = sb("rcb_s", [128, 2, 2, 8], F32)
    cosT, _ = sb("cos_s", [128, 16, 32], F32)
    sinT, _ = sb("sin_s", [128, 16, 32], F32)
    small, _ = sb("small", [128, 128], F32)
    assert off["v"] <= TOTAL, off["v"]
    LOC_END = C0

    ps = nc.alloc_psum_tensor("ps", [128, 4096], F32).ap()

    def PS(b, n=1):
        return ps[:, b * 512:(b + n) * 512]

    def loc_alloc(skip=0):
        st = {"v": LOC0 + skip}

        def f(name, shape, dt, base=None):
            nbytes = int(np.prod(shape[1:])) * (4 if dt == F32 else 2)
            at = st["v"]
            st["v"] = (at + nbytes + 31) // 32 * 32
            assert st["v"] <= LOC_END, (name, st["v"], LOC_END)
            return nc.alloc_sbuf_tensor_at(nm(name), list(shape), dt, offset=at).ap()
        f.state = st
        return f

    REC = [None]

    def emit_op(eng, fn, r=(), w=(), dma=None, n=64, kind=""):
        if REC[0] is not None:
            if eng == "pe":
                cost = 0.04 + n / 1800.0
            elif eng == "act":
                cost = 0.2 + n * 0.00085
            elif eng == "dve":
                cost = 0.08 + n * 0.0016
            elif kind == "pow":
                cost = 0.35 + n * 0.16
            else:
                cost = 0.1 + n * 0.0026
            REC[0].append((eng, fn, tuple(r), tuple(w), dma, cost))
        else:
            S.op(eng, fn, r, w, dma=dma)

    def record(fn, *args):
        REC[0] = []
        fn(*args)
        ops = REC[0]
        REC[0] = None
        return ops

    def emit_scheduled(prog, lat=0.2, keep_pe_order=True):
        n_ = len(prog)
        lw, rd = {}, {}
        preds = [set() for _ in range(n_)]
        for i, (eng, fn, r, w, d, c) in enumerate(prog):
            for x in r:
                if x in lw:
                    preds[i].add(lw[x])
            for x in w:
                if x in lw:
                    preds[i].add(lw[x])
                preds[i].update(rd.get(x, ()))
            preds[i].discard(i)
            for x in w:
                lw[x] = i
                rd[x] = []
            for x in r:
                rd.setdefault(x, []).append(i)
        if keep_pe_order:
            prev = None
            for i in range(n_):
                if prog[i][0] == "pe":
                    if prev is not None:
                        preds[i].add(prev)
                    prev = i
        succs = [[] for _ in range(n_)]
        indeg = [len(p) for p in preds]
        for i, p in enumerate(preds):
            for j in p:
                succs[j].append(i)
        finish = [0.0] * n_
        efree = {e: 0.0 for e in Sched.ENG}
        ready = [i for i in range(n_) if indeg[i] == 0]
        order = []
        while ready:
            best, bstart = None, None
            for i in ready:
                st_ = efree[prog[i][0]]
                for j in preds[i]:
                    t_ = finish[j] + (0.0 if prog[j][0] == prog[i][0] == "pe" else lat)
                    if t_ > st_:
                        st_ = t_
                if best is None or st_ < bstart - 1e-9 or (abs(st_ - bstart) <= 1e-9 and i < best):
                    best, bstart = i, st_
            ready.remove(best)
            finish[best] = bstart + prog[best][5]
            efree[prog[best][0]] = finish[best]
            order.append(best)
            for k in succs[best]:
                indeg[k] -= 1
                if indeg[k] == 0:
                    ready.append(k)
        assert len(order) == n_
        for i in order:
            eng, fn, r, w, d, c = prog[i]
            S.op(eng, fn, r, w, dma=d)

    def emit_zip(lists):
        for x in lists:
            for it_ in x:
                if it_ is not None:
                    S.op(it_[0], it_[1], it_[2], it_[3], dma=it_[4])

    def fsz(ap):
        return int(np.prod(ap.shape[1:]))

    def mm(out, lhsT, rhs, start, stop, r, w, **kw):
        emit_op("pe", lambda e: e.matmul(out, lhsT=lhsT, rhs=rhs, start=start, stop=stop, **kw), r, w, n=fsz(rhs))

    def tr(out, in_, idn, r, w):
        emit_op("pe", lambda e: e.transpose(out=out, in_=in_, identity=idn), r, w, n=128)

    def act(out, in_, func, r, w, bias=None, scale=None, accum=None):
        kw = {}
        if bias is not None:
            kw["bias"] = bias
        if scale is not None:
            kw["scale"] = scale
        if accum is not None:
            kw["accum_out"] = accum
        emit_op("act", lambda e: e.activation(out=out, in_=in_, func=func, **kw), r, w, n=fsz(out))

    def tt(eng, out, in0, in1, op, r, w):
        emit_op(eng, lambda e: e.tensor_tensor(out=out, in0=in0, in1=in1, op=op), r, w, n=fsz(out),
                kind="pow" if op == ALU.pow else "")

    def ts(eng, out, in0, s1, s2, op0, op1, r, w):
        if s2 is None:
            emit_op(eng, lambda e: e.tensor_scalar(out=out, in0=in0, scalar1=s1, scalar2=None, op0=op0), r, w, n=fsz(out))
        else:
            emit_op(eng, lambda e: e.tensor_scalar(out=out, in0=in0, scalar1=s1, scalar2=s2, op0=op0, op1=op1), r, w, n=fsz(out))

    def stt(out, in0, scalar, in1, op0, op1, r, w):
        emit_op("dve", lambda e: e.scalar_tensor_tensor(out=out, in0=in0, scalar=scalar, in1=in1, op0=op0, op1=op1), r, w, n=fsz(out))

    def cp(eng, out, in_, r, w):
        if eng == "act":
            emit_op("act", lambda e: e.copy(out=out, in_=in_), r, w, n=fsz(out))
        else:
            emit_op(eng, lambda e: e.tensor_copy(out=out, in_=in_), r, w, n=fsz(out))

    def memset(eng, ap, val, w):
        emit_op(eng, lambda e: e.memset(ap, val), (), w)

    def dma(eng, out, in_, r, w, key):
        emit_op(eng, lambda e: e.dma_start(out=out, in_=in_), r, w, dma=key)

    dump_aps = {}

    def dump(name, ap, res):
        if name not in dumps:
            return
        d = nc.dram_tensor("dbg_" + name, list(ap.shape), ap.dtype, kind="ExternalOutput").ap()
        dma("sp", d, ap, res, ["dbg_" + name], "dbg_" + name)
        dump_aps[name] = "dbg_" + name

    def xres(ti, c=None):
        if c is None:
            return ["x%d_%d" % (ti, cc) for cc in range(8)]
        return ["x%d_%d" % (ti, c)]

    WIN_BYTES = 8 * 1536 * 2

    def load_win(l, win, after=()):
        after = list(after)
        wv = win_d[l].rearrange("(k p) n -> p k n", p=128)
        for g in range(2):
            for c in range(4):
                dma("pool", win[:, :, (c * 2 + g) * 64:(c * 2 + g + 1) * 64], wv[:, :, (g * 4 + c) * 64:(g * 4 + c + 1) * 64],
                    after, ["win"], "win")
        dma("pool", win[:, :, 512:768], wv[:, :, 512:768], after, ["win"], "win")
        dma("pool", win[:, :, 768:1024], wv[:, :, 1024:1280], after, ["win"], "win")
        dma("pool", win[:, :, 1024:1280], wv[:, :, 768:1024], after, ["win"], "win")
        dma("pool", win[:, :, 1280:1536], wv[:, :, 1280:1536], after, ["win"], "win")

    win0 = nc.alloc_sbuf_tensor_at(nm("win0"), [128, 8, 1536], BF16, offset=LOC0).ap()
    L = loc_alloc(WIN_BYTES)
    vst = L("vst", [128, 128], F32)
    vst2 = L("vst2", [32, 128], F32)
    wsst = L("wsst", [128, 8, 128], F32)
    xin = [L("xin0", [128, 1024], F32), L("xin1", [128, 1024], F32)]

    memset("dve", ones_bf, 1.0, ["ones_bf"])
    memset("dve", ones_f, 1.0, ["ones_f"])
    memset("dve", mhalf, -0.5, ["mhalf"])
    memset("dve", identf, 0.0, ["identf"])
    S.op("pool", lambda e: e.affine_select(out=identf, in_=identf, pattern=[[-1, 128]], compare_op=ALU.not_equal,
                                           fill=1.0, base=0, channel_multiplier=1), ["identf"], ["identf"])
    cp("dve", ident, identf, ["identf"], ["ident"])
    memset("dve", invw[0:64, 0:1], 0.5, ["invw"])
    memset("dve", invw[64:128, 0:1], 0.25, ["invw"])
    memset("dve", invw[0:64, 1:2], 0.125, ["invw"])
    memset("dve", invw[64:128, 1:2], 0.0625, ["invw"])
    memset("dve", poolbd, 0.0, ["poolbd"])
    memset("dve", vst2, 0.0, ["vst2"])

    dma("sp", vst, vecs_d, [], ["vst"], "vst")
    dma("sp", vst2[0:16, :], cvec_d, ["vst2"], ["vst2"], "vst2")
    dma("sp", vst2[16:20, :], pscale_d, ["vst2"], ["vst2"], "vst2")
    dma("sp", cosT, cos_d, [], ["cosT"], "cosT")
    dma("sp", sinT, sin_d, [], ["sinT"], "sinT")
    dma("sp", rcb, rcb_d, [], ["rcb"], "rcb")
    for l in range(DEPTH):
        dma("sp", qg_bc[:, l, :], qn_d[l].partition_broadcast(128), [], ["qg_bc"], "qg")
        dma("sp", kg_bc[:, l, :], kn_d[l].partition_broadcast(128), [], ["kg_bc"], "kg")
        dma("sp", sgn_bc[:, l, :], sgn_d[l].partition_broadcast(128), [], ["sgn_bc"], "sgn")
        for g in range(4):
            h0 = (g % 2) * 64
            dma("sp", bs_bc[h0:h0 + 64, l, g // 2, :], bs_d[l, g].partition_broadcast(64), [], ["bs_bc"], "bs")
            dma("pool", poolbd[h0:h0 + 64, l, g // 2, h0:h0 + 64], pw_d[l, g], ["poolbd"], ["poolbd"], "poolbd")
        dma("sp", wsst[:, l * 4:(l + 1) * 4, :], ws_d[l].rearrange("h p q -> p h q"), [], ["wsst"], "wsst")
    ts("dve", qg_bc, qg_bc, 0.125, None, ALU.mult, None, ["qg_bc"], ["qg_bc"])
    ts("dve", sgn_bc, sgn_bc, 0.5, None, ALU.mult, None, ["sgn_bc"], ["sgn_bc"])

    tr(PS(4)[:, 0:128], vst, identf, ["vst", "identf"], ["ps4"])
    cp("dve", vecT, PS(4)[:, 0:128], ["ps4"], ["vecT"])
    tr(PS(5)[:, 0:32], vst2, identf[0:32, 0:32], ["vst2", "identf"], ["ps5"])
    cp("dve", cT, PS(5)[:, 0:16], ["ps5"], ["cT"])
    cp("dve", pscT, PS(5)[:, 16:20], ["ps5"], ["pscT"])
    sil_t = small[:, 0:16]
    act(sil_t, cT, AF.Tanh, ["cT"], ["small"], scale=0.5)
    stt(sil_t, sil_t, 1.0, cT, ALU.add, ALU.mult, ["small", "cT"], ["small"])
    ts("dve", silT.rearrange("p k s -> p s k"), sil_t.rearrange("p (s k) -> p s k", s=2), 0.5, None, ALU.mult, None,
       ["small"], ["silT"])
    for i in range(8):
        b = 6 + (i % 2)
        tr(PS(b)[:, 0:128], wsst[:, i, :], identf, ["wsst", "identf"], ["ps%d" % b])
        cp("dve" if i % 2 else "act", wsT[:, i // 4, i % 4, :], PS(b)[:, 0:128], ["ps%d" % b], ["wsT"])

    for t128 in range(18):
        ti = min(t128 // 4, 4)
        slot = t128 % 2
        src = x_d[t128 * 128:(t128 + 1) * 128, :] if t128 < 16 else ctx_d[(t128 - 16) * 128:(t128 - 15) * 128, :]
        dma("sp", xin[slot], src, [], ["xin%d" % slot], "xin%d" % slot)
        b0 = slot * 2
        if t128 == 9:
            load_win(0, win0, after=["xin%d" % slot])
        for c in range(8):
            tr(PS(b0 + c // 4)[:, (c % 4) * 128:(c % 4 + 1) * 128], xin[slot][:, c * 128:(c + 1) * 128], identf,
               ["xin%d" % slot, "identf"], ["ps%d" % (b0 + c // 4)])
        for hh in range(2):
            cp("act" if hh else "dve", xT[:, hh * 4:(hh + 1) * 4, t128 * 128:(t128 + 1) * 128],
               PS(b0 + hh).rearrange("p (c t) -> p c t", c=4), ["ps%d" % (b0 + hh)],
               ["x%d_%d" % (ti, c) for c in range(hh * 4, hh * 4 + 4)])

    def mod_chunk(l, ch, wm):
        pm = PS(5)[:, 0:96]
        s = ch % 2
        dma("pool", wm[s], wmod_d[l].rearrange("(k p) n -> p k n", p=128)[:, :, ch * 512:(ch + 1) * 512],
            [], ["wm%d" % s], "wm%d" % s)
        for jj in range(4):
            jc = ch * 4 + jj
            for k in range(8):
                mm(pm[:, jc * 2:jc * 2 + 2], wm[s][:, k, jj * 128:(jj + 1) * 128], silT[:, k, :], k == 0, k == 7,
                   ["wm%d" % s, "silT"], ["ps5"])

    def mod_finish(l):
        pm = PS(5)[:, 0:96]
        modv, a1, a2 = modvs[l], a1s[l], a2s[l]
        tt("dve", modv, pm.rearrange("p (j s) -> p j s", s=2),
           vecT[:, l * 64:l * 64 + 48].unsqueeze(2).broadcast_to([128, 48, 2]), ALU.add, ["ps5", "vecT"], ["modv%d" % l])
        stt(a1, modv[:, 8:16, :], 1.0, vecT[:, l * 64 + 48:l * 64 + 56].unsqueeze(2).broadcast_to([128, 8, 2]),
            ALU.add, ALU.mult, ["modv%d" % l, "vecT"], ["a1_%d" % l])
        stt(a2, modv[:, 32:40, :], 1.0, vecT[:, l * 64 + 56:l * 64 + 64].unsqueeze(2).broadcast_to([128, 8, 2]),
            ALU.add, ALU.mult, ["modv%d" % l, "vecT"], ["a2_%d" % l])

    def mod_phase(l):
        if l > 0:
            return
        wm = [L("wm0", [128, 8, 512], BF16), L("wm1", [128, 8, 512], BF16)]
        for ch in range(12):
            mod_chunk(0, ch, wm)
        mod_finish(0)

    def MOD(j, c, s):
        return modvs[CUR_L[0]][:, j * 8 + c, s:s + 1]

    def MODR():
        return "modv%d" % CUR_L[0]

    def norm_mod(ti, hbuf, hres, aT, bj, bS, bB, R, rs, tmp, tmpres):
        t0, T = TILES[ti]
        s = 1 if ti == 4 else 0
        nsub = T // 128
        act(hbuf[:, :, 0:T], xT[:, :, t0:t0 + T], AF.Square, xres(ti), hres)
        for sub in range(nsub):
            for k in range(8):
                mm(PS(bS)[:, sub:sub + 1], hbuf[:, k, sub * 128:(sub + 1) * 128], ones_bf[:, 0:1], k == 0, k == 7,
                   [hres[k], "ones_bf"], ["ps%d" % bS])
        ts("dve", rs[:, 0:nsub], PS(bS)[:, 0:nsub], 1.0 / 1024, EPS, ALU.mult, ALU.add, ["ps%d" % bS], ["rs_ms"])
        tt("pool", rs[:, 4:4 + nsub], rs[:, 0:nsub], mhalf[:, 0:nsub], ALU.pow, ["rs_ms", "mhalf"], ["rs_r"])
        for sub in range(nsub):
            ts("dve", R[:, sub, :], ones_f, rs[:, 4 + sub:5 + sub], None, ALU.mult, None, ["rs_r", "ones_f"],
               ["R%d" % sub, "gsc"])
            mm(PS(bB)[:, sub * 128:(sub + 1) * 128], R[:, sub, :], identf, True, True, ["R%d" % sub, "gsc", "identf"],
               ["ps%d" % bB])
        for c in range(8):
            j = c % 2
            stt(tmp[j][:, 0:T], xT[:, c, t0:t0 + T], aT[:, c, s:s + 1], PS(bB)[:, 0:T], ALU.mult, ALU.mult,
                xres(ti, c) + ["ps%d" % bB, "a1_%d" % CUR_L[0], "a2_%d" % CUR_L[0]], [tmpres[j]])
            act(hbuf[:, c, 0:T], tmp[j][:, 0:T], AF.Identity, [tmpres[j], MODR()], [hres[c]], bias=MOD(bj, c, s))

    def gelu2(src, dst, g1, r_src, w_dst, g1res):
        act(g1, src, AF.Square, r_src, [g1res])
        ts("dve", g1, g1, 0.044715, 1.0, ALU.mult, ALU.add, [g1res], [g1res])
        tt("dve", g1, g1, src, ALU.mult, [g1res] + r_src, [g1res])
        act(g1, g1, AF.Tanh, [g1res], [g1res], scale=GELU_C)
        stt(dst, g1, 1.0, src, ALU.add, ALU.mult, [g1res] + r_src, w_dst)

    def phase_a(l):
        S.barrier()
        CUR_L[0] = l
        last = l == DEPTH - 1
        La = loc_alloc()
        m67 = {"v": MIX67}

        def Lm(name, shape, dt):
            nbytes = int(np.prod(shape[1:])) * (4 if dt == F32 else 2)
            at = m67["v"]
            end = (at + nbytes + 31) // 32 * 32
            if end <= MIX67 + 2 * NT * 2:
                m67["v"] = end
                return nc.alloc_sbuf_tensor_at(nm(name), list(shape), dt, offset=at).ap()
            return La(name, shape, dt)

        if l == 0:
            win = win0
            La.state["v"] = LOC0 + WIN_BYTES
        else:
            win = La("win", [128, 8, 1536], BF16)
        hq = [La("hq0", [128, 8, 512], BF16), La("hq1", [128, 8, 512], BF16)]
        R = La("R", [128, 4, 128], F32)
        tmp = [La("tmp0", [128, 512], F32), La("tmp1", [128, 512], F32)]
        uT0_ = La("uT0", [128, 2, 512], BF16)
        uTs = [uT0_, uT0_]
        gsc = R.rearrange("p s t -> p (s t)")
        TS = []
        for pq in range(2):
            A = La
            g1_ = A("g1", [128, 256], F32)
            TS.append(dict(qn=A("qn", [128, 10, 64], F32), rt=[A("rt0", [128, 10, 32], F32), A("rt1", [128, 10, 32], F32)],
                           qb=A("qb", [128, 10, 64], BF16), g1=g1_, v2=A("v2", [128, 256], F32),
                           vn=A("vn", [128, 256], BF16), sg=g1_.rearrange("p (c t) -> p c t", c=2)))
        if l > 0:
            load_win(l, win)

        def hres_of(ti):
            return ["hq%d_%d" % (ti % 2, k) for k in range(8)]

        def tm_mm(ti, sub, need_q):
            hbuf, hres = hq[ti % 2], hres_of(ti)
            bq, bk = (4, 5) if sub % 2 == 0 else (2, 3)
            for k in range(8):
                lt = hbuf[:, k, sub * 128:(sub + 1) * 128]
                if need_q:
                    mm(PS(bq), lt, win[:, k, 0:512], k == 0, k == 7, [hres[k], "win"], ["ps%d" % bq])
                mm(PS(bk), lt, win[:, k, 512:1024], k == 0, k == 7, [hres[k], "win"], ["ps%d" % bk])

        def tm_post(ti, sub, full):
            t0, T = TILES[ti]
            is_ctx = ti == 4
            pq = sub % 2
            X = TS[pq]
            qn, rt, qb, g1, v2, vn, sg = X["qn"], X["rt"], X["qb"], X["g1"], X["v2"], X["vn"], X["sg"]
            sqq = qn.rearrange("p h d -> p (h d)")
            uT = uTs[ti % 2]
            P = lambda n: "%s_%d" % (n, pq)
            t128 = t0 // 128 + sub
            tok = slice(t128 * 128, (t128 + 1) * 128)
            bq, bk = (4, 5) if pq == 0 else (2, 3)
            psT6 = PS(6).bitcast(BF16)
            psT7 = PS(7).bitcast(BF16)
            qT = psT6[:, pq * 512:(pq + 1) * 512]
            kT = psT7[:, pq * 128:(pq + 1) * 128]
            psG = PS(bk)[:, 256:512]
            rq, rk = ["ps%d" % bq], ["ps%d" % bk]
            psQ, psK, psV, psGV = PS(bq), PS(bk)[:, 0:128], PS(bk)[:, 128:256], PS(bk)[:, 256:512]
            h0 = 0 if full else 8
            h1 = 11 if full else 10
            sc0 = 16 + pq * 40
            ss, ms, rs = small[:, sc0:sc0 + 11], small[:, sc0 + 11:sc0 + 22], small[:, sc0 + 22:sc0 + 33]
            cp("act", Vt[:, t128, :, :], psV.rearrange("p (g d) -> p g d", g=2), rk, ["V%d" % t128])
            if full:
                gelu2(psGV, v2, g1, rk, [P("v2")], P("g1"))
                act(g1, v2, AF.Square, [P("v2")], [P("g1"), P("ss")], accum=ss[:, 10:11])
                act(sqq[:, 0:512], psQ, AF.Square, rq, [P("qn_q")])
            act(sqq[:, 512:640], psK, AF.Square, rk, [P("qn_k")])
            emit_op("dve", lambda e: e.tensor_reduce(out=ss[:, h0:10], in_=qn[:, h0:10, :], axis=AX.X, op=ALU.add),
                    [P("qn_q"), P("qn_k")], [P("ss")], n=640)
            ts("dve", ms[:, h0:10], ss[:, h0:10], 1.0 / 64, EPS, ALU.mult, ALU.add, [P("ss")], [P("ms")])
            if full:
                ts("dve", ms[:, 10:11], ss[:, 10:11], 0.25 / 256, EPS, ALU.mult, ALU.add, [P("ss")], [P("ms")])
            tt("pool", rs[:, h0:h1], ms[:, h0:h1], mhalf[:, h0:h1], ALU.pow, [P("ms"), "mhalf"], [P("rs")])
            if full:
                stt(vn, v2, rs[:, 10:11], sgn_bc[:, l, :], ALU.mult, ALU.mult, [P("v2"), P("rs"), "sgn_bc"], [P("vn")])
                for h in range(4):
                    o0 = (h % 2) * 64
                    mm(psG[o0:o0 + 64, (h // 2) * 128:(h // 2 + 1) * 128], vn[:, h * 64:(h + 1) * 64], wsT[:, l, h, :],
                       True, True, [P("vn"), "wsT"], ["ps%d" % bk], tile_position=(0, o0))
                tt("dve", sg, psG.rearrange("p (c t) -> p c t", c=2), bs_bc[:, l, :, :], ALU.add,
                   ["ps%d" % bk, "bs_bc"], [P("g1")])
                stt(mixT[:, 4:6, tok], sg, 0.5, uT[:, :, sub * 128:(sub + 1) * 128], ALU.mult, ALU.mult,
                    [P("g1"), "uT0_0", "uT0_1"], ["mix4_%d" % ti, "mix5_%d" % ti])
                tt("dve", qn[:, 0:8, :], psQ.rearrange("p (h d) -> p h d", d=64),
                   rs[:, 0:8].unsqueeze(2).broadcast_to([128, 8, 64]), ALU.mult, rq + [P("rs")], [P("qn_q")])
                tt("pool", qn[:, 0:8, :], qn[:, 0:8, :], qg_bc[:, l, :].unsqueeze(1).broadcast_to([128, 8, 64]), ALU.mult,
                   [P("qn_q"), "qg_bc"], [P("qn_q")])
            tt("dve", qn[:, 8:10, :], psK.rearrange("p (h d) -> p h d", d=64),
               rs[:, 8:10].unsqueeze(2).broadcast_to([128, 2, 64]), ALU.mult, rk + [P("rs")], [P("qn_k")])
            tt("pool", qn[:, 8:10, :], qn[:, 8:10, :], kg_bc[:, l, :].unsqueeze(1).broadcast_to([128, 2, 64]), ALU.mult,
               [P("qn_k"), "kg_bc"], [P("qn_k")])
            nh = 10 - h0
            if not is_ctx:
                cs = cosT[:, t128, :].unsqueeze(1).broadcast_to([128, nh, 32])
                sn = sinT[:, t128, :].unsqueeze(1).broadcast_to([128, nh, 32])
                x1, x2 = qn[:, h0:10, 0:32], qn[:, h0:10, 32:64]
                rr = [P("qn_q"), P("qn_k"), "cosT", "sinT"]
                tt("pool", rt[0][:, h0:10, :], x1, cs, ALU.mult, rr, [P("rt0")])
                tt("pool", rt[1][:, h0:10, :], x2, sn, ALU.mult, rr, [P("rt1")])
                tt("dve", qb[:, h0:10, 0:32], rt[0][:, h0:10, :], rt[1][:, h0:10, :], ALU.subtract, [P("rt0"), P("rt1")], [P("qb")])
                tt("pool", rt[0][:, h0:10, :], x2, cs, ALU.mult, rr, [P("rt0")])
                tt("pool", rt[1][:, h0:10, :], x1, sn, ALU.mult, rr, [P("rt1")])
                tt("dve", qb[:, h0:10, 32:64], rt[0][:, h0:10, :], rt[1][:, h0:10, :], ALU.add, [P("rt0"), P("rt1")], [P("qb")])
            else:
                cp("dve", qb[:, h0:10, :], qn[:, h0:10, :], [P("qn_q"), P("qn_k")], [P("qb")])
            qbf = qb.rearrange("p h d -> p (h d)")
            if full:
                for c in range(4):
                    tr(qT[:, c * 128:(c + 1) * 128], qbf[:, c * 128:(c + 1) * 128], ident, [P("qb"), "ident"], ["ps6"])
            tr(kT, qbf[:, 512:640], ident, [P("qb"), "ident"], ["ps7"])
            if full:
                cp("act", mixT[:, 0:4, tok], qT.rearrange("p (c t) -> p c t", c=4), ["ps6"],
                   ["mix%d_%d" % (c, ti) for c in range(4)])
            cp("act", KA[:, tok], kT, ["ps7"], ["K%d" % t128])

        def norm_fm(ti):
            t0, T = TILES[ti]
            full = not (last and ti == 4)
            hbuf, hres = hq[ti % 2], hres_of(ti)
            uT = uTs[ti % 2]
            norm_mod(ti, hbuf, hres, a1s[l], 0, 0, 1, R, small[:, 0:8], tmp, ["tmp0", "tmp1"])
            if full:
                for fc in range(4):
                    b = 2 + fc % 2
                    for k in range(8):
                        mm(PS(b)[:, 0:T], win[:, k, 1024 + fc * 128:1024 + (fc + 1) * 128], hbuf[:, k, 0:T], k == 0, k == 7,
                           [hres[k], "win"], ["ps%d" % b])
                    if fc < 2:
                        gelu2(PS(b)[:, 0:T], uT[:, fc, 0:T], gsc[:, 0:T], ["ps%d" % b], ["uT0_%d" % fc], "gsc")
                    else:
                        cp("act", pT[:, fc - 2, t0:t0 + T], PS(b)[:, 0:T], ["ps%d" % b], ["pT%d_%d" % (fc - 2, ti)])

        def fullf(ti):
            return not (last and ti == 4)

        def zipped(lists, burst=None, at=0, pre=None):
            out = list(pre) if pre else []
            n_ = max(len(x) for x in lists)
            for i in range(max(n_, at + 1)):
                for x in lists:
                    if i < len(x) and x[i] is not None:
                        out.append(x[i])
                if burst is not None and i == at:
                    out.extend(burst)
            return out

        prog = record(norm_fm, 0) + record(tm_mm, 0, 0, fullf(0)) + record(tm_mm, 0, 1, fullf(0))
        for ti in range(5):
            nsub = TILES[ti][1] // 128
            full = fullf(ti)
            lists = [record(tm_post, ti, 0, full), record(tm_post, ti, 1, full)]
            burst = pre = nf_rest = None
            if nsub == 4:
                burst = record(tm_mm, ti, 2, full) + record(tm_mm, ti, 3, full)
                if ti + 1 < 5:
                    nf = record(norm_fm, ti + 1)
                    ns1 = TILES[ti + 1][1] // 128
                    n_norm = 1 + 10 * ns1 + 18
                    pre, nf_rest = nf[:n_norm], nf[n_norm:]
            prog += zipped(lists, burst, TM_SKEW, pre)
            if nsub == 4:
                lists = [record(tm_post, ti, 2, full), record(tm_post, ti, 3, full)]
                burst = None
                if ti + 1 < 5:
                    burst = (nf_rest + record(tm_mm, ti + 1, 0, fullf(ti + 1))
                             + record(tm_mm, ti + 1, 1, fullf(ti + 1)))
                prog += zipped(lists, burst, TM_SKEW)
        if ZIP:
            emit_scheduled(prog)
        else:
            emit_zip([prog])

    def pool_phase(l):
        return

    def pool_prep(l, A):
        last = l == DEPTH - 1
        P0 = A("P0", [128, 2064], F32)
        s2 = A("s2", [128, 2064], F32)
        s4 = A("s4", [128, 2064], F32)
        s8 = A("s8", [128, 2064], F32)
        s16 = s2
        fx = A("fx", [128, 16], F32)
        streams = [(0, 2048, [0, 1, 2, 3])] + ([] if last else [(2048, 256, [4])])
        for (t0, N, tis) in streams:
            for ch in range(2):
                pres = ["pT%d_%d" % (ch, ti) for ti in tis]
                memset("dve", P0[:, 0:8], 0.0, ["P0"])
                memset("dve", P0[:, 8 + N:16 + N], 0.0, ["P0"])
                cp("dve", P0[:, 8:8 + N], pT[:, ch, t0:t0 + N], pres, ["P0"])
                tt("dve", s2[:, 1:N + 16], P0[:, 0:N + 15], P0[:, 1:N + 16], ALU.add, ["P0"], ["s2"])
                tt("dve", s4[:, 2:N + 15], s2[:, 1:N + 14], s2[:, 3:N + 16], ALU.add, ["s2"], ["s4"])
                if ch == 0:
                    srcs = [(0, 64, s2, 8), (64, 128, s4, 8)]
                else:
                    tt("dve", s8[:, 4:N + 13], s4[:, 2:N + 11], s4[:, 6:N + 15], ALU.add, ["s4"], ["s8"])
                    tt("dve", s16[64:128, 0:N], s8[64:128, 4:N + 4], s8[64:128, 12:N + 12], ALU.add, ["s8"], ["s2"])
                    srcs = [(0, 64, s8, 8), (64, 128, s16, 0)]
                for (p0, p1, sw, o_) in srcs:
                    stt(pT[p0:p1, ch, t0:t0 + N], sw[p0:p1, o_:o_ + N], invw[p0:p1, ch:ch + 1], P0[p0:p1, 8:8 + N],
                        ALU.mult, ALU.subtract, ["s2", "s4", "s8", "P0", "invw"], pres)
                    for side, c0 in ((0, 0), (1, N - 8)):
                        tt("dve", fx[p0:p1, side * 8:side * 8 + 8], sw[p0:p1, o_ + c0:o_ + c0 + 8], rcb[p0:p1, ch, side, :],
                           ALU.mult, ["s2", "s4", "s8", "rcb"], ["fx"])
                        tt("dve", pT[p0:p1, ch, t0 + c0:t0 + c0 + 8], fx[p0:p1, side * 8:side * 8 + 8],
                           P0[p0:p1, 8 + c0:16 + c0], ALU.subtract, ["fx", "P0"] + pres, pres)

    def pool_finish(l):
        last = l == DEPTH - 1
        i = 0
        for ti in range(4 if last else 5):
            t0, T = TILES[ti]
            for ch in range(2):
                b = 6 + i % 2
                i += 1
                mm(PS(b)[:, 0:T], poolbd[:, l, ch, :], pT[:, ch, t0:t0 + T], True, True, ["pT%d_%d" % (ch, ti), "poolbd"],
                   ["ps%d" % b])
                act(mixT[:, 6 + ch, t0:t0 + T], PS(b)[:, 0:T], AF.Copy, ["ps%d" % b, "pscT"], ["mix%d_%d" % (6 + ch, ti)],
                    scale=pscT[:, l * 2 + ch:l * 2 + ch + 1])

    def attention(l):
        S.barrier()
        CUR_L[0] = l
        last = l == DEPTH - 1
        Lb = loc_alloc()
        PT = [Lb("PT%d" % i, [128, 1024], BF16) for i in range(3)]
        rec = [Lb("rec%d" % i, [128, 512], F32) for i in range(2)]
        wo = Lb("wo", [128, 8, 1024], BF16)
        WO[l] = (wo, Lb.state["v"])
        for e_ in range(2):
            dma("pool", wo[e_ * 64:(e_ + 1) * 64, 0:4, :],
                wout_d[l][e_ * 256:(e_ + 1) * 256, :].rearrange("(c d) n -> d c n", d=64), [], ["wo"], "wo")
        dma("pool", wo[:, 4:8, :], wout_d[l][512:1024, :].rearrange("(k p) n -> p k n", p=128), [], ["wo"], "wo")
        blocks = [(c, ti, list(range(18))) for c in range(4) for ti in range(4)]
        if not last:
            blocks += [(c, 4, [16, 17]) for c in range(4)]
        prep = record(pool_prep, l, Lb)
        prep_pos = [0]
        per_block = -(-len(prep) // 12)

        def drip(n):
            for eng, fn, r_, w_, d_, c_ in prep[prep_pos[0]:prep_pos[0] + n]:
                S.op(eng, fn, r_, w_, dma=d_)
            prep_pos[0] += n

        units = []
        for bi, (c, ti, kts) in enumerate(blocks):
            for ki, kt in enumerate(kts):
                units.append((bi, c, ti, kt, ki, len(kts)))

        def emit_s(j):
            bi, c, ti, kt, ki, nk = units[j]
            t0, T = TILES[ti]
            g = c // 2
            sb0 = (j % 2) * 2
            pt = PT[j % 3]
            qres = ["mix%d_%d" % (c, ti)]
            for e_ in range(2):
                mm(PS(sb0 + e_)[:, 0:T], KA[e_ * 64:(e_ + 1) * 64, kt * 128:(kt + 1) * 128],
                   mixT[e_ * 64:(e_ + 1) * 64, c, t0:t0 + T], True, True, ["K%d" % kt] + qres, ["ps%d" % (sb0 + e_)])
            S.op("act", (lambda pt=pt, sb0=sb0, T=T: lambda e: e.activation(
                out=pt.rearrange("p (e t) -> p e t", e=2)[:, :, 0:T],
                in_=PS(sb0, 2).rearrange("p (e t) -> p e t", e=2)[:, :, 0:T], func=AF.Exp))(),
                ["ps%d" % sb0, "ps%d" % (sb0 + 1)], ["PT%d" % (j % 3)])

        def emit_pv(j):
            bi, c, ti, kt, ki, nk = units[j]
            t0, T = TILES[ti]
            g = c // 2
            bo = 4 + bi % 2
            bd = 6 + bi % 2
            pt = PT[j % 3]
            ptr = "PT%d" % (j % 3)
            for e_ in range(2):
                o0 = e_ * 64
                mm(PS(bo)[o0:o0 + 64, 0:T], Vt[:, kt, e_, :], pt[:, e_ * 512:e_ * 512 + T], ki == 0, ki == nk - 1,
                   ["V%d" % kt, ptr], ["ps%d_%d" % (bo, e_)], tile_position=(0, o0))
            for e_ in range(2):
                o0 = e_ * 64
                mm(PS(bd)[o0:o0 + 64, 0:T], ones_bf[:, 0:64], pt[:, e_ * 512:e_ * 512 + T], ki == 0, ki == nk - 1,
                   ["ones_bf", ptr], ["ps%d_%d" % (bd, e_)], tile_position=(0, o0))
            if ki == nk - 1:
                rc_ = rec[bi % 2]
                S.op("dve", (lambda rc_=rc_, bd=bd, T=T: lambda e: e.reciprocal(out=rc_[:, 0:T], in_=PS(bd)[:, 0:T]))(),
                     ["ps%d_0" % bd, "ps%d_1" % bd], ["rec%d" % (bi % 2)])
                tt("dve", mixT[:, c, t0:t0 + T], PS(bo)[:, 0:T], rc_[:, 0:T], ALU.mult,
                   ["ps%d_0" % bo, "ps%d_1" % bo, "rec%d" % (bi % 2)], ["mix%d_%d" % (c, ti)])
                drip(per_block)

        for j in range(len(units)):
            emit_s(j)
            if j >= 1:
                emit_pv(j - 1)
        emit_pv(len(units) - 1)
        drip(len(prep))

    def phase_c1(l):
        S.barrier()
        CUR_L[0] = l
        last = l == DEPTH - 1
        Lc = loc_alloc()
        wo, wo_end = WO[l]
        Lc.state["v"] = wo_end
        pool_finish(l)
        nxt = l + 1 if l + 1 < DEPTH else None
        if nxt is not None:
            wmn = [Lc("wm0", [128, 8, 512], BF16), Lc("wm1", [128, 8, 512], BF16)]
        i = 0
        for ti in range(4 if last else 5):
            t0, T = TILES[ti]
            s = 1 if ti == 4 else 0
            for dc in range(8):
                if nxt is not None and i % 3 == 0 and i // 3 < 12:
                    mod_chunk(nxt, i // 3, wmn)
                b = i % 4
                i += 1
                for k in range(8):
                    mm(PS(b)[:, 0:T], wo[:, k, dc * 128:(dc + 1) * 128], mixT[:, k, t0:t0 + T], k == 0, k == 7,
                       ["wo", "mix%d_%d" % (k, ti)], ["ps%d" % b])
                stt(xT[:, dc, t0:t0 + T], PS(b)[:, 0:T], MOD(2, dc, s), xT[:, dc, t0:t0 + T], ALU.mult, ALU.add,
                    ["ps%d" % b, MODR()] + xres(ti, dc), xres(ti, dc))
        if nxt is not None:
            mod_finish(nxt)

    def ffn_phase(l):
        S.barrier()
        CUR_L[0] = l
        last = l == DEPTH - 1
        groups = [[0, 1], [2, 3] if last else [2, 3, 4]]
        st = {"v": MIX0}

        def Lf(name, shape, dt):
            nbytes = int(np.prod(shape[1:])) * (4 if dt == F32 else 2)
            at = st["v"]
            st["v"] = (at + nbytes + 31) // 32 * 32
            assert st["v"] <= LOC_END, (name, st["v"], LOC_END)
            return nc.alloc_sbuf_tensor_at(nm(name), list(shape), dt, offset=at).ap()

        h2 = Lf("h2", [128, 8, 1280], BF16)
        aT = Lf("aT", [128, 22, 1280], BF16)
        R = Lf("R", [128, 4, 128], F32)
        tmp = [Lf("tmp0", [128, 512], F32), Lf("tmp1", [128, 512], F32)]
        th = [Lf("th0", [128, 512], F32), Lf("th1", [128, 512], F32)]
        wgs = [Lf("wg%d" % i, [128, 8, 128], BF16) for i in range(3)]
        wus = [Lf("wu%d" % i, [128, 8, 128], BF16) for i in range(3)]
        wds = [Lf("wd%d" % i, [128, 22, 128], BF16) for i in range(2)]
        allmix = ["mix%d_%d" % (c, ti) for c in range(8) for ti in range(5)]
        fence = allmix + ["K%d" % t for t in range(18)] + ["V%d" % t for t in range(18)] + \
            ["pT%d_%d" % (c, ti) for c in range(2) for ti in range(5)] + ["wo", "PT0", "PT1", "PT2", "rec0", "rec1"]
        fi = [0]
        wgv = wg_d[l].rearrange("(k p) n -> p k n", p=128)
        wuv = wu_d[l].rearrange("(k p) n -> p k n", p=128)
        wdv = wd_d[l].rearrange("(f p) n -> p f n", p=128)

        def load_gu(f):
            s_ = f % 3
            dma("pool", wgs[s_], wgv[:, :, f * 128:(f + 1) * 128], [], ["wg%d" % s_], "wg%d" % s_)
            dma("pool", wus[s_], wuv[:, :, f * 128:(f + 1) * 128], [], ["wu%d" % s_], "wu%d" % s_)

        def load_d(dc):
            s_ = dc % 2
            dma("pool", wds[s_], wdv[:, :, dc * 128:(dc + 1) * 128], [], ["wd%d" % s_], "wd%d" % s_)

        def goffs(tis):
            offs, o = {}, 0
            for ti in tis:
                offs[ti] = o
                o += TILES[ti][1]
            return offs

        def h2res(offs, ti):
            return ["h2_%d_%d" % (offs[ti], k) for k in range(8)]

        def do_norm(offs, ti):
            T = TILES[ti][1]
            hb = h2[:, :, offs[ti]:offs[ti] + T]
            norm_mod(ti, hb, h2res(offs, ti), a2s[l], 3, 0, 1, R, small[:, 0:8], tmp, ["tmp0", "tmp1"])

        def emit_list(ops):
            for eng, fn, r, w, d, c in ops:
                S.op(eng, fn, r, w, dma=d)

        for gi, tis in enumerate(groups):
            offs = goffs(tis)
            load_gu(0)
            load_gu(1)
            if gi == 0:
                for ti in tis:
                    do_norm(offs, ti)
                offs1 = goffs(groups[1])
                pieces = []
                for ti in groups[1]:
                    ops = record(do_norm, offs1, ti)
                    nsub = TILES[ti][1] // 128
                    n1 = 1 + 8 * nsub + 2
                    rb = ops[n1:n1 + 2 * nsub]
                    pieces.append((ops[:1], ops[1:n1] + rb[0::2], rb[1::2] + ops[n1 + 2 * nsub:]))
            for f in range(22):
                if f + 2 < 22:
                    load_gu(f + 2)
                elif f == 20:
                    load_d(0)
                elif f == 21:
                    load_d(1)
                s_ = f % 3
                for ti in tis:
                    T = TILES[ti][1]
                    o = offs[ti]
                    j = fi[0] % 2
                    fi[0] += 1
                    bg, bu = 2 + j, 4 + j
                    hres = h2res(offs, ti)
                    for k in range(8):
                        mm(PS(bg)[:, 0:T], wgs[s_][:, k, :], h2[:, k, o:o + T], k == 0, k == 7, ["wg%d" % s_, hres[k]],
                           ["ps%d" % bg])
                    for k in range(8):
                        mm(PS(bu)[:, 0:T], wus[s_][:, k, :], h2[:, k, o:o + T], k == 0, k == 7, ["wu%d" % s_, hres[k]],
                           ["ps%d" % bu])
                    act(th[j][:, 0:T], PS(bg)[:, 0:T], AF.Tanh, ["ps%d" % bg], ["th%d" % j], scale=0.5)
                    stt(th[j][:, 0:T], th[j][:, 0:T], 1.0, PS(bg)[:, 0:T], ALU.add, ALU.mult, ["th%d" % j, "ps%d" % bg],
                        ["th%d" % j])
                    stt(aT[:, f, o:o + T], th[j][:, 0:T], 0.5, PS(bu)[:, 0:T], ALU.mult, ALU.mult, ["th%d" % j, "ps%d" % bu],
                        ["aT%d_%d" % (f, ti)])
            yi = 0
            if gi == 0:
                for pc in pieces:
                    emit_list(pc[0])
            for dc in range(8):
                if gi == 0:
                    if 2 <= dc <= len(pieces) + 1:
                        emit_list(pieces[dc - 2][2])
                    if 1 <= dc <= len(pieces):
                        emit_list(pieces[dc - 1][1])
                s_ = dc % 2
                for ti in tis:
                    t0, T = TILES[ti]
                    o = offs[ti]
                    sidx = 1 if ti == 4 else 0
                    b = 6 + yi % 2
                    yi += 1
                    for f in range(22):
                        mm(PS(b)[:, 0:T], wds[s_][:, f, :], aT[:, f, o:o + T], f == 0, f == 21,
                           ["wd%d" % s_, "aT%d_%d" % (f, ti)], ["ps%d" % b])
                    stt(xT[:, dc, t0:t0 + T], PS(b)[:, 0:T], MOD(5, dc, sidx), xT[:, dc, t0:t0 + T], ALU.mult, ALU.add,
                        ["ps%d" % b, MODR()] + xres(ti, dc), xres(ti, dc))
                if dc + 2 < 8:
                    load_d(dc + 2)

    def final_phase():
        S.barrier()
        st = {"v": MIX0}

        def Lz(name, shape, dt):
            nbytes = int(np.prod(shape[1:])) * (4 if dt == F32 else 2)
            at = st["v"]
            st["v"] = (at + nbytes + 31) // 32 * 32
            assert st["v"] <= LOC_END
            return nc.alloc_sbuf_tensor_at(nm(name), list(shape), dt, offset=at).ap()

        fn_bc = Lz("fn_bc", [128, 1024], F32)
        ob = [Lz("ob0", [128, 1024], F32), Lz("ob1", [128, 1024], F32)]
        junk = Lz("junk", [128, 1024], F32)
        fence = ["h2_%d_%d" % (ti, k) for ti in range(5) for k in range(8)] + \
            ["aT%d_%d" % (f, ti) for f in range(22) for ti in range(5)] + ["R0", "R1", "R2", "R3", "tmp0", "tmp1"]
        dma("sp", fn_bc, fn_d.partition_broadcast(128), [], ["fn_bc"], "fn_bc")
        for t128 in range(16):
            ti = t128 // 4
            slot = t128 % 2
            b0 = slot * 2
            for c in range(8):
                tr(PS(b0 + c // 4)[:, (c % 4) * 128:(c % 4 + 1) * 128], xT[:, c, t128 * 128:(t128 + 1) * 128], identf,
                   xres(ti, c) + ["identf"], ["ps%d" % (b0 + c // 4)])
            pr = ["ps%d" % b0, "ps%d" % (b0 + 1)]
            ss, ms, rs_ = small[:, 100 + slot * 3:101 + slot * 3], small[:, 101 + slot * 3:102 + slot * 3], small[:, 102 + slot * 3:103 + slot * 3]
            act(junk, PS(b0, 2), AF.Square, pr, ["junk", "fss%d" % slot], accum=ss)
            ts("dve", ms, ss, 1.0 / 1024, EPS, ALU.mult, ALU.add, ["fss%d" % slot], ["fms%d" % slot])
            tt("pool", rs_, ms, mhalf[:, 0:1], ALU.pow, ["fms%d" % slot, "mhalf"], ["frs%d" % slot])
            stt(ob[slot], PS(b0, 2), rs_, fn_bc, ALU.mult, ALU.mult, pr + ["frs%d" % slot, "fn_bc"], ["ob%d" % slot])
            dma("sp", out_d[t128 * 128:(t128 + 1) * 128, :], ob[slot], ["ob%d" % slot], ["out%d" % t128], "out%d" % slot)
        S.op("sp", None, ["out%d" % t for t in range(16)] + list(dump_aps.values()), ())

    stop = ""
    phases = []
    for l in range(DEPTH):
        phases += [("mod%d" % l, mod_phase, l), ("a%d" % l, phase_a, l), ("pool%d" % l, pool_phase, l),
                   ("attn%d" % l, attention, l), ("c1%d" % l, phase_c1, l), ("ffn%d" % l, ffn_phase, l)]
    for name, fn, l in phases:
        fn(l)
        if name == stop:
            break
    final_phase()
    S.build()
    return nc, list(dump_aps.values())


def _host_consts():
    t = np.arange(2048)
    rows = (t // 64).astype(np.float32)
    cols = (t % 64).astype(np.float32)
    inv = (10000.0 ** (-np.arange(16, dtype=np.float32) / 16)).astype(np.float32)
    ang = np.concatenate([rows[:, None] * inv, cols[:, None] * inv], axis=-1).astype(np.float32)
    cosT = np.cos(ang).astype(np.float32).reshape(16, 128, 32).transpose(1, 0, 2).copy()
    sinT = np.sin(ang).astype(np.float32).reshape(16, 128, 32).transpose(1, 0, 2).copy()
    rcb = np.zeros((128, 2, 2, 8), np.float32)
    for ch in range(2):
        for half in range(2):
            w = 2 ** (2 * ch + half + 1)
            for i in range(8):
                cl = (i + w // 2) - max(i - w // 2, 0)
                tr_ = -8 + i
                cr = min(tr_ + w // 2, 0) - (tr_ - w // 2)
                rcb[half * 64:(half + 1) * 64, ch, 0, i] = 1.0 / cl
                rcb[half * 64:(half + 1) * 64, ch, 1, i] = 1.0 / cr
    return cosT, sinT, rcb


_CACHE = {}


def kernel(x, c, ctx, c_ctx, w_mod, b_mod, norm1, norm2, w_in, q_norm, k_norm, sgu_norm, w_s, b_s, pool_w,
           pool_scale, w_out, w_gate, w_up, w_down, final_norm, _dumps=()):
    f = lambda a: np.ascontiguousarray(np.asarray(a, dtype=np.float32))
    x, c, ctx, c_ctx = f(x), f(c), f(ctx), f(c_ctx)
    key = tuple(_dumps)
    if key not in _CACHE:
        _CACHE[key] = build_program(_dumps)
    nc, dnames = _CACHE[key]
    cosT, sinT, rcb = _host_consts()
    vecs = np.concatenate([np.concatenate([f(b_mod)[l].reshape(48, 128), f(norm1)[l].reshape(8, 128),
                                           f(norm2)[l].reshape(8, 128)], 0) for l in range(DEPTH)], 0)
    shared = {
        "w_mod": f(w_mod), "vecs": np.ascontiguousarray(vecs), "pool_scale": f(pool_scale).reshape(4, 128),
        "w_in": f(w_in), "q_norm": f(q_norm), "k_norm": f(k_norm), "sgu_norm": f(sgu_norm), "w_s": f(w_s),
        "b_s": f(b_s), "pool_w": f(pool_w), "w_out": f(w_out), "w_gate": f(w_gate), "w_up": f(w_up),
        "w_down": f(w_down), "final_norm": f(final_norm), "cosT": cosT, "sinT": sinT, "rcb": rcb,
    }
    in_maps = []
    for b in range(8):
        m = dict(shared)
        m["x"] = x[b]
        m["ctx"] = ctx[b]
        m["cvec"] = np.ascontiguousarray(np.concatenate([c[b].reshape(8, 128), c_ctx.reshape(8, 128)], 0))
        in_maps.append(m)
    res = run_bass_kernel_spmd(nc, in_maps, core_ids=list(range(8)))
    out = np.stack([np.asarray(r["out"], dtype=np.float32) for r in res.results], 0)
    if _dumps:
        return out, [{d: np.asarray(r[d]) for d in dnames} for r in res.results]
    return out
```

```python
import contextlib
import numpy as np
import concourse.bass as bass
import concourse.mybir as mybir
from concourse.bass_utils import run_bass_kernel_spmd

F32 = mybir.dt.float32
BF16 = mybir.dt.bfloat16
ALU = mybir.AluOpType
AF = mybir.ActivationFunctionType
AX = mybir.AxisListType

EPS = 1e-6
DEPTH = 2
NT = 2304
TILES = [(0, 512), (512, 512), (1024, 512), (1536, 512), (2048, 256)]
GELU_C = 0.7978845608028654
TM_SKEW = 24
ZIP = 1


class Sched:
    ENG = ("pe", "act", "dve", "pool", "sp")

    def __init__(self, nc, same_engine_sync=True):
        self.nc = nc
        self.ops = {e: [] for e in self.ENG}
        self.ncomp = {e: 0 for e in self.ENG}
        self.waited = {e: {} for e in self.ENG}
        self.last_w = {}
        self.readers = {}
        self.dma_cnt = {}
        self.same = same_engine_sync
        self.semkeys = []

    def _semkey(self, k):
        if k not in self.semkeys:
            self.semkeys.append(k)
        return k

    def op(self, eng, emit, reads=(), writes=(), dma=None):
        deps = []
        for r in reads:
            t = self.last_w.get(r)
            if t is not None:
                deps.append((t, "raw"))
        for w in writes:
            t = self.last_w.get(w)
            if t is not None:
                deps.append((t, "waw"))
            for t in self.readers.get(w, ()):
                deps.append((t, "war"))
        need = {}
        for (key, val, teng), kind in deps:
            if teng == eng:
                if eng == "pe" or not self.same or kind == "war":
                    continue
            if self.waited[eng].get(key, 0) >= val:
                continue
            if need.get(key, 0) < val:
                need[key] = val
        for key, val in need.items():
            self.waited[eng][key] = val
        if dma is not None:
            key = self._semkey("D:" + dma)
            self.dma_cnt[key] = self.dma_cnt.get(key, 0) + 1
            tok = (key, 16 * self.dma_cnt[key], None)
        elif emit is not None:
            key = self._semkey("E:" + eng)
            self.ncomp[eng] += 1
            tok = (key, self.ncomp[eng], eng)
        else:
            tok = None
        self.ops[eng].append((emit, sorted(need.items()), tok))
        if tok is not None:
            for w in writes:
                self.last_w[w] = tok
                self.readers[w] = []
            for r in reads:
                self.readers.setdefault(r, []).append(tok)
        return tok

    def barrier(self):
        state = {}
        for e in self.ENG:
            if self.ncomp[e]:
                state["E:" + e] = self.ncomp[e]
        for k, c in self.dma_cnt.items():
            state[k] = 16 * c
        for eng in self.ENG:
            need = {}
            for key, val in state.items():
                if self.waited[eng].get(key, 0) >= val:
                    continue
                need[key] = val
                self.waited[eng][key] = val
            if need:
                self.ops[eng].append((None, sorted(need.items()), None))

    def build(self):
        nc = self.nc
        with contextlib.ExitStack() as st:
            sems = {}
            for i, k in enumerate(self.semkeys):
                sems[k] = st.enter_context(nc.semaphore("s%d" % i))
            block = st.enter_context(nc.Block())

            def run(name):
                def f(e):
                    for emit, waits, tok in self.ops[name]:
                        for key, val in waits:
                            e.wait_ge(sems[key], val)
                        if emit is None:
                            continue
                        ins = emit(e)
                        if tok is not None:
                            ins.then_inc(sems[tok[0]], 16 if tok[2] is None else 1)
                return f

            block.tensor(run("pe"))
            block.scalar(run("act"))
            block.vector(run("dve"))
            block.gpsimd(run("pool"))
            block.sync(run("sp"))


def build_program(dumps=()):
    nc = bass.Bass("TRN2", target_bir_lowering=False)
    S = Sched(nc)

    def din(name, shape):
        return nc.dram_tensor(name, list(shape), F32, kind="ExternalInput").ap()

    x_d = din("x", [2048, 1024])
    ctx_d = din("ctx", [256, 1024])
    cvec_d = din("cvec", [16, 128])
    wmod_d = din("w_mod", [DEPTH, 1024, 6144])
    vecs_d = din("vecs", [128, 128])
    pscale_d = din("pool_scale", [4, 128])
    win_d = din("w_in", [DEPTH, 1024, 1536])
    qn_d = din("q_norm", [DEPTH, 64])
    kn_d = din("k_norm", [DEPTH, 64])
    sgn_d = din("sgu_norm", [DEPTH, 256])
    ws_d = din("w_s", [DEPTH, 4, 128, 128])
    bs_d = din("b_s", [DEPTH, 4, 128])
    pw_d = din("pool_w", [DEPTH, 4, 64, 64])
    wout_d = din("w_out", [DEPTH, 1024, 1024])
    wg_d = din("w_gate", [DEPTH, 1024, 2816])
    wu_d = din("w_up", [DEPTH, 1024, 2816])
    wd_d = din("w_down", [DEPTH, 2816, 1024])
    fn_d = din("final_norm", [1024])
    cos_d = din("cosT", [128, 16, 32])
    sin_d = din("sinT", [128, 16, 32])
    rcb_d = din("rcb", [128, 2, 2, 8])
    out_d = nc.dram_tensor("out", [2048, 1024], F32, kind="ExternalOutput").ap()

    KB_ = 1024
    uid = [0]

    def nm(s_):
        uid[0] += 1
        return "%s_%d" % (s_, uid[0])

    SB_BASE = 16512
    SB_END = 229376
    off = {"v": SB_BASE}

    def sb(name, shape, dt, at=None):
        nbytes = int(np.prod(shape[1:])) * (4 if dt == F32 else 2)
        if at is None:
            at = off["v"]
            off["v"] = (at + nbytes + 31) // 32 * 32
        return nc.alloc_sbuf_tensor_at(name, list(shape), dt, offset=at).ap(), at + nbytes

    xT, _ = sb("xT", [128, 8, NT], F32)
    MIX0 = off["v"]
    MIX67 = MIX0 + 6 * NT * 2
    mixT, _ = sb("mixT", [128, 8, NT], BF16)
    KA, _ = sb("KA", [128, NT], BF16)
    Vt, _ = sb("Vt", [128, 18, 2, 64], BF16)
    pT, _ = sb("pT", [128, 2, NT], BF16)
    LOC0 = off["v"]
    CONST_BYTES = 15616 + 256 + 640
    TOTAL = SB_END
    C0 = TOTAL - CONST_BYTES
    off["v"] = C0
    ident, _ = sb("ident", [128, 128], BF16)
    identf, _ = sb("identf", [128, 128], F32)
    ones_bf, _ = sb("ones_bf", [128, 128], BF16)
    ones_f, _ = sb("ones_f", [128, 128], F32)
    mhalf, _ = sb("mhalf", [128, 16], F32)
    vecT, _ = sb("vecT", [128, 128], F32)
    cT, _ = sb("cT", [128, 16], F32)
    pscT, _ = sb("pscT", [128, 4], F32)
    silT, _ = sb("silT", [128, 8, 2], BF16)
    modvs = [sb("modv%d" % i, [128, 48, 2], F32)[0] for i in range(DEPTH)]
    a1s = [sb("a1_%d" % i, [128, 8, 2], F32)[0] for i in range(DEPTH)]
    a2s = [sb("a2_%d" % i, [128, 8, 2], F32)[0] for i in range(DEPTH)]
    CUR_L = [0]
    WO = {}
    qg_bc, _ = sb("qg_bc", [128, DEPTH, 64], F32)
    kg_bc, _ = sb("kg_bc", [128, DEPTH, 64], F32)
    sgn_bc, _ = sb("sgn_bc", [128, DEPTH, 256], F32)
    bs_bc, _ = sb("bs_bc", [128, DEPTH, 2, 128], F32)
    wsT, _ = sb("wsT", [128, DEPTH, 4, 128], BF16)
    poolbd, _ = sb("poolbd", [128, DEPTH, 2, 128], BF16)
    invw, _ = sb("invw", [128, 2], F32)
    rcb, _ = sb("rcb_s", [128, 2, 2, 8], F32)
    cosT, _ = sb("cos_s", [128, 16, 32], F32)
    sinT, _ = sb("sin_s", [128, 16, 32], F32)
    small, _ = sb("small", [128, 128], F32)
    assert off["v"] <= TOTAL, off["v"]
    LOC_END = C0

    ps = nc.alloc_psum_tensor("ps", [128, 4096], F32).ap()

    def PS(b, n=1):
        return ps[:, b * 512:(b + n) * 512]

    def loc_alloc(skip=0):
        st = {"v": LOC0 + skip}

        def f(name, shape, dt, base=None):
            nbytes = int(np.prod(shape[1:])) * (4 if dt == F32 else 2)
            at = st["v"]
            st["v"] = (at + nbytes + 31) // 32 * 32
            assert st["v"] <= LOC_END, (name, st["v"], LOC_END)
            return nc.alloc_sbuf_tensor_at(nm(name), list(shape), dt, offset=at).ap()
        f.state = st
        return f

    REC = [None]

    def emit_op(eng, fn, r=(), w=(), dma=None, n=64, kind=""):
        if REC[0] is not None:
            if eng == "pe":
                cost = 0.04 + n / 1800.0
            elif eng == "act":
                cost = 0.2 + n * 0.00085
            elif eng == "dve":
                cost = 0.08 + n * 0.0016
            elif kind == "pow":
                cost = 0.35 + n * 0.16
            else:
                cost = 0.1 + n * 0.0026
            REC[0].append((eng, fn, tuple(r), tuple(w), dma, cost))
        else:
            S.op(eng, fn, r, w, dma=dma)

    def record(fn, *args):
        REC[0] = []
        fn(*args)
        ops = REC[0]
        REC[0] = None
        return ops

    def emit_scheduled(prog, lat=0.2, keep_pe_order=True):
        n_ = len(prog)
        lw, rd = {}, {}
        preds = [set() for _ in range(n_)]
        for i, (eng, fn, r, w, d, c) in enumerate(prog):
            for x in r:
                if x in lw:
                    preds[i].add(lw[x])
            for x in w:
                if x in lw:
                    preds[i].add(lw[x])
                preds[i].update(rd.get(x, ()))
            preds[i].discard(i)
            for x in w:
                lw[x] = i
                rd[x] = []
            for x in r:
                rd.setdefault(x, []).append(i)
        if keep_pe_order:
            prev = None
            for i in range(n_):
                if prog[i][0] == "pe":
                    if prev is not None:
                        preds[i].add(prev)
                    prev = i
        succs = [[] for _ in range(n_)]
        indeg = [len(p) for p in preds]
        for i, p in enumerate(preds):
            for j in p:
                succs[j].append(i)
        finish = [0.0] * n_
        efree = {e: 0.0 for e in Sched.ENG}
        ready = [i for i in range(n_) if indeg[i] == 0]
        order = []
        while ready:
            best, bstart = None, None
            for i in ready:
                st_ = efree[prog[i][0]]
                for j in preds[i]:
                    t_ = finish[j] + (0.0 if prog[j][0] == prog[i][0] == "pe" else lat)
                    if t_ > st_:
                        st_ = t_
                if best is None or st_ < bstart - 1e-9 or (abs(st_ - bstart) <= 1e-9 and i < best):
                    best, bstart = i, st_
            ready.remove(best)
            finish[best] = bstart + prog[best][5]
            efree[prog[best][0]] = finish[best]
            order.append(best)
            for k in succs[best]:
                indeg[k] -= 1
                if indeg[k] == 0:
                    ready.append(k)
        assert len(order) == n_
        for i in order:
            eng, fn, r, w, d, c = prog[i]
            S.op(eng, fn, r, w, dma=d)

    def emit_zip(lists):
        for x in lists:
            for it_ in x:
                if it_ is not None:
                    S.op(it_[0], it_[1], it_[2], it_[3], dma=it_[4])

    def fsz(ap):
        return int(np.prod(ap.shape[1:]))

    def mm(out, lhsT, rhs, start, stop, r, w, **kw):
        emit_op("pe", lambda e: e.matmul(out, lhsT=lhsT, rhs=rhs, start=start, stop=stop, **kw), r, w, n=fsz(rhs))

    def tr(out, in_, idn, r, w):
        emit_op("pe", lambda e: e.transpose(out=out, in_=in_, identity=idn), r, w, n=128)

    def act(out, in_, func, r, w, bias=None, scale=None, accum=None):
        kw = {}
        if bias is not None:
            kw["bias"] = bias
        if scale is not None:
            kw["scale"] = scale
        if accum is not None:
            kw["accum_out"] = accum
        emit_op("act", lambda e: e.activation(out=out, in_=in_, func=func, **kw), r, w, n=fsz(out))

    def tt(eng, out, in0, in1, op, r, w):
        emit_op(eng, lambda e: e.tensor_tensor(out=out, in0=in0, in1=in1, op=op), r, w, n=fsz(out),
                kind="pow" if op == ALU.pow else "")

    def ts(eng, out, in0, s1, s2, op0, op1, r, w):
        if s2 is None:
            emit_op(eng, lambda e: e.tensor_scalar(out=out, in0=in0, scalar1=s1, scalar2=None, op0=op0), r, w, n=fsz(out))
        else:
            emit_op(eng, lambda e: e.tensor_scalar(out=out, in0=in0, scalar1=s1, scalar2=s2, op0=op0, op1=op1), r, w, n=fsz(out))

    def stt(out, in0, scalar, in1, op0, op1, r, w):
        emit_op("dve", lambda e: e.scalar_tensor_tensor(out=out, in0=in0, scalar=scalar, in1=in1, op0=op0, op1=op1), r, w, n=fsz(out))

    def cp(eng, out, in_, r, w):
        if eng == "act":
            emit_op("act", lambda e: e.copy(out=out, in_=in_), r, w, n=fsz(out))
        else:
            emit_op(eng, lambda e: e.tensor_copy(out=out, in_=in_), r, w, n=fsz(out))

    def memset(eng, ap, val, w):
        emit_op(eng, lambda e: e.memset(ap, val), (), w)

    def dma(eng, out, in_, r, w, key):
        emit_op(eng, lambda e: e.dma_start(out=out, in_=in_), r, w, dma=key)

    dump_aps = {}

    def dump(name, ap, res):
        if name not in dumps:
            return
        d = nc.dram_tensor("dbg_" + name, list(ap.shape), ap.dtype, kind="ExternalOutput").ap()
        dma("sp", d, ap, res, ["dbg_" + name], "dbg_" + name)
        dump_aps[name] = "dbg_" + name

    def xres(ti, c=None):
        if c is None:
            return ["x%d_%d" % (ti, cc) for cc in range(8)]
        return ["x%d_%d" % (ti, c)]

    WIN_BYTES = 8 * 1536 * 2

    def load_win(l, win):
        wv = win_d[l].rearrange("(k p) n -> p k n", p=128)
        for g in range(2):
            for c in range(4):
                dma("pool", win[:, :, (c * 2 + g) * 64:(c * 2 + g + 1) * 64], wv[:, :, (g * 4 + c) * 64:(g * 4 + c + 1) * 64],
                    [], ["win"], "win")
        dma("pool", win[:, :, 512:768], wv[:, :, 512:768], [], ["win"], "win")
        dma("pool", win[:, :, 768:1024], wv[:, :, 1024:1280], [], ["win"], "win")
        dma("pool", win[:, :, 1024:1280], wv[:, :, 768:1024], [], ["win"], "win")
        dma("pool", win[:, :, 1280:1536], wv[:, :, 1280:1536], [], ["win"], "win")

    win0 = nc.alloc_sbuf_tensor_at(nm("win0"), [128, 8, 1536], BF16, offset=LOC0).ap()
    L = loc_alloc(WIN_BYTES)
    vst = L("vst", [128, 128], F32)
    vst2 = L("vst2", [32, 128], F32)
    wsst = L("wsst", [128, 8, 128], F32)
    xin = [L("xin0", [128, 1024], F32), L("xin1", [128, 1024], F32)]

    memset("dve", ones_bf, 1.0, ["ones_bf"])
    memset("dve", ones_f, 1.0, ["ones_f"])
    memset("dve", mhalf, -0.5, ["mhalf"])
    memset("dve", identf, 0.0, ["identf"])
    S.op("pool", lambda e: e.affine_select(out=identf, in_=identf, pattern=[[-1, 128]], compare_op=ALU.not_equal,
                                           fill=1.0, base=0, channel_multiplier=1), ["identf"], ["identf"])
    cp("dve", ident, identf, ["identf"], ["ident"])
    load_win(0, win0)
    memset("dve", invw[0:64, 0:1], 0.5, ["invw"])
    memset("dve", invw[64:128, 0:1], 0.25, ["invw"])
    memset("dve", invw[0:64, 1:2], 0.125, ["invw"])
    memset("dve", invw[64:128, 1:2], 0.0625, ["invw"])
    memset("dve", poolbd, 0.0, ["poolbd"])
    memset("dve", vst2, 0.0, ["vst2"])

    dma("sp", vst, vecs_d, [], ["vst"], "vst")
    dma("sp", vst2[0:16, :], cvec_d, ["vst2"], ["vst2"], "vst2")
    dma("sp", vst2[16:20, :], pscale_d, ["vst2"], ["vst2"], "vst2")
    dma("sp", cosT, cos_d, [], ["cosT"], "cosT")
    dma("sp", sinT, sin_d, [], ["sinT"], "sinT")
    dma("sp", rcb, rcb_d, [], ["rcb"], "rcb")
    for l in range(DEPTH):
        dma("sp", qg_bc[:, l, :], qn_d[l].partition_broadcast(128), [], ["qg_bc"], "qg")
        dma("sp", kg_bc[:, l, :], kn_d[l].partition_broadcast(128), [], ["kg_bc"], "kg")
        dma("sp", sgn_bc[:, l, :], sgn_d[l].partition_broadcast(128), [], ["sgn_bc"], "sgn")
        for g in range(4):
            h0 = (g % 2) * 64
            dma("sp", bs_bc[h0:h0 + 64, l, g // 2, :], bs_d[l, g].partition_broadcast(64), [], ["bs_bc"], "bs")
            dma("pool", poolbd[h0:h0 + 64, l, g // 2, h0:h0 + 64], pw_d[l, g], ["poolbd"], ["poolbd"], "poolbd")
        dma("sp", wsst[:, l * 4:(l + 1) * 4, :], ws_d[l].rearrange("h p q -> p h q"), [], ["wsst"], "wsst")
    ts("dve", qg_bc, qg_bc, 0.125, None, ALU.mult, None, ["qg_bc"], ["qg_bc"])
    ts("dve", sgn_bc, sgn_bc, 0.5, None, ALU.mult, None, ["sgn_bc"], ["sgn_bc"])

    tr(PS(4)[:, 0:128], vst, identf, ["vst", "identf"], ["ps4"])
    cp("dve", vecT, PS(4)[:, 0:128], ["ps4"], ["vecT"])
    tr(PS(5)[:, 0:32], vst2, identf[0:32, 0:32], ["vst2", "identf"], ["ps5"])
    cp("dve", cT, PS(5)[:, 0:16], ["ps5"], ["cT"])
    cp("dve", pscT, PS(5)[:, 16:20], ["ps5"], ["pscT"])
    sil_t = small[:, 0:16]
    act(sil_t, cT, AF.Tanh, ["cT"], ["small"], scale=0.5)
    stt(sil_t, sil_t, 1.0, cT, ALU.add, ALU.mult, ["small", "cT"], ["small"])
    ts("dve", silT.rearrange("p k s -> p s k"), sil_t.rearrange("p (s k) -> p s k", s=2), 0.5, None, ALU.mult, None,
       ["small"], ["silT"])
    for i in range(8):
        b = 6 + (i % 2)
        tr(PS(b)[:, 0:128], wsst[:, i, :], identf, ["wsst", "identf"], ["ps%d" % b])
        cp("dve" if i % 2 else "act", wsT[:, i // 4, i % 4, :], PS(b)[:, 0:128], ["ps%d" % b], ["wsT"])

    for t128 in range(18):
        ti = min(t128 // 4, 4)
        slot = t128 % 2
        src = x_d[t128 * 128:(t128 + 1) * 128, :] if t128 < 16 else ctx_d[(t128 - 16) * 128:(t128 - 15) * 128, :]
        dma("sp", xin[slot], src, [], ["xin%d" % slot], "xin%d" % slot)
        b0 = slot * 2
        for c in range(8):
            tr(PS(b0 + c // 4)[:, (c % 4) * 128:(c % 4 + 1) * 128], xin[slot][:, c * 128:(c + 1) * 128], identf,
               ["xin%d" % slot, "identf"], ["ps%d" % (b0 + c // 4)])
        for hh in range(2):
            cp("act" if hh else "dve", xT[:, hh * 4:(hh + 1) * 4, t128 * 128:(t128 + 1) * 128],
               PS(b0 + hh).rearrange("p (c t) -> p c t", c=4), ["ps%d" % (b0 + hh)],
               ["x%d_%d" % (ti, c) for c in range(hh * 4, hh * 4 + 4)])

    def mod_chunk(l, ch, wm):
        pm = PS(5)[:, 0:96]
        s = ch % 2
        dma("pool", wm[s], wmod_d[l].rearrange("(k p) n -> p k n", p=128)[:, :, ch * 512:(ch + 1) * 512],
            [], ["wm%d" % s], "wm%d" % s)
        for jj in range(4):
            jc = ch * 4 + jj
            for k in range(8):
                mm(pm[:, jc * 2:jc * 2 + 2], wm[s][:, k, jj * 128:(jj + 1) * 128], silT[:, k, :], k == 0, k == 7,
                   ["wm%d" % s, "silT"], ["ps5"])

    def mod_finish(l):
        pm = PS(5)[:, 0:96]
        modv, a1, a2 = modvs[l], a1s[l], a2s[l]
        tt("dve", modv, pm.rearrange("p (j s) -> p j s", s=2),
           vecT[:, l * 64:l * 64 + 48].unsqueeze(2).broadcast_to([128, 48, 2]), ALU.add, ["ps5", "vecT"], ["modv%d" % l])
        stt(a1, modv[:, 8:16, :], 1.0, vecT[:, l * 64 + 48:l * 64 + 56].unsqueeze(2).broadcast_to([128, 8, 2]),
            ALU.add, ALU.mult, ["modv%d" % l, "vecT"], ["a1_%d" % l])
        stt(a2, modv[:, 32:40, :], 1.0, vecT[:, l * 64 + 56:l * 64 + 64].unsqueeze(2).broadcast_to([128, 8, 2]),
            ALU.add, ALU.mult, ["modv%d" % l, "vecT"], ["a2_%d" % l])

    def mod_phase(l):
        if l > 0:
            return
        wm = [L("wm0", [128, 8, 512], BF16), L("wm1", [128, 8, 512], BF16)]
        for ch in range(12):
            mod_chunk(0, ch, wm)
        mod_finish(0)

    def MOD(j, c, s):
        return modvs[CUR_L[0]][:, j * 8 + c, s:s + 1]

    def MODR():
        return "modv%d" % CUR_L[0]

    def norm_mod(ti, hbuf, hres, aT, bj, bS, bB, R, rs, tmp, tmpres):
        t0, T = TILES[ti]
        s = 1 if ti == 4 else 0
        nsub = T // 128
        act(hbuf[:, :, 0:T], xT[:, :, t0:t0 + T], AF.Square, xres(ti), hres)
        for sub in range(nsub):
            for k in range(8):
                mm(PS(bS)[:, sub:sub + 1], hbuf[:, k, sub * 128:(sub + 1) * 128], ones_bf[:, 0:1], k == 0, k == 7,
                   [hres[k], "ones_bf"], ["ps%d" % bS])
        ts("dve", rs[:, 0:nsub], PS(bS)[:, 0:nsub], 1.0 / 1024, EPS, ALU.mult, ALU.add, ["ps%d" % bS], ["rs_ms"])
        tt("pool", rs[:, 4:4 + nsub], rs[:, 0:nsub], mhalf[:, 0:nsub], ALU.pow, ["rs_ms", "mhalf"], ["rs_r"])
        for sub in range(nsub):
            ts("dve", R[:, sub, :], ones_f, rs[:, 4 + sub:5 + sub], None, ALU.mult, None, ["rs_r", "ones_f"],
               ["R%d" % sub, "gsc"])
            mm(PS(bB)[:, sub * 128:(sub + 1) * 128], R[:, sub, :], identf, True, True, ["R%d" % sub, "gsc", "identf"],
               ["ps%d" % bB])
        for c in range(8):
            j = c % 2
            stt(tmp[j][:, 0:T], xT[:, c, t0:t0 + T], aT[:, c, s:s + 1], PS(bB)[:, 0:T], ALU.mult, ALU.mult,
                xres(ti, c) + ["ps%d" % bB, "a1_%d" % CUR_L[0], "a2_%d" % CUR_L[0]], [tmpres[j]])
            act(hbuf[:, c, 0:T], tmp[j][:, 0:T], AF.Identity, [tmpres[j], MODR()], [hres[c]], bias=MOD(bj, c, s))

    def gelu2(src, dst, g1, r_src, w_dst, g1res):
        act(g1, src, AF.Square, r_src, [g1res])
        ts("dve", g1, g1, 0.044715, 1.0, ALU.mult, ALU.add, [g1res], [g1res])
        tt("dve", g1, g1, src, ALU.mult, [g1res] + r_src, [g1res])
        act(g1, g1, AF.Tanh, [g1res], [g1res], scale=GELU_C)
        stt(dst, g1, 1.0, src, ALU.add, ALU.mult, [g1res] + r_src, w_dst)

    def phase_a(l):
        S.barrier()
        CUR_L[0] = l
        last = l == DEPTH - 1
        La = loc_alloc()
        m67 = {"v": MIX67}

        def Lm(name, shape, dt):
            nbytes = int(np.prod(shape[1:])) * (4 if dt == F32 else 2)
            at = m67["v"]
            end = (at + nbytes + 31) // 32 * 32
            if end <= MIX67 + 2 * NT * 2:
                m67["v"] = end
                return nc.alloc_sbuf_tensor_at(nm(name), list(shape), dt, offset=at).ap()
            return La(name, shape, dt)

        if l == 0:
            win = win0
            La.state["v"] = LOC0 + WIN_BYTES
        else:
            win = La("win", [128, 8, 1536], BF16)
        hq = [La("hq0", [128, 8, 512], BF16), La("hq1", [128, 8, 512], BF16)]
        R = La("R", [128, 4, 128], F32)
        tmp = [La("tmp0", [128, 512], F32), La("tmp1", [128, 512], F32)]
        uT0_ = La("uT0", [128, 2, 512], BF16)
        uTs = [uT0_, uT0_]
        gsc = R.rearrange("p s t -> p (s t)")
        TS = []
        for pq in range(2):
            A = La
            g1_ = A("g1", [128, 256], F32)
            TS.append(dict(qn=A("qn", [128, 10, 64], F32), rt=[A("rt0", [128, 10, 32], F32), A("rt1", [128, 10, 32], F32)],
                           qb=A("qb", [128, 10, 64], BF16), g1=g1_, v2=A("v2", [128, 256], F32),
                           vn=A("vn", [128, 256], BF16), sg=g1_.rearrange("p (c t) -> p c t", c=2)))
        if l > 0:
            load_win(l, win)

        def hres_of(ti):
            return ["hq%d_%d" % (ti % 2, k) for k in range(8)]

        def tm_mm(ti, sub, need_q):
            hbuf, hres = hq[ti % 2], hres_of(ti)
            bq, bk = (4, 5) if sub % 2 == 0 else (2, 3)
            for k in range(8):
                lt = hbuf[:, k, sub * 128:(sub + 1) * 128]
                if need_q:
                    mm(PS(bq), lt, win[:, k, 0:512], k == 0, k == 7, [hres[k], "win"], ["ps%d" % bq])
                mm(PS(bk), lt, win[:, k, 512:1024], k == 0, k == 7, [hres[k], "win"], ["ps%d" % bk])

        def tm_post(ti, sub, full):
            t0, T = TILES[ti]
            is_ctx = ti == 4
            pq = sub % 2
            X = TS[pq]
            qn, rt, qb, g1, v2, vn, sg = X["qn"], X["rt"], X["qb"], X["g1"], X["v2"], X["vn"], X["sg"]
            sqq = qn.rearrange("p h d -> p (h d)")
            uT = uTs[ti % 2]
            P = lambda n: "%s_%d" % (n, pq)
            t128 = t0 // 128 + sub
            tok = slice(t128 * 128, (t128 + 1) * 128)
            bq, bk = (4, 5) if pq == 0 else (2, 3)
            psT6 = PS(6).bitcast(BF16)
            psT7 = PS(7).bitcast(BF16)
            qT = psT6[:, pq * 512:(pq + 1) * 512]
            kT = psT7[:, pq * 128:(pq + 1) * 128]
            psG = PS(bk)[:, 256:512]
            rq, rk = ["ps%d" % bq], ["ps%d" % bk]
            psQ, psK, psV, psGV = PS(bq), PS(bk)[:, 0:128], PS(bk)[:, 128:256], PS(bk)[:, 256:512]
            h0 = 0 if full else 8
            h1 = 11 if full else 10
            sc0 = 16 + pq * 40
            ss, ms, rs = small[:, sc0:sc0 + 11], small[:, sc0 + 11:sc0 + 22], small[:, sc0 + 22:sc0 + 33]
            cp("act", Vt[:, t128, :, :], psV.rearrange("p (g d) -> p g d", g=2), rk, ["V%d" % t128])
            if full:
                gelu2(psGV, v2, g1, rk, [P("v2")], P("g1"))
                act(g1, v2, AF.Square, [P("v2")], [P("g1"), P("ss")], accum=ss[:, 10:11])
                act(sqq[:, 0:512], psQ, AF.Square, rq, [P("qn_q")])
            act(sqq[:, 512:640], psK, AF.Square, rk, [P("qn_k")])
            emit_op("dve", lambda e: e.tensor_reduce(out=ss[:, h0:10], in_=qn[:, h0:10, :], axis=AX.X, op=ALU.add),
                    [P("qn_q"), P("qn_k")], [P("ss")], n=640)
            ts("dve", ms[:, h0:10], ss[:, h0:10], 1.0 / 64, EPS, ALU.mult, ALU.add, [P("ss")], [P("ms")])
            if full:
                ts("dve", ms[:, 10:11], ss[:, 10:11], 0.25 / 256, EPS, ALU.mult, ALU.add, [P("ss")], [P("ms")])
            tt("pool", rs[:, h0:h1], ms[:, h0:h1], mhalf[:, h0:h1], ALU.pow, [P("ms"), "mhalf"], [P("rs")])
            if full:
                stt(vn, v2, rs[:, 10:11], sgn_bc[:, l, :], ALU.mult, ALU.mult, [P("v2"), P("rs"), "sgn_bc"], [P("vn")])
                for h in range(4):
                    o0 = (h % 2) * 64
                    mm(psG[o0:o0 + 64, (h // 2) * 128:(h // 2 + 1) * 128], vn[:, h * 64:(h + 1) * 64], wsT[:, l, h, :],
                       True, True, [P("vn"), "wsT"], ["ps%d" % bk], tile_position=(0, o0))
                tt("dve", sg, psG.rearrange("p (c t) -> p c t", c=2), bs_bc[:, l, :, :], ALU.add,
                   ["ps%d" % bk, "bs_bc"], [P("g1")])
                stt(mixT[:, 4:6, tok], sg, 0.5, uT[:, :, sub * 128:(sub + 1) * 128], ALU.mult, ALU.mult,
                    [P("g1"), "uT0_0", "uT0_1"], ["mix4_%d" % ti, "mix5_%d" % ti])
                tt("dve", qn[:, 0:8, :], psQ.rearrange("p (h d) -> p h d", d=64),
                   rs[:, 0:8].unsqueeze(2).broadcast_to([128, 8, 64]), ALU.mult, rq + [P("rs")], [P("qn_q")])
                tt("pool", qn[:, 0:8, :], qn[:, 0:8, :], qg_bc[:, l, :].unsqueeze(1).broadcast_to([128, 8, 64]), ALU.mult,
                   [P("qn_q"), "qg_bc"], [P("qn_q")])
            tt("dve", qn[:, 8:10, :], psK.rearrange("p (h d) -> p h d", d=64),
               rs[:, 8:10].unsqueeze(2).broadcast_to([128, 2, 64]), ALU.mult, rk + [P("rs")], [P("qn_k")])
            tt("pool", qn[:, 8:10, :], qn[:, 8:10, :], kg_bc[:, l, :].unsqueeze(1).broadcast_to([128, 2, 64]), ALU.mult,
               [P("qn_k"), "kg_bc"], [P("qn_k")])
            nh = 10 - h0
            if not is_ctx:
                cs = cosT[:, t128, :].unsqueeze(1).broadcast_to([128, nh, 32])
                sn = sinT[:, t128, :].unsqueeze(1).broadcast_to([128, nh, 32])
                x1, x2 = qn[:, h0:10, 0:32], qn[:, h0:10, 32:64]
                rr = [P("qn_q"), P("qn_k"), "cosT", "sinT"]
                tt("pool", rt[0][:, h0:10, :], x1, cs, ALU.mult, rr, [P("rt0")])
                tt("pool", rt[1][:, h0:10, :], x2, sn, ALU.mult, rr, [P("rt1")])
                tt("dve", qb[:, h0:10, 0:32], rt[0][:, h0:10, :], rt[1][:, h0:10, :], ALU.subtract, [P("rt0"), P("rt1")], [P("qb")])
                tt("pool", rt[0][:, h0:10, :], x2, cs, ALU.mult, rr, [P("rt0")])
                tt("pool", rt[1][:, h0:10, :], x1, sn, ALU.mult, rr, [P("rt1")])
                tt("dve", qb[:, h0:10, 32:64], rt[0][:, h0:10, :], rt[1][:, h0:10, :], ALU.add, [P("rt0"), P("rt1")], [P("qb")])
            else:
                cp("dve", qb[:, h0:10, :], qn[:, h0:10, :], [P("qn_q"), P("qn_k")], [P("qb")])
            qbf = qb.rearrange("p h d -> p (h d)")
            if full:
                for c in range(4):
                    tr(qT[:, c * 128:(c + 1) * 128], qbf[:, c * 128:(c + 1) * 128], ident, [P("qb"), "ident"], ["ps6"])
            tr(kT, qbf[:, 512:640], ident, [P("qb"), "ident"], ["ps7"])
            if full:
                cp("act", mixT[:, 0:4, tok], qT.rearrange("p (c t) -> p c t", c=4), ["ps6"],
                   ["mix%d_%d" % (c, ti) for c in range(4)])
            cp("act", KA[:, tok], kT, ["ps7"], ["K%d" % t128])

        def norm_fm(ti):
            t0, T = TILES[ti]
            full = not (last and ti == 4)
            hbuf, hres = hq[ti % 2], hres_of(ti)
            uT = uTs[ti % 2]
            norm_mod(ti, hbuf, hres, a1s[l], 0, 0, 1, R, small[:, 0:8], tmp, ["tmp0", "tmp1"])
            if full:
                for fc in range(4):
                    b = 2 + fc % 2
                    for k in range(8):
                        mm(PS(b)[:, 0:T], win[:, k, 1024 + fc * 128:1024 + (fc + 1) * 128], hbuf[:, k, 0:T], k == 0, k == 7,
                           [hres[k], "win"], ["ps%d" % b])
                    if fc < 2:
                        gelu2(PS(b)[:, 0:T], uT[:, fc, 0:T], gsc[:, 0:T], ["ps%d" % b], ["uT0_%d" % fc], "gsc")
                    else:
                        cp("act", pT[:, fc - 2, t0:t0 + T], PS(b)[:, 0:T], ["ps%d" % b], ["pT%d_%d" % (fc - 2, ti)])

        def fullf(ti):
            return not (last and ti == 4)

        def zipped(lists, burst=None, at=0, pre=None):
            out = list(pre) if pre else []
            n_ = max(len(x) for x in lists)
            for i in range(max(n_, at + 1)):
                for x in lists:
                    if i < len(x) and x[i] is not None:
                        out.append(x[i])
                if burst is not None and i == at:
                    out.extend(burst)
            return out

        prog = record(norm_fm, 0) + record(tm_mm, 0, 0, fullf(0)) + record(tm_mm, 0, 1, fullf(0))
        for ti in range(5):
            nsub = TILES[ti][1] // 128
            full = fullf(ti)
            lists = [record(tm_post, ti, 0, full), record(tm_post, ti, 1, full)]
            burst = pre = nf_rest = None
            if nsub == 4:
                burst = record(tm_mm, ti, 2, full) + record(tm_mm, ti, 3, full)
                if ti + 1 < 5:
                    nf = record(norm_fm, ti + 1)
                    ns1 = TILES[ti + 1][1] // 128
                    n_norm = 1 + 10 * ns1 + 18
                    pre, nf_rest = nf[:n_norm], nf[n_norm:]
            prog += zipped(lists, burst, TM_SKEW, pre)
            if nsub == 4:
                lists = [record(tm_post, ti, 2, full), record(tm_post, ti, 3, full)]
                burst = None
                if ti + 1 < 5:
                    burst = (nf_rest + record(tm_mm, ti + 1, 0, fullf(ti + 1))
                             + record(tm_mm, ti + 1, 1, fullf(ti + 1)))
                prog += zipped(lists, burst, TM_SKEW)
        if ZIP:
            emit_scheduled(prog)
        else:
            emit_zip([prog])

    def pool_phase(l):
        return

    def pool_prep(l, A):
        last = l == DEPTH - 1
        P0 = A("P0", [128, 2064], F32)
        s2 = A("s2", [128, 2064], F32)
        s4 = A("s4", [128, 2064], F32)
        s8 = A("s8", [128, 2064], F32)
        s16 = s2
        fx = A("fx", [128, 16], F32)
        streams = [(0, 2048, [0, 1, 2, 3])] + ([] if last else [(2048, 256, [4])])
        for (t0, N, tis) in streams:
            for ch in range(2):
                pres = ["pT%d_%d" % (ch, ti) for ti in tis]
                memset("dve", P0[:, 0:8], 0.0, ["P0"])
                memset("dve", P0[:, 8 + N:16 + N], 0.0, ["P0"])
                cp("dve", P0[:, 8:8 + N], pT[:, ch, t0:t0 + N], pres, ["P0"])
                tt("dve", s2[:, 1:N + 16], P0[:, 0:N + 15], P0[:, 1:N + 16], ALU.add, ["P0"], ["s2"])
                tt("dve", s4[:, 2:N + 15], s2[:, 1:N + 14], s2[:, 3:N + 16], ALU.add, ["s2"], ["s4"])
                if ch == 0:
                    srcs = [(0, 64, s2, 8), (64, 128, s4, 8)]
                else:
                    tt("dve", s8[:, 4:N + 13], s4[:, 2:N + 11], s4[:, 6:N + 15], ALU.add, ["s4"], ["s8"])
                    tt("dve", s16[64:128, 0:N], s8[64:128, 4:N + 4], s8[64:128, 12:N + 12], ALU.add, ["s8"], ["s2"])
                    srcs = [(0, 64, s8, 8), (64, 128, s16, 0)]
                for (p0, p1, sw, o_) in srcs:
                    stt(pT[p0:p1, ch, t0:t0 + N], sw[p0:p1, o_:o_ + N], invw[p0:p1, ch:ch + 1], P0[p0:p1, 8:8 + N],
                        ALU.mult, ALU.subtract, ["s2", "s4", "s8", "P0", "invw"], pres)
                    for side, c0 in ((0, 0), (1, N - 8)):
                        tt("dve", fx[p0:p1, side * 8:side * 8 + 8], sw[p0:p1, o_ + c0:o_ + c0 + 8], rcb[p0:p1, ch, side, :],
                           ALU.mult, ["s2", "s4", "s8", "rcb"], ["fx"])
                        tt("dve", pT[p0:p1, ch, t0 + c0:t0 + c0 + 8], fx[p0:p1, side * 8:side * 8 + 8],
                           P0[p0:p1, 8 + c0:16 + c0], ALU.subtract, ["fx", "P0"] + pres, pres)

    def pool_finish(l):
        last = l == DEPTH - 1
        i = 0
        for ti in range(4 if last else 5):
            t0, T = TILES[ti]
            for ch in range(2):
                b = 6 + i % 2
                i += 1
                mm(PS(b)[:, 0:T], poolbd[:, l, ch, :], pT[:, ch, t0:t0 + T], True, True, ["pT%d_%d" % (ch, ti), "poolbd"],
                   ["ps%d" % b])
                act(mixT[:, 6 + ch, t0:t0 + T], PS(b)[:, 0:T], AF.Copy, ["ps%d" % b, "pscT"], ["mix%d_%d" % (6 + ch, ti)],
                    scale=pscT[:, l * 2 + ch:l * 2 + ch + 1])

    def attention(l):
        S.barrier()
        CUR_L[0] = l
        last = l == DEPTH - 1
        Lb = loc_alloc()
        PT = [Lb("PT%d" % i, [128, 1024], BF16) for i in range(3)]
        rec = [Lb("rec%d" % i, [128, 512], F32) for i in range(2)]
        wo = Lb("wo", [128, 8, 1024], BF16)
        WO[l] = (wo, Lb.state["v"])
        for e_ in range(2):
            dma("pool", wo[e_ * 64:(e_ + 1) * 64, 0:4, :],
                wout_d[l][e_ * 256:(e_ + 1) * 256, :].rearrange("(c d) n -> d c n", d=64), [], ["wo"], "wo")
        dma("pool", wo[:, 4:8, :], wout_d[l][512:1024, :].rearrange("(k p) n -> p k n", p=128), [], ["wo"], "wo")
        blocks = [(c, ti, list(range(18))) for c in range(4) for ti in range(4)]
        if not last:
            blocks += [(c, 4, [16, 17]) for c in range(4)]
        prep = record(pool_prep, l, Lb)
        prep_pos = [0]
        per_block = -(-len(prep) // 12)

        def drip(n):
            for eng, fn, r_, w_, d_, c_ in prep[prep_pos[0]:prep_pos[0] + n]:
                S.op(eng, fn, r_, w_, dma=d_)
            prep_pos[0] += n

        units = []
        for bi, (c, ti, kts) in enumerate(blocks):
            for ki, kt in enumerate(kts):
                units.append((bi, c, ti, kt, ki, len(kts)))

        def emit_s(j):
            bi, c, ti, kt, ki, nk = units[j]
            t0, T = TILES[ti]
            g = c // 2
            sb0 = (j % 2) * 2
            pt = PT[j % 3]
            qres = ["mix%d_%d" % (c, ti)]
            for e_ in range(2):
                mm(PS(sb0 + e_)[:, 0:T], KA[e_ * 64:(e_ + 1) * 64, kt * 128:(kt + 1) * 128],
                   mixT[e_ * 64:(e_ + 1) * 64, c, t0:t0 + T], True, True, ["K%d" % kt] + qres, ["ps%d" % (sb0 + e_)])
            S.op("act", (lambda pt=pt, sb0=sb0, T=T: lambda e: e.activation(
                out=pt.rearrange("p (e t) -> p e t", e=2)[:, :, 0:T],
                in_=PS(sb0, 2).rearrange("p (e t) -> p e t", e=2)[:, :, 0:T], func=AF.Exp))(),
                ["ps%d" % sb0, "ps%d" % (sb0 + 1)], ["PT%d" % (j % 3)])

        def emit_pv(j):
            bi, c, ti, kt, ki, nk = units[j]
            t0, T = TILES[ti]
            g = c // 2
            bo = 4 + bi % 2
            bd = 6 + bi % 2
            pt = PT[j % 3]
            ptr = "PT%d" % (j % 3)
            for e_ in range(2):
                o0 = e_ * 64
                mm(PS(bo)[o0:o0 + 64, 0:T], Vt[:, kt, e_, :], pt[:, e_ * 512:e_ * 512 + T], ki == 0, ki == nk - 1,
                   ["V%d" % kt, ptr], ["ps%d_%d" % (bo, e_)], tile_position=(0, o0))
            for e_ in range(2):
                o0 = e_ * 64
                mm(PS(bd)[o0:o0 + 64, 0:T], ones_bf[:, 0:64], pt[:, e_ * 512:e_ * 512 + T], ki == 0, ki == nk - 1,
                   ["ones_bf", ptr], ["ps%d_%d" % (bd, e_)], tile_position=(0, o0))
            if ki == nk - 1:
                rc_ = rec[bi % 2]
                S.op("dve", (lambda rc_=rc_, bd=bd, T=T: lambda e: e.reciprocal(out=rc_[:, 0:T], in_=PS(bd)[:, 0:T]))(),
                     ["ps%d_0" % bd, "ps%d_1" % bd], ["rec%d" % (bi % 2)])
                tt("dve", mixT[:, c, t0:t0 + T], PS(bo)[:, 0:T], rc_[:, 0:T], ALU.mult,
                   ["ps%d_0" % bo, "ps%d_1" % bo, "rec%d" % (bi % 2)], ["mix%d_%d" % (c, ti)])
                drip(per_block)

        for j in range(len(units)):
            emit_s(j)
            if j >= 1:
                emit_pv(j - 1)
        emit_pv(len(units) - 1)
        drip(len(prep))

    def phase_c1(l):
        S.barrier()
        CUR_L[0] = l
        last = l == DEPTH - 1
        Lc = loc_alloc()
        wo, wo_end = WO[l]
        Lc.state["v"] = wo_end
        pool_finish(l)
        nxt = l + 1 if l + 1 < DEPTH else None
        if nxt is not None:
            wmn = [Lc("wm0", [128, 8, 512], BF16), Lc("wm1", [128, 8, 512], BF16)]
        i = 0
        for ti in range(4 if last else 5):
            t0, T = TILES[ti]
            s = 1 if ti == 4 else 0
            for dc in range(8):
                if nxt is not None and i % 3 == 0 and i // 3 < 12:
                    mod_chunk(nxt, i // 3, wmn)
                b = i % 4
                i += 1
                for k in range(8):
                    mm(PS(b)[:, 0:T], wo[:, k, dc * 128:(dc + 1) * 128], mixT[:, k, t0:t0 + T], k == 0, k == 7,
                       ["wo", "mix%d_%d" % (k, ti)], ["ps%d" % b])
                stt(xT[:, dc, t0:t0 + T], PS(b)[:, 0:T], MOD(2, dc, s), xT[:, dc, t0:t0 + T], ALU.mult, ALU.add,
                    ["ps%d" % b, MODR()] + xres(ti, dc), xres(ti, dc))
        if nxt is not None:
            mod_finish(nxt)

    def ffn_phase(l):
        S.barrier()
        CUR_L[0] = l
        last = l == DEPTH - 1
        groups = [[0, 1], [2, 3] if last else [2, 3, 4]]
        st = {"v": MIX0}

        def Lf(name, shape, dt):
            nbytes = int(np.prod(shape[1:])) * (4 if dt == F32 else 2)
            at = st["v"]
            st["v"] = (at + nbytes + 31) // 32 * 32
            assert st["v"] <= LOC_END, (name, st["v"], LOC_END)
            return nc.alloc_sbuf_tensor_at(nm(name), list(shape), dt, offset=at).ap()

        h2 = Lf("h2", [128, 8, 1280], BF16)
        aT = Lf("aT", [128, 22, 1280], BF16)
        R = Lf("R", [128, 4, 128], F32)
        tmp = [Lf("tmp0", [128, 512], F32), Lf("tmp1", [128, 512], F32)]
        th = [Lf("th0", [128, 512], F32), Lf("th1", [128, 512], F32)]
        wgs = [Lf("wg%d" % i, [128, 8, 128], BF16) for i in range(3)]
        wus = [Lf("wu%d" % i, [128, 8, 128], BF16) for i in range(3)]
        wds = [Lf("wd%d" % i, [128, 22, 128], BF16) for i in range(2)]
        allmix = ["mix%d_%d" % (c, ti) for c in range(8) for ti in range(5)]
        fence = allmix + ["K%d" % t for t in range(18)] + ["V%d" % t for t in range(18)] + \
            ["pT%d_%d" % (c, ti) for c in range(2) for ti in range(5)] + ["wo", "PT0", "PT1", "PT2", "rec0", "rec1"]
        fi = [0]
        wgv = wg_d[l].rearrange("(k p) n -> p k n", p=128)
        wuv = wu_d[l].rearrange("(k p) n -> p k n", p=128)
        wdv = wd_d[l].rearrange("(f p) n -> p f n", p=128)

        def load_gu(f):
            s_ = f % 3
            dma("pool", wgs[s_], wgv[:, :, f * 128:(f + 1) * 128], [], ["wg%d" % s_], "wg%d" % s_)
            dma("pool", wus[s_], wuv[:, :, f * 128:(f + 1) * 128], [], ["wu%d" % s_], "wu%d" % s_)

        def load_d(dc):
            s_ = dc % 2
            dma("pool", wds[s_], wdv[:, :, dc * 128:(dc + 1) * 128], [], ["wd%d" % s_], "wd%d" % s_)

        def goffs(tis):
            offs, o = {}, 0
            for ti in tis:
                offs[ti] = o
                o += TILES[ti][1]
            return offs

        def h2res(offs, ti):
            return ["h2_%d_%d" % (offs[ti], k) for k in range(8)]

        def do_norm(offs, ti):
            T = TILES[ti][1]
            hb = h2[:, :, offs[ti]:offs[ti] + T]
            norm_mod(ti, hb, h2res(offs, ti), a2s[l], 3, 0, 1, R, small[:, 0:8], tmp, ["tmp0", "tmp1"])

        def emit_list(ops):
            for eng, fn, r, w, d, c in ops:
                S.op(eng, fn, r, w, dma=d)

        for gi, tis in enumerate(groups):
            offs = goffs(tis)
            load_gu(0)
            load_gu(1)
            if gi == 0:
                for ti in tis:
                    do_norm(offs, ti)
                offs1 = goffs(groups[1])
                pieces = []
                for ti in groups[1]:
                    ops = record(do_norm, offs1, ti)
                    nsub = TILES[ti][1] // 128
                    n1 = 1 + 8 * nsub + 2
                    rb = ops[n1:n1 + 2 * nsub]
                    pieces.append((ops[:1], ops[1:n1] + rb[0::2], rb[1::2] + ops[n1 + 2 * nsub:]))
            for f in range(22):
                if f + 2 < 22:
                    load_gu(f + 2)
                elif f == 20:
                    load_d(0)
                elif f == 21:
                    load_d(1)
                s_ = f % 3
                for ti in tis:
                    T = TILES[ti][1]
                    o = offs[ti]
                    j = fi[0] % 2
                    fi[0] += 1
                    bg, bu = 2 + j, 4 + j
                    hres = h2res(offs, ti)
                    for k in range(8):
                        mm(PS(bg)[:, 0:T], wgs[s_][:, k, :], h2[:, k, o:o + T], k == 0, k == 7, ["wg%d" % s_, hres[k]],
                           ["ps%d" % bg])
                    for k in range(8):
                        mm(PS(bu)[:, 0:T], wus[s_][:, k, :], h2[:, k, o:o + T], k == 0, k == 7, ["wu%d" % s_, hres[k]],
                           ["ps%d" % bu])
                    act(th[j][:, 0:T], PS(bg)[:, 0:T], AF.Tanh, ["ps%d" % bg], ["th%d" % j], scale=0.5)
                    stt(th[j][:, 0:T], th[j][:, 0:T], 1.0, PS(bg)[:, 0:T], ALU.add, ALU.mult, ["th%d" % j, "ps%d" % bg],
                        ["th%d" % j])
                    stt(aT[:, f, o:o + T], th[j][:, 0:T], 0.5, PS(bu)[:, 0:T], ALU.mult, ALU.mult, ["th%d" % j, "ps%d" % bu],
                        ["aT%d_%d" % (f, ti)])
            yi = 0
            if gi == 0:
                for pc in pieces:
                    emit_list(pc[0])
            for dc in range(8):
                if gi == 0:
                    if 2 <= dc <= len(pieces) + 1:
                        emit_list(pieces[dc - 2][2])
                    if 1 <= dc <= len(pieces):
                        emit_list(pieces[dc - 1][1])
                s_ = dc % 2
                for ti in tis:
                    t0, T = TILES[ti]
                    o = offs[ti]
                    sidx = 1 if ti == 4 else 0
                    b = 6 + yi % 2
                    yi += 1
                    for f in range(22):
                        mm(PS(b)[:, 0:T], wds[s_][:, f, :], aT[:, f, o:o + T], f == 0, f == 21,
                           ["wd%d" % s_, "aT%d_%d" % (f, ti)], ["ps%d" % b])
                    stt(xT[:, dc, t0:t0 + T], PS(b)[:, 0:T], MOD(5, dc, sidx), xT[:, dc, t0:t0 + T], ALU.mult, ALU.add,
                        ["ps%d" % b, MODR()] + xres(ti, dc), xres(ti, dc))
                if dc + 2 < 8:
                    load_d(dc + 2)

    def final_phase():
        S.barrier()
        st = {"v": MIX0}

        def Lz(name, shape, dt):
            nbytes = int(np.prod(shape[1:])) * (4 if dt == F32 else 2)
            at = st["v"]
            st["v"] = (at + nbytes + 31) // 32 * 32
            assert st["v"] <= LOC_END
            return nc.alloc_sbuf_tensor_at(nm(name), list(shape), dt, offset=at).ap()

        fn_bc = Lz("fn_bc", [128, 1024], F32)
        ob = [Lz("ob0", [128, 1024], F32), Lz("ob1", [128, 1024], F32)]
        junk = Lz("junk", [128, 1024], F32)
        fence = ["h2_%d_%d" % (ti, k) for ti in range(5) for k in range(8)] + \
            ["aT%d_%d" % (f, ti) for f in range(22) for ti in range(5)] + ["R0", "R1", "R2", "R3", "tmp0", "tmp1"]
        dma("sp", fn_bc, fn_d.partition_broadcast(128), [], ["fn_bc"], "fn_bc")
        for t128 in range(16):
            ti = t128 // 4
            slot = t128 % 2
            b0 = slot * 2
            for c in range(8):
                tr(PS(b0 + c // 4)[:, (c % 4) * 128:(c % 4 + 1) * 128], xT[:, c, t128 * 128:(t128 + 1) * 128], identf,
                   xres(ti, c) + ["identf"], ["ps%d" % (b0 + c // 4)])
            pr = ["ps%d" % b0, "ps%d" % (b0 + 1)]
            ss, ms, rs_ = small[:, 100 + slot * 3:101 + slot * 3], small[:, 101 + slot * 3:102 + slot * 3], small[:, 102 + slot * 3:103 + slot * 3]
            act(junk, PS(b0, 2), AF.Square, pr, ["junk", "fss%d" % slot], accum=ss)
            ts("dve", ms, ss, 1.0 / 1024, EPS, ALU.mult, ALU.add, ["fss%d" % slot], ["fms%d" % slot])
            tt("pool", rs_, ms, mhalf[:, 0:1], ALU.pow, ["fms%d" % slot, "mhalf"], ["frs%d" % slot])
            stt(ob[slot], PS(b0, 2), rs_, fn_bc, ALU.mult, ALU.mult, pr + ["frs%d" % slot, "fn_bc"], ["ob%d" % slot])
            dma("sp", out_d[t128 * 128:(t128 + 1) * 128, :], ob[slot], ["ob%d" % slot], ["out%d" % t128], "out%d" % slot)
        S.op("sp", None, ["out%d" % t for t in range(16)] + list(dump_aps.values()), ())

    stop = ""
    phases = []
    for l in range(DEPTH):
        phases += [("mod%d" % l, mod_phase, l), ("a%d" % l, phase_a, l), ("pool%d" % l, pool_phase, l),
                   ("attn%d" % l, attention, l), ("c1%d" % l, phase_c1, l), ("ffn%d" % l, ffn_phase, l)]
    for name, fn, l in phases:
        fn(l)
        if name == stop:
            break
    final_phase()
    S.build()
    return nc, list(dump_aps.values())


def _host_consts():
    t = np.arange(2048)
    rows = (t // 64).astype(np.float64)
    cols = (t % 64).astype(np.float64)
    inv = 10000.0 ** (-np.arange(16, dtype=np.float64) / 16)
    ang = np.concatenate([rows[:, None] * inv, cols[:, None] * inv], axis=-1)
    cosT = np.cos(ang).astype(np.float32).reshape(16, 128, 32).transpose(1, 0, 2).copy()
    sinT = np.sin(ang).astype(np.float32).reshape(16, 128, 32).transpose(1, 0, 2).copy()
    rcb = np.zeros((128, 2, 2, 8), np.float32)
    for ch in range(2):
        for half in range(2):
            w = 2 ** (2 * ch + half + 1)
            for i in range(8):
                cl = (i + w // 2) - max(i - w // 2, 0)
                tr_ = -8 + i
                cr = min(tr_ + w // 2, 0) - (tr_ - w // 2)
                rcb[half * 64:(half + 1) * 64, ch, 0, i] = 1.0 / cl
                rcb[half * 64:(half + 1) * 64, ch, 1, i] = 1.0 / cr
    return cosT, sinT, rcb


_CACHE = {}


def kernel(x, c, ctx, c_ctx, w_mod, b_mod, norm1, norm2, w_in, q_norm, k_norm, sgu_norm, w_s, b_s, pool_w,
           pool_scale, w_out, w_gate, w_up, w_down, final_norm, _dumps=()):
    f = lambda a: np.ascontiguousarray(np.asarray(a, dtype=np.float32))
    x, c, ctx, c_ctx = f(x), f(c), f(ctx), f(c_ctx)
    key = tuple(_dumps)
    if key not in _CACHE:
        _CACHE[key] = build_program(_dumps)
    nc, dnames = _CACHE[key]
    cosT, sinT, rcb = _host_consts()
    vecs = np.concatenate([np.concatenate([f(b_mod)[l].reshape(48, 128), f(norm1)[l].reshape(8, 128),
                                           f(norm2)[l].reshape(8, 128)], 0) for l in range(DEPTH)], 0)
    shared = {
        "w_mod": f(w_mod), "vecs": np.ascontiguousarray(vecs), "pool_scale": f(pool_scale).reshape(4, 128),
        "w_in": f(w_in), "q_norm": f(q_norm), "k_norm": f(k_norm), "sgu_norm": f(sgu_norm), "w_s": f(w_s),
        "b_s": f(b_s), "pool_w": f(pool_w), "w_out": f(w_out), "w_gate": f(w_gate), "w_up": f(w_up),
        "w_down": f(w_down), "final_norm": f(final_norm), "cosT": cosT, "sinT": sinT, "rcb": rcb,
    }
    in_maps = []
    for b in range(8):
        m = dict(shared)
        m["x"] = x[b]
        m["ctx"] = ctx[b]
        m["cvec"] = np.ascontiguousarray(np.concatenate([c[b].reshape(8, 128), c_ctx.reshape(8, 128)], 0))
        in_maps.append(m)
    res = run_bass_kernel_spmd(nc, in_maps, core_ids=list(range(8)))
    out = np.stack([np.asarray(r["out"], dtype=np.float32) for r in res.results], 0)
    if _dumps:
        return out, [{d: np.asarray(r[d]) for d in dnames} for r in res.results]
    return out
```

```python
import contextlib
import numpy as np
import concourse.bass as bass
import concourse.mybir as mybir
from concourse.bass_utils import run_bass_kernel_spmd

F32 = mybir.dt.float32
BF16 = mybir.dt.bfloat16
ALU = mybir.AluOpType
AF = mybir.ActivationFunctionType
AX = mybir.AxisListType

EPS = 1e-6
DEPTH = 2
NT = 2304
TILES = [(0, 512), (512, 512), (1024, 512), (1536, 512), (2048, 256)]
GELU_C = 0.7978845608028654
TM_SKEW = 24
ZIP = 1


class Sched:
    ENG = ("pe", "act", "dve", "pool", "sp")

    def __init__(self, nc, same_engine_sync=True):
        self.nc = nc
        self.ops = {e: [] for e in self.ENG}
        self.ncomp = {e: 0 for e in self.ENG}
        self.waited = {e: {} for e in self.ENG}
        self.last_w = {}
        self.readers = {}
        self.dma_cnt = {}
        self.same = same_engine_sync
        self.semkeys = []

    def _semkey(self, k):
        if k not in self.semkeys:
            self.semkeys.append(k)
        return k

    def op(self, eng, emit, reads=(), writes=(), dma=None):
        deps = []
        for r in reads:
            t = self.last_w.get(r)
            if t is not None:
                deps.append((t, "raw"))
        for w in writes:
            t = self.last_w.get(w)
            if t is not None:
                deps.append((t, "waw"))
            for t in self.readers.get(w, ()):
                deps.append((t, "war"))
        need = {}
        for (key, val, teng), kind in deps:
            if teng == eng:
                if eng == "pe" or not self.same or kind == "war":
                    continue
            if self.waited[eng].get(key, 0) >= val:
                continue
            if need.get(key, 0) < val:
                need[key] = val
        for key, val in need.items():
            self.waited[eng][key] = val
        if dma is not None:
            key = self._semkey("D:" + dma)
            self.dma_cnt[key] = self.dma_cnt.get(key, 0) + 1
            tok = (key, 16 * self.dma_cnt[key], None)
        elif emit is not None:
            key = self._semkey("E:" + eng)
            self.ncomp[eng] += 1
            tok = (key, self.ncomp[eng], eng)
        else:
            tok = None
        self.ops[eng].append((emit, sorted(need.items()), tok))
        if tok is not None:
            for w in writes:
                self.last_w[w] = tok
                self.readers[w] = []
            for r in reads:
                self.readers.setdefault(r, []).append(tok)
        return tok

    def barrier(self):
        state = {}
        for e in self.ENG:
            if self.ncomp[e]:
                state["E:" + e] = self.ncomp[e]
        for k, c in self.dma_cnt.items():
            state[k] = 16 * c
        for eng in self.ENG:
            need = {}
            for key, val in state.items():
                if self.waited[eng].get(key, 0) >= val:
                    continue
                need[key] = val
                self.waited[eng][key] = val
            if need:
                self.ops[eng].append((None, sorted(need.items()), None))

    def build(self):
        nc = self.nc
        with contextlib.ExitStack() as st:
            sems = {}
            for i, k in enumerate(self.semkeys):
                sems[k] = st.enter_context(nc.semaphore("s%d" % i))
            block = st.enter_context(nc.Block())

            def run(name):
                def f(e):
                    for emit, waits, tok in self.ops[name]:
                        for key, val in waits:
                            e.wait_ge(sems[key], val)
                        if emit is None:
                            continue
                        ins = emit(e)
                        if tok is not None:
                            ins.then_inc(sems[tok[0]], 16 if tok[2] is None else 1)
                return f

            block.tensor(run("pe"))
            block.scalar(run("act"))
            block.vector(run("dve"))
            block.gpsimd(run("pool"))
            block.sync(run("sp"))


def build_program(dumps=()):
    nc = bass.Bass("TRN2", target_bir_lowering=False)
    S = Sched(nc)

    def din(name, shape):
        return nc.dram_tensor(name, list(shape), F32, kind="ExternalInput").ap()

    x_d = din("x", [2048, 1024])
    ctx_d = din("ctx", [256, 1024])
    cvec_d = din("cvec", [16, 128])
    wmod_d = din("w_mod", [DEPTH, 1024, 6144])
    vecs_d = din("vecs", [128, 128])
    pscale_d = din("pool_scale", [4, 128])
    win_d = din("w_in", [DEPTH, 1024, 1536])
    qn_d = din("q_norm", [DEPTH, 64])
    kn_d = din("k_norm", [DEPTH, 64])
    sgn_d = din("sgu_norm", [DEPTH, 256])
    ws_d = din("w_s", [DEPTH, 4, 128, 128])
    bs_d = din("b_s", [DEPTH, 4, 128])
    pw_d = din("pool_w", [DEPTH, 4, 64, 64])
    wout_d = din("w_out", [DEPTH, 1024, 1024])
    wg_d = din("w_gate", [DEPTH, 1024, 2816])
    wu_d = din("w_up", [DEPTH, 1024, 2816])
    wd_d = din("w_down", [DEPTH, 2816, 1024])
    fn_d = din("final_norm", [1024])
    cos_d = din("cosT", [128, 16, 32])
    sin_d = din("sinT", [128, 16, 32])
    rcb_d = din("rcb", [128, 2, 2, 8])
    out_d = nc.dram_tensor("out", [2048, 1024], F32, kind="ExternalOutput").ap()

    KB_ = 1024
    uid = [0]

    def nm(s_):
        uid[0] += 1
        return "%s_%d" % (s_, uid[0])

    SB_BASE = 16512
    SB_END = 229376
    off = {"v": SB_BASE}

    def sb(name, shape, dt, at=None):
        nbytes = int(np.prod(shape[1:])) * (4 if dt == F32 else 2)
        if at is None:
            at = off["v"]
            off["v"] = (at + nbytes + 31) // 32 * 32
        return nc.alloc_sbuf_tensor_at(name, list(shape), dt, offset=at).ap(), at + nbytes

    xT, _ = sb("xT", [128, 8, NT], F32)
    MIX0 = off["v"]
    MIX67 = MIX0 + 6 * NT * 2
    mixT, _ = sb("mixT", [128, 8, NT], BF16)
    KA, _ = sb("KA", [128, NT], BF16)
    Vt, _ = sb("Vt", [128, 18, 2, 64], BF16)
    pT, _ = sb("pT", [128, 2, NT], BF16)
    LOC0 = off["v"]
    CONST_BYTES = 15616 + 256 + 640
    TOTAL = SB_END
    C0 = TOTAL - CONST_BYTES
    off["v"] = C0
    ident, _ = sb("ident", [128, 128], BF16)
    identf, _ = sb("identf", [128, 128], F32)
    ones_bf, _ = sb("ones_bf", [128, 128], BF16)
    ones_f, _ = sb("ones_f", [128, 128], F32)
    mhalf, _ = sb("mhalf", [128, 16], F32)
    vecT, _ = sb("vecT", [128, 128], F32)
    cT, _ = sb("cT", [128, 16], F32)
    pscT, _ = sb("pscT", [128, 4], F32)
    silT, _ = sb("silT", [128, 8, 2], BF16)
    modvs = [sb("modv%d" % i, [128, 48, 2], F32)[0] for i in range(DEPTH)]
    a1s = [sb("a1_%d" % i, [128, 8, 2], F32)[0] for i in range(DEPTH)]
    a2s = [sb("a2_%d" % i, [128, 8, 2], F32)[0] for i in range(DEPTH)]
    CUR_L = [0]
    WO = {}
    qg_bc, _ = sb("qg_bc", [128, DEPTH, 64], F32)
    kg_bc, _ = sb("kg_bc", [128, DEPTH, 64], F32)
    sgn_bc, _ = sb("sgn_bc", [128, DEPTH, 256], F32)
    bs_bc, _ = sb("bs_bc", [128, DEPTH, 2, 128], F32)
    wsT, _ = sb("wsT", [128, DEPTH, 4, 128], BF16)
    poolbd, _ = sb("poolbd", [128, DEPTH, 2, 128], BF16)
    invw, _ = sb("invw", [128, 2], F32)
    rcb, _ = sb("rcb_s", [128, 2, 2, 8], F32)
    cosT, _ = sb("cos_s", [128, 16, 32], F32)
    sinT, _ = sb("sin_s", [128, 16, 32], F32)
    small, _ = sb("small", [128, 128], F32)
    assert off["v"] <= TOTAL, off["v"]
    LOC_END = C0

    ps = nc.alloc_psum_tensor("ps", [128, 4096], F32).ap()

    def PS(b, n=1):
        return ps[:, b * 512:(b + n) * 512]

    def loc_alloc(skip=0):
        st = {"v": LOC0 + skip}

        def f(name, shape, dt, base=None):
            nbytes = int(np.prod(shape[1:])) * (4 if dt == F32 else 2)
            at = st["v"]
            st["v"] = (at + nbytes + 31) // 32 * 32
            assert st["v"] <= LOC_END, (name, st["v"], LOC_END)
            return nc.alloc_sbuf_tensor_at(nm(name), list(shape), dt, offset=at).ap()
        f.state = st
        return f

    REC = [None]

    def emit_op(eng, fn, r=(), w=(), dma=None, n=64, kind=""):
        if REC[0] is not None:
            if eng == "pe":
                cost = 0.04 + n / 1800.0
            elif eng == "act":
                cost = 0.2 + n * 0.00085
            elif eng == "dve":
                cost = 0.08 + n * 0.0016
            elif kind == "pow":
                cost = 0.35 + n * 0.16
            else:
                cost = 0.1 + n * 0.0026
            REC[0].append((eng, fn, tuple(r), tuple(w), dma, cost))
        else:
            S.op(eng, fn, r, w, dma=dma)

    def record(fn, *args):
        REC[0] = []
        fn(*args)
        ops = REC[0]
        REC[0] = None
        return ops

    def emit_scheduled(prog, lat=0.2, keep_pe_order=True):
        n_ = len(prog)
        lw, rd = {}, {}
        preds = [set() for _ in range(n_)]
        for i, (eng, fn, r, w, d, c) in enumerate(prog):
            for x in r:
                if x in lw:
                    preds[i].add(lw[x])
            for x in w:
                if x in lw:
                    preds[i].add(lw[x])
                preds[i].update(rd.get(x, ()))
            preds[i].discard(i)
            for x in w:
                lw[x] = i
                rd[x] = []
            for x in r:
                rd.setdefault(x, []).append(i)
        if keep_pe_order:
            prev = None
            for i in range(n_):
                if prog[i][0] == "pe":
                    if prev is not None:
                        preds[i].add(prev)
                    prev = i
        succs = [[] for _ in range(n_)]
        indeg = [len(p) for p in preds]
        for i, p in enumerate(preds):
            for j in p:
                succs[j].append(i)
        finish = [0.0] * n_
        efree = {e: 0.0 for e in Sched.ENG}
        ready = [i for i in range(n_) if indeg[i] == 0]
        order = []
        while ready:
            best, bstart = None, None
            for i in ready:
                st_ = efree[prog[i][0]]
                for j in preds[i]:
                    t_ = finish[j] + (0.0 if prog[j][0] == prog[i][0] == "pe" else lat)
                    if t_ > st_:
                        st_ = t_
                if best is None or st_ < bstart - 1e-9 or (abs(st_ - bstart) <= 1e-9 and i < best):
                    best, bstart = i, st_
            ready.remove(best)
            finish[best] = bstart + prog[best][5]
            efree[prog[best][0]] = finish[best]
            order.append(best)
            for k in succs[best]:
                indeg[k] -= 1
                if indeg[k] == 0:
                    ready.append(k)
        assert len(order) == n_
        for i in order:
            eng, fn, r, w, d, c = prog[i]
            S.op(eng, fn, r, w, dma=d)

    def emit_zip(lists):
        for x in lists:
            for it_ in x:
                if it_ is not None:
                    S.op(it_[0], it_[1], it_[2], it_[3], dma=it_[4])

    def fsz(ap):
        return int(np.prod(ap.shape[1:]))

    def mm(out, lhsT, rhs, start, stop, r, w, **kw):
        emit_op("pe", lambda e: e.matmul(out, lhsT=lhsT, rhs=rhs, start=start, stop=stop, **kw), r, w, n=fsz(rhs))

    def tr(out, in_, idn, r, w):
        emit_op("pe", lambda e: e.transpose(out=out, in_=in_, identity=idn), r, w, n=128)

    def act(out, in_, func, r, w, bias=None, scale=None, accum=None):
        kw = {}
        if bias is not None:
            kw["bias"] = bias
        if scale is not None:
            kw["scale"] = scale
        if accum is not None:
            kw["accum_out"] = accum
        emit_op("act", lambda e: e.activation(out=out, in_=in_, func=func, **kw), r, w, n=fsz(out))

    def tt(eng, out, in0, in1, op, r, w):
        emit_op(eng, lambda e: e.tensor_tensor(out=out, in0=in0, in1=in1, op=op), r, w, n=fsz(out),
                kind="pow" if op == ALU.pow else "")

    def ts(eng, out, in0, s1, s2, op0, op1, r, w):
        if s2 is None:
            emit_op(eng, lambda e: e.tensor_scalar(out=out, in0=in0, scalar1=s1, scalar2=None, op0=op0), r, w, n=fsz(out))
        else:
            emit_op(eng, lambda e: e.tensor_scalar(out=out, in0=in0, scalar1=s1, scalar2=s2, op0=op0, op1=op1), r, w, n=fsz(out))

    def stt(out, in0, scalar, in1, op0, op1, r, w):
        emit_op("dve", lambda e: e.scalar_tensor_tensor(out=out, in0=in0, scalar=scalar, in1=in1, op0=op0, op1=op1), r, w, n=fsz(out))

    def cp(eng, out, in_, r, w):
        if eng == "act":
            emit_op("act", lambda e: e.copy(out=out, in_=in_), r, w, n=fsz(out))
        else:
            emit_op(eng, lambda e: e.tensor_copy(out=out, in_=in_), r, w, n=fsz(out))

    def memset(eng, ap, val, w):
        emit_op(eng, lambda e: e.memset(ap, val), (), w)

    def dma(eng, out, in_, r, w, key):
        emit_op(eng, lambda e: e.dma_start(out=out, in_=in_), r, w, dma=key)

    dump_aps = {}

    def dump(name, ap, res):
        if name not in dumps:
            return
        d = nc.dram_tensor("dbg_" + name, list(ap.shape), ap.dtype, kind="ExternalOutput").ap()
        dma("sp", d, ap, res, ["dbg_" + name], "dbg_" + name)
        dump_aps[name] = "dbg_" + name

    def xres(ti, c=None):
        if c is None:
            return ["x%d_%d" % (ti, cc) for cc in range(8)]
        return ["x%d_%d" % (ti, c)]

    WIN_BYTES = 8 * 1536 * 2

    def load_win(l, win):
        wv = win_d[l].rearrange("(k p) n -> p k n", p=128)
        for g in range(2):
            for c in range(4):
                dma("pool", win[:, :, (c * 2 + g) * 64:(c * 2 + g + 1) * 64], wv[:, :, (g * 4 + c) * 64:(g * 4 + c + 1) * 64],
                    [], ["win"], "win")
        dma("pool", win[:, :, 512:768], wv[:, :, 512:768], [], ["win"], "win")
        dma("pool", win[:, :, 768:1024], wv[:, :, 1024:1280], [], ["win"], "win")
        dma("pool", win[:, :, 1024:1280], wv[:, :, 768:1024], [], ["win"], "win")
        dma("pool", win[:, :, 1280:1536], wv[:, :, 1280:1536], [], ["win"], "win")

    win0 = nc.alloc_sbuf_tensor_at(nm("win0"), [128, 8, 1536], BF16, offset=LOC0).ap()
    L = loc_alloc(WIN_BYTES)
    vst = L("vst", [128, 128], F32)
    vst2 = L("vst2", [32, 128], F32)
    wsst = L("wsst", [128, 8, 128], F32)
    xin = [L("xin0", [128, 1024], F32), L("xin1", [128, 1024], F32)]

    memset("dve", ones_bf, 1.0, ["ones_bf"])
    memset("dve", ones_f, 1.0, ["ones_f"])
    memset("dve", mhalf, -0.5, ["mhalf"])
    memset("dve", identf, 0.0, ["identf"])
    S.op("pool", lambda e: e.affine_select(out=identf, in_=identf, pattern=[[-1, 128]], compare_op=ALU.not_equal,
                                           fill=1.0, base=0, channel_multiplier=1), ["identf"], ["identf"])
    cp("dve", ident, identf, ["identf"], ["ident"])
    load_win(0, win0)
    memset("dve", invw[0:64, 0:1], 0.5, ["invw"])
    memset("dve", invw[64:128, 0:1], 0.25, ["invw"])
    memset("dve", invw[0:64, 1:2], 0.125, ["invw"])
    memset("dve", invw[64:128, 1:2], 0.0625, ["invw"])
    memset("dve", poolbd, 0.0, ["poolbd"])
    memset("dve", vst2, 0.0, ["vst2"])

    dma("sp", vst, vecs_d, [], ["vst"], "vst")
    dma("sp", vst2[0:16, :], cvec_d, ["vst2"], ["vst2"], "vst2")
    dma("sp", vst2[16:20, :], pscale_d, ["vst2"], ["vst2"], "vst2")
    dma("sp", cosT, cos_d, [], ["cosT"], "cosT")
    dma("sp", sinT, sin_d, [], ["sinT"], "sinT")
    dma("sp", rcb, rcb_d, [], ["rcb"], "rcb")
    for l in range(DEPTH):
        dma("sp", qg_bc[:, l, :], qn_d[l].partition_broadcast(128), [], ["qg_bc"], "qg")
        dma("sp", kg_bc[:, l, :], kn_d[l].partition_broadcast(128), [], ["kg_bc"], "kg")
        dma("sp", sgn_bc[:, l, :], sgn_d[l].partition_broadcast(128), [], ["sgn_bc"], "sgn")
        for g in range(4):
            h0 = (g % 2) * 64
            dma("sp", bs_bc[h0:h0 + 64, l, g // 2, :], bs_d[l, g].partition_broadcast(64), [], ["bs_bc"], "bs")
            dma("pool", poolbd[h0:h0 + 64, l, g // 2, h0:h0 + 64], pw_d[l, g], ["poolbd"], ["poolbd"], "poolbd")
        dma("sp", wsst[:, l * 4:(l + 1) * 4, :], ws_d[l].rearrange("h p q -> p h q"), [], ["wsst"], "wsst")
    ts("dve", qg_bc, qg_bc, 0.125, None, ALU.mult, None, ["qg_bc"], ["qg_bc"])
    ts("dve", sgn_bc, sgn_bc, 0.5, None, ALU.mult, None, ["sgn_bc"], ["sgn_bc"])

    tr(PS(4)[:, 0:128], vst, identf, ["vst", "identf"], ["ps4"])
    cp("dve", vecT, PS(4)[:, 0:128], ["ps4"], ["vecT"])
    tr(PS(5)[:, 0:32], vst2, identf[0:32, 0:32], ["vst2", "identf"], ["ps5"])
    cp("dve", cT, PS(5)[:, 0:16], ["ps5"], ["cT"])
    cp("dve", pscT, PS(5)[:, 16:20], ["ps5"], ["pscT"])
    sil_t = small[:, 0:16]
    act(sil_t, cT, AF.Tanh, ["cT"], ["small"], scale=0.5)
    stt(sil_t, sil_t, 1.0, cT, ALU.add, ALU.mult, ["small", "cT"], ["small"])
    ts("dve", silT.rearrange("p k s -> p s k"), sil_t.rearrange("p (s k) -> p s k", s=2), 0.5, None, ALU.mult, None,
       ["small"], ["silT"])
    for i in range(8):
        b = 6 + (i % 2)
        tr(PS(b)[:, 0:128], wsst[:, i, :], identf, ["wsst", "identf"], ["ps%d" % b])
        cp("dve" if i % 2 else "act", wsT[:, i // 4, i % 4, :], PS(b)[:, 0:128], ["ps%d" % b], ["wsT"])

    for t128 in range(18):
        ti = min(t128 // 4, 4)
        slot = t128 % 2
        src = x_d[t128 * 128:(t128 + 1) * 128, :] if t128 < 16 else ctx_d[(t128 - 16) * 128:(t128 - 15) * 128, :]
        dma("sp", xin[slot], src, [], ["xin%d" % slot], "xin%d" % slot)
        b0 = slot * 2
        for c in range(8):
            tr(PS(b0 + c // 4)[:, (c % 4) * 128:(c % 4 + 1) * 128], xin[slot][:, c * 128:(c + 1) * 128], identf,
               ["xin%d" % slot, "identf"], ["ps%d" % (b0 + c // 4)])
        for hh in range(2):
            cp("act" if hh else "dve", xT[:, hh * 4:(hh + 1) * 4, t128 * 128:(t128 + 1) * 128],
               PS(b0 + hh).rearrange("p (c t) -> p c t", c=4), ["ps%d" % (b0 + hh)],
               ["x%d_%d" % (ti, c) for c in range(hh * 4, hh * 4 + 4)])

    def mod_chunk(l, ch, wm):
        pm = PS(5)[:, 0:96]
        s = ch % 2
        dma("pool", wm[s], wmod_d[l].rearrange("(k p) n -> p k n", p=128)[:, :, ch * 512:(ch + 1) * 512],
            [], ["wm%d" % s], "wm%d" % s)
        for jj in range(4):
            jc = ch * 4 + jj
            for k in range(8):
                mm(pm[:, jc * 2:jc * 2 + 2], wm[s][:, k, jj * 128:(jj + 1) * 128], silT[:, k, :], k == 0, k == 7,
                   ["wm%d" % s, "silT"], ["ps5"])

    def mod_finish(l):
        pm = PS(5)[:, 0:96]
        modv, a1, a2 = modvs[l], a1s[l], a2s[l]
        tt("dve", modv, pm.rearrange("p (j s) -> p j s", s=2),
           vecT[:, l * 64:l * 64 + 48].unsqueeze(2).broadcast_to([128, 48, 2]), ALU.add, ["ps5", "vecT"], ["modv%d" % l])
        stt(a1, modv[:, 8:16, :], 1.0, vecT[:, l * 64 + 48:l * 64 + 56].unsqueeze(2).broadcast_to([128, 8, 2]),
            ALU.add, ALU.mult, ["modv%d" % l, "vecT"], ["a1_%d" % l])
        stt(a2, modv[:, 32:40, :], 1.0, vecT[:, l * 64 + 56:l * 64 + 64].unsqueeze(2).broadcast_to([128, 8, 2]),
            ALU.add, ALU.mult, ["modv%d" % l, "vecT"], ["a2_%d" % l])

    def mod_phase(l):
        if l > 0:
            return
        wm = [L("wm0", [128, 8, 512], BF16), L("wm1", [128, 8, 512], BF16)]
        for ch in range(12):
            mod_chunk(0, ch, wm)
        mod_finish(0)

    def MOD(j, c, s):
        return modvs[CUR_L[0]][:, j * 8 + c, s:s + 1]

    def MODR():
        return "modv%d" % CUR_L[0]

    def norm_mod(ti, hbuf, hres, aT, bj, bS, bB, R, rs, tmp, tmpres):
        t0, T = TILES[ti]
        s = 1 if ti == 4 else 0
        nsub = T // 128
        act(hbuf[:, :, 0:T], xT[:, :, t0:t0 + T], AF.Square, xres(ti), hres)
        for sub in range(nsub):
            for k in range(8):
                mm(PS(bS)[:, sub:sub + 1], hbuf[:, k, sub * 128:(sub + 1) * 128], ones_bf[:, 0:1], k == 0, k == 7,
                   [hres[k], "ones_bf"], ["ps%d" % bS])
        ts("dve", rs[:, 0:nsub], PS(bS)[:, 0:nsub], 1.0 / 1024, EPS, ALU.mult, ALU.add, ["ps%d" % bS], ["rs_ms"])
        tt("pool", rs[:, 4:4 + nsub], rs[:, 0:nsub], mhalf[:, 0:nsub], ALU.pow, ["rs_ms", "mhalf"], ["rs_r"])
        for sub in range(nsub):
            ts("dve", R[:, sub, :], ones_f, rs[:, 4 + sub:5 + sub], None, ALU.mult, None, ["rs_r", "ones_f"],
               ["R%d" % sub, "gsc"])
            mm(PS(bB)[:, sub * 128:(sub + 1) * 128], R[:, sub, :], identf, True, True, ["R%d" % sub, "gsc", "identf"],
               ["ps%d" % bB])
        for c in range(8):
            j = c % 2
            stt(tmp[j][:, 0:T], xT[:, c, t0:t0 + T], aT[:, c, s:s + 1], PS(bB)[:, 0:T], ALU.mult, ALU.mult,
                xres(ti, c) + ["ps%d" % bB, "a1_%d" % CUR_L[0], "a2_%d" % CUR_L[0]], [tmpres[j]])
            act(hbuf[:, c, 0:T], tmp[j][:, 0:T], AF.Identity, [tmpres[j], MODR()], [hres[c]], bias=MOD(bj, c, s))

    def gelu2(src, dst, g1, r_src, w_dst, g1res):
        act(g1, src, AF.Square, r_src, [g1res])
        ts("dve", g1, g1, 0.044715, 1.0, ALU.mult, ALU.add, [g1res], [g1res])
        tt("dve", g1, g1, src, ALU.mult, [g1res] + r_src, [g1res])
        act(g1, g1, AF.Tanh, [g1res], [g1res], scale=GELU_C)
        stt(dst, g1, 1.0, src, ALU.add, ALU.mult, [g1res] + r_src, w_dst)

    def phase_a(l):
        S.barrier()
        CUR_L[0] = l
        last = l == DEPTH - 1
        La = loc_alloc()
        m67 = {"v": MIX67}

        def Lm(name, shape, dt):
            nbytes = int(np.prod(shape[1:])) * (4 if dt == F32 else 2)
            at = m67["v"]
            end = (at + nbytes + 31) // 32 * 32
            if end <= MIX67 + 2 * NT * 2:
                m67["v"] = end
                return nc.alloc_sbuf_tensor_at(nm(name), list(shape), dt, offset=at).ap()
            return La(name, shape, dt)

        if l == 0:
            win = win0
            La.state["v"] = LOC0 + WIN_BYTES
        else:
            win = La("win", [128, 8, 1536], BF16)
        hq = [La("hq0", [128, 8, 512], BF16), La("hq1", [128, 8, 512], BF16)]
        R = La("R", [128, 4, 128], F32)
        tmp = [La("tmp0", [128, 512], F32), La("tmp1", [128, 512], F32)]
        uT0_ = La("uT0", [128, 2, 512], BF16)
        uTs = [uT0_, uT0_]
        gsc = R.rearrange("p s t -> p (s t)")
        TS = []
        for pq in range(2):
            A = La
            g1_ = A("g1", [128, 256], F32)
            TS.append(dict(qn=A("qn", [128, 10, 64], F32), rt=[A("rt0", [128, 10, 32], F32), A("rt1", [128, 10, 32], F32)],
                           qb=A("qb", [128, 10, 64], BF16), g1=g1_, v2=A("v2", [128, 256], F32),
                           vn=A("vn", [128, 256], BF16), sg=g1_.rearrange("p (c t) -> p c t", c=2)))
        if l > 0:
            load_win(l, win)

        def hres_of(ti):
            return ["hq%d_%d" % (ti % 2, k) for k in range(8)]

        def tm_mm(ti, sub, need_q):
            hbuf, hres = hq[ti % 2], hres_of(ti)
            bq, bk = (4, 5) if sub % 2 == 0 else (2, 3)
            for k in range(8):
                lt = hbuf[:, k, sub * 128:(sub + 1) * 128]
                if need_q:
                    mm(PS(bq), lt, win[:, k, 0:512], k == 0, k == 7, [hres[k], "win"], ["ps%d" % bq])
                mm(PS(bk), lt, win[:, k, 512:1024], k == 0, k == 7, [hres[k], "win"], ["ps%d" % bk])

        def tm_post(ti, sub, full):
            t0, T = TILES[ti]
            is_ctx = ti == 4
            pq = sub % 2
            X = TS[pq]
            qn, rt, qb, g1, v2, vn, sg = X["qn"], X["rt"], X["qb"], X["g1"], X["v2"], X["vn"], X["sg"]
            sqq = qn.rearrange("p h d -> p (h d)")
            uT = uTs[ti % 2]
            P = lambda n: "%s_%d" % (n, pq)
            t128 = t0 // 128 + sub
            tok = slice(t128 * 128, (t128 + 1) * 128)
            bq, bk = (4, 5) if pq == 0 else (2, 3)
            psT6 = PS(6).bitcast(BF16)
            psT7 = PS(7).bitcast(BF16)
            qT = psT6[:, pq * 512:(pq + 1) * 512]
            kT = psT7[:, pq * 128:(pq + 1) * 128]
            psG = PS(bk)[:, 256:512]
            rq, rk = ["ps%d" % bq], ["ps%d" % bk]
            psQ, psK, psV, psGV = PS(bq), PS(bk)[:, 0:128], PS(bk)[:, 128:256], PS(bk)[:, 256:512]
            h0 = 0 if full else 8
            h1 = 11 if full else 10
            sc0 = 16 + pq * 40
            ss, ms, rs = small[:, sc0:sc0 + 11], small[:, sc0 + 11:sc0 + 22], small[:, sc0 + 22:sc0 + 33]
            cp("act", Vt[:, t128, :, :], psV.rearrange("p (g d) -> p g d", g=2), rk, ["V%d" % t128])
            if full:
                gelu2(psGV, v2, g1, rk, [P("v2")], P("g1"))
                act(g1, v2, AF.Square, [P("v2")], [P("g1"), P("ss")], accum=ss[:, 10:11])
                act(sqq[:, 0:512], psQ, AF.Square, rq, [P("qn_q")])
            act(sqq[:, 512:640], psK, AF.Square, rk, [P("qn_k")])
            emit_op("dve", lambda e: e.tensor_reduce(out=ss[:, h0:10], in_=qn[:, h0:10, :], axis=AX.X, op=ALU.add),
                    [P("qn_q"), P("qn_k")], [P("ss")], n=640)
            ts("dve", ms[:, h0:10], ss[:, h0:10], 1.0 / 64, EPS, ALU.mult, ALU.add, [P("ss")], [P("ms")])
            if full:
                ts("dve", ms[:, 10:11], ss[:, 10:11], 0.25 / 256, EPS, ALU.mult, ALU.add, [P("ss")], [P("ms")])
            tt("pool", rs[:, h0:h1], ms[:, h0:h1], mhalf[:, h0:h1], ALU.pow, [P("ms"), "mhalf"], [P("rs")])
            if full:
                stt(vn, v2, rs[:, 10:11], sgn_bc[:, l, :], ALU.mult, ALU.mult, [P("v2"), P("rs"), "sgn_bc"], [P("vn")])
                for h in range(4):
                    o0 = (h % 2) * 64
                    mm(psG[o0:o0 + 64, (h // 2) * 128:(h // 2 + 1) * 128], vn[:, h * 64:(h + 1) * 64], wsT[:, l, h, :],
                       True, True, [P("vn"), "wsT"], ["ps%d" % bk], tile_position=(0, o0))
                tt("dve", sg, psG.rearrange("p (c t) -> p c t", c=2), bs_bc[:, l, :, :], ALU.add,
                   ["ps%d" % bk, "bs_bc"], [P("g1")])
                stt(mixT[:, 4:6, tok], sg, 0.5, uT[:, :, sub * 128:(sub + 1) * 128], ALU.mult, ALU.mult,
                    [P("g1"), "uT0_0", "uT0_1"], ["mix4_%d" % ti, "mix5_%d" % ti])
                tt("dve", qn[:, 0:8, :], psQ.rearrange("p (h d) -> p h d", d=64),
                   rs[:, 0:8].unsqueeze(2).broadcast_to([128, 8, 64]), ALU.mult, rq + [P("rs")], [P("qn_q")])
                tt("pool", qn[:, 0:8, :], qn[:, 0:8, :], qg_bc[:, l, :].unsqueeze(1).broadcast_to([128, 8, 64]), ALU.mult,
                   [P("qn_q"), "qg_bc"], [P("qn_q")])
            tt("dve", qn[:, 8:10, :], psK.rearrange("p (h d) -> p h d", d=64),
               rs[:, 8:10].unsqueeze(2).broadcast_to([128, 2, 64]), ALU.mult, rk + [P("rs")], [P("qn_k")])
            tt("pool", qn[:, 8:10, :], qn[:, 8:10, :], kg_bc[:, l, :].unsqueeze(1).broadcast_to([128, 2, 64]), ALU.mult,
               [P("qn_k"), "kg_bc"], [P("qn_k")])
            nh = 10 - h0
            if not is_ctx:
                cs = cosT[:, t128, :].unsqueeze(1).broadcast_to([128, nh, 32])
                sn = sinT[:, t128, :].unsqueeze(1).broadcast_to([128, nh, 32])
                x1, x2 = qn[:, h0:10, 0:32], qn[:, h0:10, 32:64]
                rr = [P("qn_q"), P("qn_k"), "cosT", "sinT"]
                tt("pool", rt[0][:, h0:10, :], x1, cs, ALU.mult, rr, [P("rt0")])
                tt("pool", rt[1][:, h0:10, :], x2, sn, ALU.mult, rr, [P("rt1")])
                tt("dve", qb[:, h0:10, 0:32], rt[0][:, h0:10, :], rt[1][:, h0:10, :], ALU.subtract, [P("rt0"), P("rt1")], [P("qb")])
                tt("pool", rt[0][:, h0:10, :], x2, cs, ALU.mult, rr, [P("rt0")])
                tt("pool", rt[1][:, h0:10, :], x1, sn, ALU.mult, rr, [P("rt1")])
                tt("dve", qb[:, h0:10, 32:64], rt[0][:, h0:10, :], rt[1][:, h0:10, :], ALU.add, [P("rt0"), P("rt1")], [P("qb")])
            else:
                cp("dve", qb[:, h0:10, :], qn[:, h0:10, :], [P("qn_q"), P("qn_k")], [P("qb")])
            qbf = qb.rearrange("p h d -> p (h d)")
            if full:
                for c in range(4):
                    tr(qT[:, c * 128:(c + 1) * 128], qbf[:, c * 128:(c + 1) * 128], ident, [P("qb"), "ident"], ["ps6"])
            tr(kT, qbf[:, 512:640], ident, [P("qb"), "ident"], ["ps7"])
            if full:
                cp("act", mixT[:, 0:4, tok], qT.rearrange("p (c t) -> p c t", c=4), ["ps6"],
                   ["mix%d_%d" % (c, ti) for c in range(4)])
            cp("act", KA[:, tok], kT, ["ps7"], ["K%d" % t128])

        def norm_fm(ti):
            t0, T = TILES[ti]
            full = not (last and ti == 4)
            hbuf, hres = hq[ti % 2], hres_of(ti)
            uT = uTs[ti % 2]
            norm_mod(ti, hbuf, hres, a1s[l], 0, 0, 1, R, small[:, 0:8], tmp, ["tmp0", "tmp1"])
            if full:
                for fc in range(4):
                    b = 2 + fc % 2
                    for k in range(8):
                        mm(PS(b)[:, 0:T], win[:, k, 1024 + fc * 128:1024 + (fc + 1) * 128], hbuf[:, k, 0:T], k == 0, k == 7,
                           [hres[k], "win"], ["ps%d" % b])
                    if fc < 2:
                        gelu2(PS(b)[:, 0:T], uT[:, fc, 0:T], gsc[:, 0:T], ["ps%d" % b], ["uT0_%d" % fc], "gsc")
                    else:
                        cp("act", pT[:, fc - 2, t0:t0 + T], PS(b)[:, 0:T], ["ps%d" % b], ["pT%d_%d" % (fc - 2, ti)])

        def fullf(ti):
            return not (last and ti == 4)

        def zipped(lists, burst=None, at=0, pre=None):
            out = list(pre) if pre else []
            n_ = max(len(x) for x in lists)
            for i in range(max(n_, at + 1)):
                for x in lists:
                    if i < len(x) and x[i] is not None:
                        out.append(x[i])
                if burst is not None and i == at:
                    out.extend(burst)
            return out

        prog = record(norm_fm, 0) + record(tm_mm, 0, 0, fullf(0)) + record(tm_mm, 0, 1, fullf(0))
        for ti in range(5):
            nsub = TILES[ti][1] // 128
            full = fullf(ti)
            lists = [record(tm_post, ti, 0, full), record(tm_post, ti, 1, full)]
            burst = pre = nf_rest = None
            if nsub == 4:
                burst = record(tm_mm, ti, 2, full) + record(tm_mm, ti, 3, full)
                if ti + 1 < 5:
                    nf = record(norm_fm, ti + 1)
                    ns1 = TILES[ti + 1][1] // 128
                    n_norm = 1 + 10 * ns1 + 18
                    pre, nf_rest = nf[:n_norm], nf[n_norm:]
            prog += zipped(lists, burst, TM_SKEW, pre)
            if nsub == 4:
                lists = [record(tm_post, ti, 2, full), record(tm_post, ti, 3, full)]
                burst = None
                if ti + 1 < 5:
                    burst = (nf_rest + record(tm_mm, ti + 1, 0, fullf(ti + 1))
                             + record(tm_mm, ti + 1, 1, fullf(ti + 1)))
                prog += zipped(lists, burst, TM_SKEW)
        if ZIP:
            emit_scheduled(prog)
        else:
            emit_zip([prog])

    def pool_phase(l):
        return

    def pool_prep(l, A):
        last = l == DEPTH - 1
        P0 = A("P0", [128, 2064], F32)
        s2 = A("s2", [128, 2064], F32)
        s4 = A("s4", [128, 2064], F32)
        s8 = A("s8", [128, 2064], F32)
        s16 = s2
        fx = A("fx", [128, 16], F32)
        streams = [(0, 2048, [0, 1, 2, 3])] + ([] if last else [(2048, 256, [4])])
        for (t0, N, tis) in streams:
            for ch in range(2):
                pres = ["pT%d_%d" % (ch, ti) for ti in tis]
                memset("dve", P0[:, 0:8], 0.0, ["P0"])
                memset("dve", P0[:, 8 + N:16 + N], 0.0, ["P0"])
                cp("dve", P0[:, 8:8 + N], pT[:, ch, t0:t0 + N], pres, ["P0"])
                tt("dve", s2[:, 1:N + 16], P0[:, 0:N + 15], P0[:, 1:N + 16], ALU.add, ["P0"], ["s2"])
                tt("dve", s4[:, 2:N + 15], s2[:, 1:N + 14], s2[:, 3:N + 16], ALU.add, ["s2"], ["s4"])
                if ch == 0:
                    srcs = [(0, 64, s2, 8), (64, 128, s4, 8)]
                else:
                    tt("dve", s8[:, 4:N + 13], s4[:, 2:N + 11], s4[:, 6:N + 15], ALU.add, ["s4"], ["s8"])
                    tt("dve", s16[64:128, 0:N], s8[64:128, 4:N + 4], s8[64:128, 12:N + 12], ALU.add, ["s8"], ["s2"])
                    srcs = [(0, 64, s8, 8), (64, 128, s16, 0)]
                for (p0, p1, sw, o_) in srcs:
                    stt(pT[p0:p1, ch, t0:t0 + N], sw[p0:p1, o_:o_ + N], invw[p0:p1, ch:ch + 1], P0[p0:p1, 8:8 + N],
                        ALU.mult, ALU.subtract, ["s2", "s4", "s8", "P0", "invw"], pres)
                    for side, c0 in ((0, 0), (1, N - 8)):
                        tt("dve", fx[p0:p1, side * 8:side * 8 + 8], sw[p0:p1, o_ + c0:o_ + c0 + 8], rcb[p0:p1, ch, side, :],
                           ALU.mult, ["s2", "s4", "s8", "rcb"], ["fx"])
                        tt("dve", pT[p0:p1, ch, t0 + c0:t0 + c0 + 8], fx[p0:p1, side * 8:side * 8 + 8],
                           P0[p0:p1, 8 + c0:16 + c0], ALU.subtract, ["fx", "P0"] + pres, pres)

    def pool_finish(l):
        last = l == DEPTH - 1
        i = 0
        for ti in range(4 if last else 5):
            t0, T = TILES[ti]
            for ch in range(2):
                b = 6 + i % 2
                i += 1
                mm(PS(b)[:, 0:T], poolbd[:, l, ch, :], pT[:, ch, t0:t0 + T], True, True, ["pT%d_%d" % (ch, ti), "poolbd"],
                   ["ps%d" % b])
                act(mixT[:, 6 + ch, t0:t0 + T], PS(b)[:, 0:T], AF.Copy, ["ps%d" % b, "pscT"], ["mix%d_%d" % (6 + ch, ti)],
                    scale=pscT[:, l * 2 + ch:l * 2 + ch + 1])

    def attention(l):
        S.barrier()
        CUR_L[0] = l
        last = l == DEPTH - 1
        Lb = loc_alloc()
        PT = [Lb("PT%d" % i, [128, 1024], BF16) for i in range(3)]
        rec = [Lb("rec%d" % i, [128, 512], F32) for i in range(2)]
        wo = Lb("wo", [128, 8, 1024], BF16)
        WO[l] = (wo, Lb.state["v"])
        for e_ in range(2):
            dma("pool", wo[e_ * 64:(e_ + 1) * 64, 0:4, :],
                wout_d[l][e_ * 256:(e_ + 1) * 256, :].rearrange("(c d) n -> d c n", d=64), [], ["wo"], "wo")
        dma("pool", wo[:, 4:8, :], wout_d[l][512:1024, :].rearrange("(k p) n -> p k n", p=128), [], ["wo"], "wo")
        blocks = [(c, ti, list(range(18))) for c in range(4) for ti in range(4)]
        if not last:
            blocks += [(c, 4, [16, 17]) for c in range(4)]
        prep = record(pool_prep, l, Lb)
        prep_pos = [0]
        per_block = -(-len(prep) // 12)

        def drip(n):
            for eng, fn, r_, w_, d_, c_ in prep[prep_pos[0]:prep_pos[0] + n]:
                S.op(eng, fn, r_, w_, dma=d_)
            prep_pos[0] += n

        units = []
        for bi, (c, ti, kts) in enumerate(blocks):
            for ki, kt in enumerate(kts):
                units.append((bi, c, ti, kt, ki, len(kts)))

        def emit_s(j):
            bi, c, ti, kt, ki, nk = units[j]
            t0, T = TILES[ti]
            g = c // 2
            sb0 = (j % 2) * 2
            pt = PT[j % 3]
            qres = ["mix%d_%d" % (c, ti)]
            for e_ in range(2):
                mm(PS(sb0 + e_)[:, 0:T], KA[e_ * 64:(e_ + 1) * 64, kt * 128:(kt + 1) * 128],
                   mixT[e_ * 64:(e_ + 1) * 64, c, t0:t0 + T], True, True, ["K%d" % kt] + qres, ["ps%d" % (sb0 + e_)])
            S.op("act", (lambda pt=pt, sb0=sb0, T=T: lambda e: e.activation(
                out=pt.rearrange("p (e t) -> p e t", e=2)[:, :, 0:T],
                in_=PS(sb0, 2).rearrange("p (e t) -> p e t", e=2)[:, :, 0:T], func=AF.Exp))(),
                ["ps%d" % sb0, "ps%d" % (sb0 + 1)], ["PT%d" % (j % 3)])

        def emit_pv(j):
            bi, c, ti, kt, ki, nk = units[j]
            t0, T = TILES[ti]
            g = c // 2
            bo = 4 + bi % 2
            bd = 6 + bi % 2
            pt = PT[j % 3]
            ptr = "PT%d" % (j % 3)
            for e_ in range(2):
                o0 = e_ * 64
                mm(PS(bo)[o0:o0 + 64, 0:T], Vt[:, kt, e_, :], pt[:, e_ * 512:e_ * 512 + T], ki == 0, ki == nk - 1,
                   ["V%d" % kt, ptr], ["ps%d_%d" % (bo, e_)], tile_position=(0, o0))
            for e_ in range(2):
                o0 = e_ * 64
                mm(PS(bd)[o0:o0 + 64, 0:T], ones_bf[:, 0:64], pt[:, e_ * 512:e_ * 512 + T], ki == 0, ki == nk - 1,
                   ["ones_bf", ptr], ["ps%d_%d" % (bd, e_)], tile_position=(0, o0))
            if ki == nk - 1:
                rc_ = rec[bi % 2]
                S.op("dve", (lambda rc_=rc_, bd=bd, T=T: lambda e: e.reciprocal(out=rc_[:, 0:T], in_=PS(bd)[:, 0:T]))(),
                     ["ps%d_0" % bd, "ps%d_1" % bd], ["rec%d" % (bi % 2)])
                tt("dve", mixT[:, c, t0:t0 + T], PS(bo)[:, 0:T], rc_[:, 0:T], ALU.mult,
                   ["ps%d_0" % bo, "ps%d_1" % bo, "rec%d" % (bi % 2)], ["mix%d_%d" % (c, ti)])
                drip(per_block)

        for j in range(len(units)):
            emit_s(j)
            if j >= 1:
                emit_pv(j - 1)
        emit_pv(len(units) - 1)
        drip(len(prep))

    FCTX = {}
    KA0 = MIX0 + 8 * NT * 2

    def ffn_setup(l):
        def at(name, shape, dt, pos):
            nbytes = int(np.prod(shape[1:])) * (4 if dt == F32 else 2)
            end = (pos + nbytes + 31) // 32 * 32
            return nc.alloc_sbuf_tensor_at(nm(name), list(shape), dt, offset=pos).ap(), end
        p = MIX0
        th = []
        for i in range(2):
            t_, p = at("th%d" % i, [128, 512], F32, p)
            th.append(t_)
        wgs, wus, wds = [], [], []
        for i in range(3):
            t_, p = at("wg%d" % i, [128, 8, 128], BF16, p)
            wgs.append(t_)
        for i in range(3):
            t_, p = at("wu%d" % i, [128, 8, 128], BF16, p)
            wus.append(t_)
        for i in range(2):
            t_, p = at("wd%d" % i, [128, 22, 128], BF16, p)
            wds.append(t_)
        assert p <= KA0, (p, KA0)
        p = KA0
        h2, p = at("h2", [128, 8, 1280], BF16, p)
        R, p = at("R", [128, 4, 128], F32, p)
        tmp = []
        for i in range(2):
            t_, p = at("tmp%d" % i, [128, 512], F32, p)
            tmp.append(t_)
        assert p <= LOC0 + 10240, (p, LOC0 + 10240)
        aT, p = at("aT", [128, 22, 1280], BF16, p)
        assert p <= LOC_END, (p, LOC_END)
        FCTX[l] = (h2, aT, R, tmp, th, wgs, wus, wds)

    def ffn_goffs(tis):
        offs, o = {}, 0
        for ti in tis:
            offs[ti] = o
            o += TILES[ti][1]
        return offs

    def ffn_h2res(offs, ti):
        return ["h2_%d_%d" % (offs[ti], k) for k in range(8)]

    def ffn_norm(l, offs, ti, bS, bB):
        h2, aT, R, tmp = FCTX[l][0:4]
        T = TILES[ti][1]
        hb = h2[:, :, offs[ti]:offs[ti] + T]
        norm_mod(ti, hb, ffn_h2res(offs, ti), a2s[l], 3, bS, bB, R, small[:, 0:8], tmp, ["tmp0", "tmp1"])

    def split_norm(ops, ti):
        nsub = TILES[ti][1] // 128
        n1 = 1 + 8 * nsub + 2
        rb = ops[n1:n1 + 2 * nsub]
        return (ops[:1], ops[1:n1] + rb[0::2], rb[1::2] + ops[n1 + 2 * nsub:])

    def emit_list(ops):
        for eng, fn, r, w, d, c in ops:
            S.op(eng, fn, r, w, dma=d)

    def phase_c1(l):
        S.barrier()
        CUR_L[0] = l
        last = l == DEPTH - 1
        Lc = loc_alloc()
        wo, wo_end = WO[l]
        Lc.state["v"] = wo_end
        pool_finish(l)
        ffn_setup(l)
        g0 = [0, 1]
        offs0 = ffn_goffs(g0)
        pcs = [split_norm(record(ffn_norm, l, offs0, ti, 4, 6), ti) for ti in g0]
        ptres = ["pT%d_%d" % (c, ti) for c in range(2) for ti in range(5)]
        nxt = l + 1 if l + 1 < DEPTH else None
        if nxt is not None:
            wmn = [Lc("wm0", [128, 8, 512], BF16), Lc("wm1", [128, 8, 512], BF16)]
        i = 0
        for ti in range(4 if last else 5):
            t0, T = TILES[ti]
            s = 1 if ti == 4 else 0
            for dc in range(8):
                if nxt is not None and i % 3 == 0 and i // 3 < 12:
                    mod_chunk(nxt, i // 3, wmn)
                if i == 16:
                    for e_ in ("act", "dve"):
                        S.op(e_, None, (), ptres)
                    emit_list(pcs[0][0])
                    emit_list(pcs[1][0])
                elif i == 17:
                    emit_list(pcs[0][1])
                elif i == 19:
                    emit_list(pcs[0][2])
                    emit_list(pcs[1][1])
                elif i == 21:
                    emit_list(pcs[1][2])
                b = i % 4
                i += 1
                for k in range(8):
                    mm(PS(b)[:, 0:T], wo[:, k, dc * 128:(dc + 1) * 128], mixT[:, k, t0:t0 + T], k == 0, k == 7,
                       ["wo", "mix%d_%d" % (k, ti)], ["ps%d" % b])
                stt(xT[:, dc, t0:t0 + T], PS(b)[:, 0:T], MOD(2, dc, s), xT[:, dc, t0:t0 + T], ALU.mult, ALU.add,
                    ["ps%d" % b, MODR()] + xres(ti, dc), xres(ti, dc))
        if nxt is not None:
            mod_finish(nxt)

    def ffn_phase(l):
        S.barrier()
        CUR_L[0] = l
        last = l == DEPTH - 1
        groups = [[0, 1], [2, 3] if last else [2, 3, 4]]
        h2, aT, R, tmp, th, wgs, wus, wds = FCTX[l]
        allmix = ["mix%d_%d" % (c, ti) for c in range(8) for ti in range(5)]
        fence = allmix + ["K%d" % t for t in range(18)] + ["V%d" % t for t in range(18)] + \
            ["pT%d_%d" % (c, ti) for c in range(2) for ti in range(5)] + ["wo", "PT0", "PT1", "PT2", "rec0", "rec1"]
        fi = [0]
        wgv = wg_d[l].rearrange("(k p) n -> p k n", p=128)
        wuv = wu_d[l].rearrange("(k p) n -> p k n", p=128)
        wdv = wd_d[l].rearrange("(f p) n -> p f n", p=128)

        def load_gu(f):
            s_ = f % 3
            dma("pool", wgs[s_], wgv[:, :, f * 128:(f + 1) * 128], [], ["wg%d" % s_], "wg%d" % s_)
            dma("pool", wus[s_], wuv[:, :, f * 128:(f + 1) * 128], [], ["wu%d" % s_], "wu%d" % s_)

        def load_d(dc):
            s_ = dc % 2
            dma("pool", wds[s_], wdv[:, :, dc * 128:(dc + 1) * 128], [], ["wd%d" % s_], "wd%d" % s_)

        goffs, h2res = ffn_goffs, ffn_h2res

        def do_norm(offs, ti):
            ffn_norm(l, offs, ti, 0, 1)

        for gi, tis in enumerate(groups):
            offs = goffs(tis)
            load_gu(0)
            load_gu(1)
            if gi == 0:
                offs1 = goffs(groups[1])
                pieces = []
                for ti in groups[1]:
                    pieces.append(split_norm(record(do_norm, offs1, ti), ti))
            for f in range(22):
                if f + 2 < 22:
                    load_gu(f + 2)
                elif f == 20:
                    load_d(0)
                elif f == 21:
                    load_d(1)
                s_ = f % 3
                for ti in tis:
                    T = TILES[ti][1]
                    o = offs[ti]
                    j = fi[0] % 2
                    fi[0] += 1
                    bg, bu = 2 + j, 4 + j
                    hres = h2res(offs, ti)
                    for k in range(8):
                        mm(PS(bg)[:, 0:T], wgs[s_][:, k, :], h2[:, k, o:o + T], k == 0, k == 7, ["wg%d" % s_, hres[k]],
                           ["ps%d" % bg])
                    for k in range(8):
                        mm(PS(bu)[:, 0:T], wus[s_][:, k, :], h2[:, k, o:o + T], k == 0, k == 7, ["wu%d" % s_, hres[k]],
                           ["ps%d" % bu])
                    act(th[j][:, 0:T], PS(bg)[:, 0:T], AF.Tanh, ["ps%d" % bg], ["th%d" % j], scale=0.5)
                    stt(th[j][:, 0:T], th[j][:, 0:T], 1.0, PS(bg)[:, 0:T], ALU.add, ALU.mult, ["th%d" % j, "ps%d" % bg],
                        ["th%d" % j])
                    stt(aT[:, f, o:o + T], th[j][:, 0:T], 0.5, PS(bu)[:, 0:T], ALU.mult, ALU.mult, ["th%d" % j, "ps%d" % bu],
                        ["aT%d_%d" % (f, ti)])
            yi = 0
            if gi == 0:
                for pc in pieces:
                    emit_list(pc[0])
            for dc in range(8):
                if gi == 0:
                    if 2 <= dc <= len(pieces) + 1:
                        emit_list(pieces[dc - 2][2])
                    if 1 <= dc <= len(pieces):
                        emit_list(pieces[dc - 1][1])
                s_ = dc % 2
                for ti in tis:
                    t0, T = TILES[ti]
                    o = offs[ti]
                    sidx = 1 if ti == 4 else 0
                    b = 6 + yi % 2
                    yi += 1
                    for f in range(22):
                        mm(PS(b)[:, 0:T], wds[s_][:, f, :], aT[:, f, o:o + T], f == 0, f == 21,
                           ["wd%d" % s_, "aT%d_%d" % (f, ti)], ["ps%d" % b])
                    stt(xT[:, dc, t0:t0 + T], PS(b)[:, 0:T], MOD(5, dc, sidx), xT[:, dc, t0:t0 + T], ALU.mult, ALU.add,
                        ["ps%d" % b, MODR()] + xres(ti, dc), xres(ti, dc))
                if dc + 2 < 8:
                    load_d(dc + 2)

    def final_phase():
        S.barrier()
        st = {"v": MIX0}

        def Lz(name, shape, dt):
            nbytes = int(np.prod(shape[1:])) * (4 if dt == F32 else 2)
            at = st["v"]
            st["v"] = (at + nbytes + 31) // 32 * 32
            assert st["v"] <= LOC_END
            return nc.alloc_sbuf_tensor_at(nm(name), list(shape), dt, offset=at).ap()

        fn_bc = Lz("fn_bc", [128, 1024], F32)
        ob = [Lz("ob0", [128, 1024], F32), Lz("ob1", [128, 1024], F32)]
        junk = Lz("junk", [128, 1024], F32)
        fence = ["h2_%d_%d" % (ti, k) for ti in range(5) for k in range(8)] + \
            ["aT%d_%d" % (f, ti) for f in range(22) for ti in range(5)] + ["R0", "R1", "R2", "R3", "tmp0", "tmp1"]
        dma("sp", fn_bc, fn_d.partition_broadcast(128), [], ["fn_bc"], "fn_bc")
        for t128 in range(16):
            ti = t128 // 4
            slot = t128 % 2
            b0 = slot * 2
            for c in range(8):
                tr(PS(b0 + c // 4)[:, (c % 4) * 128:(c % 4 + 1) * 128], xT[:, c, t128 * 128:(t128 + 1) * 128], identf,
                   xres(ti, c) + ["identf"], ["ps%d" % (b0 + c // 4)])
            pr = ["ps%d" % b0, "ps%d" % (b0 + 1)]
            ss, ms, rs_ = small[:, 100 + slot * 3:101 + slot * 3], small[:, 101 + slot * 3:102 + slot * 3], small[:, 102 + slot * 3:103 + slot * 3]
            act(junk, PS(b0, 2), AF.Square, pr, ["junk", "fss%d" % slot], accum=ss)
            ts("dve", ms, ss, 1.0 / 1024, EPS, ALU.mult, ALU.add, ["fss%d" % slot], ["fms%d" % slot])
            tt("pool", rs_, ms, mhalf[:, 0:1], ALU.pow, ["fms%d" % slot, "mhalf"], ["frs%d" % slot])
            stt(ob[slot], PS(b0, 2), rs_, fn_bc, ALU.mult, ALU.mult, pr + ["frs%d" % slot, "fn_bc"], ["ob%d" % slot])
            dma("sp", out_d[t128 * 128:(t128 + 1) * 128, :], ob[slot], ["ob%d" % slot], ["out%d" % t128], "out%d" % slot)
        S.op("sp", None, ["out%d" % t for t in range(16)] + list(dump_aps.values()), ())

    stop = ""
    phases = []
    for l in range(DEPTH):
        phases += [("mod%d" % l, mod_phase, l), ("a%d" % l, phase_a, l), ("pool%d" % l, pool_phase, l),
                   ("attn%d" % l, attention, l), ("c1%d" % l, phase_c1, l), ("ffn%d" % l, ffn_phase, l)]
    for name, fn, l in phases:
        fn(l)
        if name == stop:
            break
    final_phase()
    S.build()
    return nc, list(dump_aps.values())


def _host_consts():
    t = np.arange(2048)
    rows = (t // 64).astype(np.float64)
    cols = (t % 64).astype(np.float64)
    inv = 10000.0 ** (-np.arange(16, dtype=np.float64) / 16)
    ang = np.concatenate([rows[:, None] * inv, cols[:, None] * inv], axis=-1)
    cosT = np.cos(ang).astype(np.float32).reshape(16, 128, 32).transpose(1, 0, 2).copy()
    sinT = np.sin(ang).astype(np.float32).reshape(16, 128, 32).transpose(1, 0, 2).copy()
    rcb = np.zeros((128, 2, 2, 8), np.float32)
    for ch in range(2):
        for half in range(2):
            w = 2 ** (2 * ch + half + 1)
            for i in range(8):
                cl = (i + w // 2) - max(i - w // 2, 0)
                tr_ = -8 + i
                cr = min(tr_ + w // 2, 0) - (tr_ - w // 2)
                rcb[half * 64:(half + 1) * 64, ch, 0, i] = 1.0 / cl
                rcb[half * 64:(half + 1) * 64, ch, 1, i] = 1.0 / cr
    return cosT, sinT, rcb


_CACHE = {}


def kernel(x, c, ctx, c_ctx, w_mod, b_mod, norm1, norm2, w_in, q_norm, k_norm, sgu_norm, w_s, b_s, pool_w,
           pool_scale, w_out, w_gate, w_up, w_down, final_norm, _dumps=()):
    f = lambda a: np.ascontiguousarray(np.asarray(a, dtype=np.float32))
    x, c, ctx, c_ctx = f(x), f(c), f(ctx), f(c_ctx)
    key = tuple(_dumps)
    if key not in _CACHE:
        _CACHE[key] = build_program(_dumps)
    nc, dnames = _CACHE[key]
    cosT, sinT, rcb = _host_consts()
    vecs = np.concatenate([np.concatenate([f(b_mod)[l].reshape(48, 128), f(norm1)[l].reshape(8, 128),
                                           f(norm2)[l].reshape(8, 128)], 0) for l in range(DEPTH)], 0)
    shared = {
        "w_mod": f(w_mod), "vecs": np.ascontiguousarray(vecs), "pool_scale": f(pool_scale).reshape(4, 128),
        "w_in": f(w_in), "q_norm": f(q_norm), "k_norm": f(k_norm), "sgu_norm": f(sgu_norm), "w_s": f(w_s),
        "b_s": f(b_s), "pool_w": f(pool_w), "w_out": f(w_out), "w_gate": f(w_gate), "w_up": f(w_up),
        "w_down": f(w_down), "final_norm": f(final_norm), "cosT": cosT, "sinT": sinT, "rcb": rcb,
    }
    in_maps = []
    for b in range(8):
        m = dict(shared)
        m["x"] = x[b]
        m["ctx"] = ctx[b]
        m["cvec"] = np.ascontiguousarray(np.concatenate([c[b].reshape(8, 128), c_ctx.reshape(8, 128)], 0))
        in_maps.append(m)
    res = run_bass_kernel_spmd(nc, in_maps, core_ids=list(range(8)))
    out = np.stack([np.asarray(r["out"], dtype=np.float32) for r in res.results], 0)
    if _dumps:
        return out, [{d: np.asarray(r[d]) for d in dnames} for r in res.results]
    return out
```

```python
import contextlib
import numpy as np
import concourse.bass as bass
import concourse.mybir as mybir
from concourse.bass_utils import run_bass_kernel_spmd

F32 = mybir.dt.float32
BF16 = mybir.dt.bfloat16
ALU = mybir.AluOpType
AF = mybir.ActivationFunctionType
AX = mybir.AxisListType

EPS = 1e-6
DEPTH = 2
NT = 2304
TILES = [(0, 512), (512, 512), (1024, 512), (1536, 512), (2048, 256)]
GELU_C = 0.7978845608028654
TM_SKEW = 24
ZIP = 1


class Sched:
    ENG = ("pe", "act", "dve", "pool", "sp")

    def __init__(self, nc, same_engine_sync=True):
        self.nc = nc
        self.ops = {e: [] for e in self.ENG}
        self.ncomp = {e: 0 for e in self.ENG}
        self.waited = {e: {} for e in self.ENG}
        self.last_w = {}
        self.readers = {}
        self.dma_cnt = {}
        self.same = same_engine_sync
        self.semkeys = []

    def _semkey(self, k):
        if k not in self.semkeys:
            self.semkeys.append(k)
        return k

    def op(self, eng, emit, reads=(), writes=(), dma=None):
        deps = []
        for r in reads:
            t = self.last_w.get(r)
            if t is not None:
                deps.append((t, "raw"))
        for w in writes:
            t = self.last_w.get(w)
            if t is not None:
                deps.append((t, "waw"))
            for t in self.readers.get(w, ()):
                deps.append((t, "war"))
        need = {}
        for (key, val, teng), kind in deps:
            if teng == eng:
                if eng == "pe" or not self.same or kind == "war":
                    continue
            if self.waited[eng].get(key, 0) >= val:
                continue
            if need.get(key, 0) < val:
                need[key] = val
        for key, val in need.items():
            self.waited[eng][key] = val
        if dma is not None:
            key = self._semkey("D:" + dma)
            self.dma_cnt[key] = self.dma_cnt.get(key, 0) + 1
            tok = (key, 16 * self.dma_cnt[key], None)
        elif emit is not None:
            key = self._semkey("E:" + eng)
            self.ncomp[eng] += 1
            tok = (key, self.ncomp[eng], eng)
        else:
            tok = None
        self.ops[eng].append((emit, sorted(need.items()), tok))
        if tok is not None:
            for w in writes:
                self.last_w[w] = tok
                self.readers[w] = []
            for r in reads:
                self.readers.setdefault(r, []).append(tok)
        return tok

    def barrier(self):
        state = {}
        for e in self.ENG:
            if self.ncomp[e]:
                state["E:" + e] = self.ncomp[e]
        for k, c in self.dma_cnt.items():
            state[k] = 16 * c
        for eng in self.ENG:
            need = {}
            for key, val in state.items():
                if self.waited[eng].get(key, 0) >= val:
                    continue
                need[key] = val
                self.waited[eng][key] = val
            if need:
                self.ops[eng].append((None, sorted(need.items()), None))

    def build(self):
        nc = self.nc
        with contextlib.ExitStack() as st:
            sems = {}
            for i, k in enumerate(self.semkeys):
                sems[k] = st.enter_context(nc.semaphore("s%d" % i))
            block = st.enter_context(nc.Block())

            def run(name):
                def f(e):
                    for emit, waits, tok in self.ops[name]:
                        for key, val in waits:
                            e.wait_ge(sems[key], val)
                        if emit is None:
                            continue
                        ins = emit(e)
                        if tok is not None:
                            ins.then_inc(sems[tok[0]], 16 if tok[2] is None else 1)
                return f

            block.tensor(run("pe"))
            block.scalar(run("act"))
            block.vector(run("dve"))
            block.gpsimd(run("pool"))
            block.sync(run("sp"))


def build_program(dumps=()):
    nc = bass.Bass("TRN2", target_bir_lowering=False)
    S = Sched(nc)

    def din(name, shape):
        return nc.dram_tensor(name, list(shape), F32, kind="ExternalInput").ap()

    x_d = din("x", [2048, 1024])
    ctx_d = din("ctx", [256, 1024])
    cvec_d = din("cvec", [16, 128])
    wmod_d = din("w_mod", [DEPTH, 1024, 6144])
    vecs_d = din("vecs", [128, 128])
    pscale_d = din("pool_scale", [4, 128])
    win_d = din("w_in", [DEPTH, 1024, 1536])
    qn_d = din("q_norm", [DEPTH, 64])
    kn_d = din("k_norm", [DEPTH, 64])
    sgn_d = din("sgu_norm", [DEPTH, 256])
    ws_d = din("w_s", [DEPTH, 4, 128, 128])
    bs_d = din("b_s", [DEPTH, 4, 128])
    pw_d = din("pool_w", [DEPTH, 4, 64, 64])
    wout_d = din("w_out", [DEPTH, 1024, 1024])
    wg_d = din("w_gate", [DEPTH, 1024, 2816])
    wu_d = din("w_up", [DEPTH, 1024, 2816])
    wd_d = din("w_down", [DEPTH, 2816, 1024])
    fn_d = din("final_norm", [1024])
    cos_d = din("cosT", [128, 16, 32])
    sin_d = din("sinT", [128, 16, 32])
    rcb_d = din("rcb", [128, 2, 2, 8])
    out_d = nc.dram_tensor("out", [2048, 1024], F32, kind="ExternalOutput").ap()

    KB_ = 1024
    uid = [0]

    def nm(s_):
        uid[0] += 1
        return "%s_%d" % (s_, uid[0])

    SB_BASE = 16512
    SB_END = 229376
    off = {"v": SB_BASE}

    def sb(name, shape, dt, at=None):
        nbytes = int(np.prod(shape[1:])) * (4 if dt == F32 else 2)
        if at is None:
            at = off["v"]
            off["v"] = (at + nbytes + 31) // 32 * 32
        return nc.alloc_sbuf_tensor_at(name, list(shape), dt, offset=at).ap(), at + nbytes

    xT, _ = sb("xT", [128, 8, NT], F32)
    MIX0 = off["v"]
    MIX67 = MIX0 + 6 * NT * 2
    mixT, _ = sb("mixT", [128, 8, NT], BF16)
    KA, _ = sb("KA", [128, NT], BF16)
    Vt, _ = sb("Vt", [128, 18, 2, 64], BF16)
    pT, _ = sb("pT", [128, 2, NT], BF16)
    LOC0 = off["v"]
    CONST_BYTES = 15616 + 256 + 640
    TOTAL = SB_END
    C0 = TOTAL - CONST_BYTES
    off["v"] = C0
    ident, _ = sb("ident", [128, 128], BF16)
    identf, _ = sb("identf", [128, 128], F32)
    ones_bf, _ = sb("ones_bf", [128, 128], BF16)
    ones_f, _ = sb("ones_f", [128, 128], F32)
    mhalf, _ = sb("mhalf", [128, 16], F32)
    vecT, _ = sb("vecT", [128, 128], F32)
    cT, _ = sb("cT", [128, 16], F32)
    pscT, _ = sb("pscT", [128, 4], F32)
    silT, _ = sb("silT", [128, 8, 2], BF16)
    modvs = [sb("modv%d" % i, [128, 48, 2], F32)[0] for i in range(DEPTH)]
    a1s = [sb("a1_%d" % i, [128, 8, 2], F32)[0] for i in range(DEPTH)]
    a2s = [sb("a2_%d" % i, [128, 8, 2], F32)[0] for i in range(DEPTH)]
    CUR_L = [0]
    WO = {}
    qg_bc, _ = sb("qg_bc", [128, DEPTH, 64], F32)
    kg_bc, _ = sb("kg_bc", [128, DEPTH, 64], F32)
    sgn_bc, _ = sb("sgn_bc", [128, DEPTH, 256], F32)
    bs_bc, _ = sb("bs_bc", [128, DEPTH, 2, 128], F32)
    wsT, _ = sb("wsT", [128, DEPTH, 4, 128], BF16)
    poolbd, _ = sb("poolbd", [128, DEPTH, 2, 128], BF16)
    invw, _ = sb("invw", [128, 2], F32)
    rcb, _ = sb("rcb_s", [128, 2, 2, 8], F32)
    cosT, _ = sb("cos_s", [128, 16, 32], F32)
    sinT, _ = sb("sin_s", [128, 16, 32], F32)
    small, _ = sb("small", [128, 128], F32)
    assert off["v"] <= TOTAL, off["v"]
    LOC_END = C0

    ps = nc.alloc_psum_tensor("ps", [128, 4096], F32).ap()

    def PS(b, n=1):
        return ps[:, b * 512:(b + n) * 512]

    def loc_alloc(skip=0):
        st = {"v": LOC0 + skip}

        def f(name, shape, dt, base=None):
            nbytes = int(np.prod(shape[1:])) * (4 if dt == F32 else 2)
            at = st["v"]
            st["v"] = (at + nbytes + 31) // 32 * 32
            assert st["v"] <= LOC_END, (name, st["v"], LOC_END)
            return nc.alloc_sbuf_tensor_at(nm(name), list(shape), dt, offset=at).ap()
        f.state = st
        return f

    REC = [None]

    def emit_op(eng, fn, r=(), w=(), dma=None, n=64, kind=""):
        if REC[0] is not None:
            if eng == "pe":
                cost = 0.04 + n / 1800.0
            elif eng == "act":
                cost = 0.2 + n * 0.00085
            elif eng == "dve":
                cost = 0.08 + n * 0.0016
            elif kind == "pow":
                cost = 0.35 + n * 0.16
            else:
                cost = 0.1 + n * 0.0026
            REC[0].append((eng, fn, tuple(r), tuple(w), dma, cost))
        else:
            S.op(eng, fn, r, w, dma=dma)

    def record(fn, *args):
        REC[0] = []
        fn(*args)
        ops = REC[0]
        REC[0] = None
        return ops

    def emit_scheduled(prog, lat=0.2, keep_pe_order=True):
        n_ = len(prog)
        lw, rd = {}, {}
        preds = [set() for _ in range(n_)]
        for i, (eng, fn, r, w, d, c) in enumerate(prog):
            for x in r:
                if x in lw:
                    preds[i].add(lw[x])
            for x in w:
                if x in lw:
                    preds[i].add(lw[x])
                preds[i].update(rd.get(x, ()))
            preds[i].discard(i)
            for x in w:
                lw[x] = i
                rd[x] = []
            for x in r:
                rd.setdefault(x, []).append(i)
        if keep_pe_order:
            prev = None
            for i in range(n_):
                if prog[i][0] == "pe":
                    if prev is not None:
                        preds[i].add(prev)
                    prev = i
        succs = [[] for _ in range(n_)]
        indeg = [len(p) for p in preds]
        for i, p in enumerate(preds):
            for j in p:
                succs[j].append(i)
        finish = [0.0] * n_
        efree = {e: 0.0 for e in Sched.ENG}
        ready = [i for i in range(n_) if indeg[i] == 0]
        order = []
        while ready:
            best, bstart = None, None
            for i in ready:
                st_ = efree[prog[i][0]]
                for j in preds[i]:
                    t_ = finish[j] + (0.0 if prog[j][0] == prog[i][0] == "pe" else lat)
                    if t_ > st_:
                        st_ = t_
                if best is None or st_ < bstart - 1e-9 or (abs(st_ - bstart) <= 1e-9 and i < best):
                    best, bstart = i, st_
            ready.remove(best)
            finish[best] = bstart + prog[best][5]
            efree[prog[best][0]] = finish[best]
            order.append(best)
            for k in succs[best]:
                indeg[k] -= 1
                if indeg[k] == 0:
                    ready.append(k)
        assert len(order) == n_
        for i in order:
            eng, fn, r, w, d, c = prog[i]
            S.op(eng, fn, r, w, dma=d)

    def emit_zip(lists):
        for x in lists:
            for it_ in x:
                if it_ is not None:
                    S.op(it_[0], it_[1], it_[2], it_[3], dma=it_[4])

    def fsz(ap):
        return int(np.prod(ap.shape[1:]))

    def mm(out, lhsT, rhs, start, stop, r, w, **kw):
        emit_op("pe", lambda e: e.matmul(out, lhsT=lhsT, rhs=rhs, start=start, stop=stop, **kw), r, w, n=fsz(rhs))

    def tr(out, in_, idn, r, w):
        emit_op("pe", lambda e: e.transpose(out=out, in_=in_, identity=idn), r, w, n=128)

    def act(out, in_, func, r, w, bias=None, scale=None, accum=None):
        kw = {}
        if bias is not None:
            kw["bias"] = bias
        if scale is not None:
            kw["scale"] = scale
        if accum is not None:
            kw["accum_out"] = accum
        emit_op("act", lambda e: e.activation(out=out, in_=in_, func=func, **kw), r, w, n=fsz(out))

    def tt(eng, out, in0, in1, op, r, w):
        emit_op(eng, lambda e: e.tensor_tensor(out=out, in0=in0, in1=in1, op=op), r, w, n=fsz(out),
                kind="pow" if op == ALU.pow else "")

    def ts(eng, out, in0, s1, s2, op0, op1, r, w):
        if s2 is None:
            emit_op(eng, lambda e: e.tensor_scalar(out=out, in0=in0, scalar1=s1, scalar2=None, op0=op0), r, w, n=fsz(out))
        else:
            emit_op(eng, lambda e: e.tensor_scalar(out=out, in0=in0, scalar1=s1, scalar2=s2, op0=op0, op1=op1), r, w, n=fsz(out))

    def stt(out, in0, scalar, in1, op0, op1, r, w):
        emit_op("dve", lambda e: e.scalar_tensor_tensor(out=out, in0=in0, scalar=scalar, in1=in1, op0=op0, op1=op1), r, w, n=fsz(out))

    def cp(eng, out, in_, r, w):
        if eng == "act":
            emit_op("act", lambda e: e.copy(out=out, in_=in_), r, w, n=fsz(out))
        else:
            emit_op(eng, lambda e: e.tensor_copy(out=out, in_=in_), r, w, n=fsz(out))

    def memset(eng, ap, val, w):
        emit_op(eng, lambda e: e.memset(ap, val), (), w)

    def dma(eng, out, in_, r, w, key):
        emit_op(eng, lambda e: e.dma_start(out=out, in_=in_), r, w, dma=key)

    dump_aps = {}

    def dump(name, ap, res):
        if name not in dumps:
            return
        d = nc.dram_tensor("dbg_" + name, list(ap.shape), ap.dtype, kind="ExternalOutput").ap()
        dma("sp", d, ap, res, ["dbg_" + name], "dbg_" + name)
        dump_aps[name] = "dbg_" + name

    def xres(ti, c=None):
        if c is None:
            return ["x%d_%d" % (ti, cc) for cc in range(8)]
        return ["x%d_%d" % (ti, c)]

    WIN_BYTES = 8 * 1536 * 2

    def load_win(l, win):
        wv = win_d[l].rearrange("(k p) n -> p k n", p=128)
        for g in range(2):
            for c in range(4):
                dma("pool", win[:, :, (c * 2 + g) * 64:(c * 2 + g + 1) * 64], wv[:, :, (g * 4 + c) * 64:(g * 4 + c + 1) * 64],
                    [], ["win"], "win")
        dma("pool", win[:, :, 512:768], wv[:, :, 512:768], [], ["win"], "win")
        dma("pool", win[:, :, 768:1024], wv[:, :, 1024:1280], [], ["win"], "win")
        dma("pool", win[:, :, 1024:1280], wv[:, :, 768:1024], [], ["win"], "win")
        dma("pool", win[:, :, 1280:1536], wv[:, :, 1280:1536], [], ["win"], "win")

    win0 = nc.alloc_sbuf_tensor_at(nm("win0"), [128, 8, 1536], BF16, offset=LOC0).ap()
    L = loc_alloc(WIN_BYTES)
    vst = L("vst", [128, 128], F32)
    vst2 = L("vst2", [32, 128], F32)
    wsst = L("wsst", [128, 8, 128], F32)
    xin = [L("xin0", [128, 1024], F32), L("xin1", [128, 1024], F32)]

    memset("dve", ones_bf, 1.0, ["ones_bf"])
    memset("dve", ones_f, 1.0, ["ones_f"])
    memset("dve", mhalf, -0.5, ["mhalf"])
    memset("dve", identf, 0.0, ["identf"])
    S.op("pool", lambda e: e.affine_select(out=identf, in_=identf, pattern=[[-1, 128]], compare_op=ALU.not_equal,
                                           fill=1.0, base=0, channel_multiplier=1), ["identf"], ["identf"])
    cp("dve", ident, identf, ["identf"], ["ident"])
    load_win(0, win0)
    memset("dve", invw[0:64, 0:1], 0.5, ["invw"])
    memset("dve", invw[64:128, 0:1], 0.25, ["invw"])
    memset("dve", invw[0:64, 1:2], 0.125, ["invw"])
    memset("dve", invw[64:128, 1:2], 0.0625, ["invw"])
    memset("dve", poolbd, 0.0, ["poolbd"])
    memset("dve", vst2, 0.0, ["vst2"])

    dma("sp", vst, vecs_d, [], ["vst"], "vst")
    dma("sp", vst2[0:16, :], cvec_d, ["vst2"], ["vst2"], "vst2")
    dma("sp", vst2[16:20, :], pscale_d, ["vst2"], ["vst2"], "vst2")
    dma("sp", cosT, cos_d, [], ["cosT"], "cosT")
    dma("sp", sinT, sin_d, [], ["sinT"], "sinT")
    dma("sp", rcb, rcb_d, [], ["rcb"], "rcb")
    for l in range(DEPTH):
        dma("sp", qg_bc[:, l, :], qn_d[l].partition_broadcast(128), [], ["qg_bc"], "qg")
        dma("sp", kg_bc[:, l, :], kn_d[l].partition_broadcast(128), [], ["kg_bc"], "kg")
        dma("sp", sgn_bc[:, l, :], sgn_d[l].partition_broadcast(128), [], ["sgn_bc"], "sgn")
        for g in range(4):
            h0 = (g % 2) * 64
            dma("sp", bs_bc[h0:h0 + 64, l, g // 2, :], bs_d[l, g].partition_broadcast(64), [], ["bs_bc"], "bs")
            dma("pool", poolbd[h0:h0 + 64, l, g // 2, h0:h0 + 64], pw_d[l, g], ["poolbd"], ["poolbd"], "poolbd")
        dma("sp", wsst[:, l * 4:(l + 1) * 4, :], ws_d[l].rearrange("h p q -> p h q"), [], ["wsst"], "wsst")
    ts("dve", qg_bc, qg_bc, 0.125, None, ALU.mult, None, ["qg_bc"], ["qg_bc"])
    ts("dve", sgn_bc, sgn_bc, 0.5, None, ALU.mult, None, ["sgn_bc"], ["sgn_bc"])

    tr(PS(4)[:, 0:128], vst, identf, ["vst", "identf"], ["ps4"])
    cp("dve", vecT, PS(4)[:, 0:128], ["ps4"], ["vecT"])
    tr(PS(5)[:, 0:32], vst2, identf[0:32, 0:32], ["vst2", "identf"], ["ps5"])
    cp("dve", cT, PS(5)[:, 0:16], ["ps5"], ["cT"])
    cp("dve", pscT, PS(5)[:, 16:20], ["ps5"], ["pscT"])
    sil_t = small[:, 0:16]
    act(sil_t, cT, AF.Tanh, ["cT"], ["small"], scale=0.5)
    stt(sil_t, sil_t, 1.0, cT, ALU.add, ALU.mult, ["small", "cT"], ["small"])
    ts("dve", silT.rearrange("p k s -> p s k"), sil_t.rearrange("p (s k) -> p s k", s=2), 0.5, None, ALU.mult, None,
       ["small"], ["silT"])
    for i in range(8):
        b = 6 + (i % 2)
        tr(PS(b)[:, 0:128], wsst[:, i, :], identf, ["wsst", "identf"], ["ps%d" % b])
        cp("dve" if i % 2 else "act", wsT[:, i // 4, i % 4, :], PS(b)[:, 0:128], ["ps%d" % b], ["wsT"])

    for t128 in range(18):
        ti = min(t128 // 4, 4)
        slot = t128 % 2
        src = x_d[t128 * 128:(t128 + 1) * 128, :] if t128 < 16 else ctx_d[(t128 - 16) * 128:(t128 - 15) * 128, :]
        dma("sp", xin[slot], src, [], ["xin%d" % slot], "xin%d" % slot)
        b0 = slot * 2
        for c in range(8):
            tr(PS(b0 + c // 4)[:, (c % 4) * 128:(c % 4 + 1) * 128], xin[slot][:, c * 128:(c + 1) * 128], identf,
               ["xin%d" % slot, "identf"], ["ps%d" % (b0 + c // 4)])
        for hh in range(2):
            cp("act" if hh else "dve", xT[:, hh * 4:(hh + 1) * 4, t128 * 128:(t128 + 1) * 128],
               PS(b0 + hh).rearrange("p (c t) -> p c t", c=4), ["ps%d" % (b0 + hh)],
               ["x%d_%d" % (ti, c) for c in range(hh * 4, hh * 4 + 4)])

    def mod_chunk(l, ch, wm):
        pm = PS(5)[:, 0:96]
        s = ch % 2
        dma("pool", wm[s], wmod_d[l].rearrange("(k p) n -> p k n", p=128)[:, :, ch * 512:(ch + 1) * 512],
            [], ["wm%d" % s], "wm%d" % s)
        for jj in range(4):
            jc = ch * 4 + jj
            for k in range(8):
                mm(pm[:, jc * 2:jc * 2 + 2], wm[s][:, k, jj * 128:(jj + 1) * 128], silT[:, k, :], k == 0, k == 7,
                   ["wm%d" % s, "silT"], ["ps5"])

    def mod_finish(l):
        pm = PS(5)[:, 0:96]
        modv, a1, a2 = modvs[l], a1s[l], a2s[l]
        tt("dve", modv, pm.rearrange("p (j s) -> p j s", s=2),
           vecT[:, l * 64:l * 64 + 48].unsqueeze(2).broadcast_to([128, 48, 2]), ALU.add, ["ps5", "vecT"], ["modv%d" % l])
        stt(a1, modv[:, 8:16, :], 1.0, vecT[:, l * 64 + 48:l * 64 + 56].unsqueeze(2).broadcast_to([128, 8, 2]),
            ALU.add, ALU.mult, ["modv%d" % l, "vecT"], ["a1_%d" % l])
        stt(a2, modv[:, 32:40, :], 1.0, vecT[:, l * 64 + 56:l * 64 + 64].unsqueeze(2).broadcast_to([128, 8, 2]),
            ALU.add, ALU.mult, ["modv%d" % l, "vecT"], ["a2_%d" % l])

    def mod_phase(l):
        if l > 0:
            return
        wm = [L("wm0", [128, 8, 512], BF16), L("wm1", [128, 8, 512], BF16)]
        for ch in range(12):
            mod_chunk(0, ch, wm)
        mod_finish(0)

    def MOD(j, c, s):
        return modvs[CUR_L[0]][:, j * 8 + c, s:s + 1]

    def MODR():
        return "modv%d" % CUR_L[0]

    def norm_mod(ti, hbuf, hres, aT, bj, bS, bB, R, rs, tmp, tmpres):
        t0, T = TILES[ti]
        s = 1 if ti == 4 else 0
        nsub = T // 128
        act(hbuf[:, :, 0:T], xT[:, :, t0:t0 + T], AF.Square, xres(ti), hres)
        for sub in range(nsub):
            for k in range(8):
                mm(PS(bS)[:, sub:sub + 1], hbuf[:, k, sub * 128:(sub + 1) * 128], ones_bf[:, 0:1], k == 0, k == 7,
                   [hres[k], "ones_bf"], ["ps%d" % bS])
        ts("dve", rs[:, 0:nsub], PS(bS)[:, 0:nsub], 1.0 / 1024, EPS, ALU.mult, ALU.add, ["ps%d" % bS], ["rs_ms"])
        tt("pool", rs[:, 4:4 + nsub], rs[:, 0:nsub], mhalf[:, 0:nsub], ALU.pow, ["rs_ms", "mhalf"], ["rs_r"])
        for sub in range(nsub):
            ts("dve", R[:, sub, :], ones_f, rs[:, 4 + sub:5 + sub], None, ALU.mult, None, ["rs_r", "ones_f"],
               ["R%d" % sub, "gsc"])
            mm(PS(bB)[:, sub * 128:(sub + 1) * 128], R[:, sub, :], identf, True, True, ["R%d" % sub, "gsc", "identf"],
               ["ps%d" % bB])
        for c in range(8):
            j = c % 2
            stt(tmp[j][:, 0:T], xT[:, c, t0:t0 + T], aT[:, c, s:s + 1], PS(bB)[:, 0:T], ALU.mult, ALU.mult,
                xres(ti, c) + ["ps%d" % bB, "a1_%d" % CUR_L[0], "a2_%d" % CUR_L[0]], [tmpres[j]])
            act(hbuf[:, c, 0:T], tmp[j][:, 0:T], AF.Identity, [tmpres[j], MODR()], [hres[c]], bias=MOD(bj, c, s))

    def gelu2(src, dst, g1, r_src, w_dst, g1res):
        act(g1, src, AF.Square, r_src, [g1res])
        ts("dve", g1, g1, 0.044715, 1.0, ALU.mult, ALU.add, [g1res], [g1res])
        tt("dve", g1, g1, src, ALU.mult, [g1res] + r_src, [g1res])
        act(g1, g1, AF.Tanh, [g1res], [g1res], scale=GELU_C)
        stt(dst, g1, 1.0, src, ALU.add, ALU.mult, [g1res] + r_src, w_dst)

    def phase_a(l):
        S.barrier()
        CUR_L[0] = l
        last = l == DEPTH - 1
        La = loc_alloc()
        m67 = {"v": MIX67}

        def Lm(name, shape, dt):
            nbytes = int(np.prod(shape[1:])) * (4 if dt == F32 else 2)
            at = m67["v"]
            end = (at + nbytes + 31) // 32 * 32
            if end <= MIX67 + 2 * NT * 2:
                m67["v"] = end
                return nc.alloc_sbuf_tensor_at(nm(name), list(shape), dt, offset=at).ap()
            return La(name, shape, dt)

        if l == 0:
            win = win0
            La.state["v"] = LOC0 + WIN_BYTES
        else:
            win = La("win", [128, 8, 1536], BF16)
        hq = [La("hq0", [128, 8, 512], BF16), La("hq1", [128, 8, 512], BF16)]
        R = La("R", [128, 4, 128], F32)
        tmp = [La("tmp0", [128, 512], F32), La("tmp1", [128, 512], F32)]
        uT0_ = La("uT0", [128, 2, 512], BF16)
        uTs = [uT0_, uT0_]
        gsc = R.rearrange("p s t -> p (s t)")
        TS = []
        for pq in range(2):
            A = La
            g1_ = A("g1", [128, 256], F32)
            TS.append(dict(qn=A("qn", [128, 10, 64], F32), rt=[A("rt0", [128, 10, 32], F32), A("rt1", [128, 10, 32], F32)],
                           qb=A("qb", [128, 10, 64], BF16), g1=g1_, v2=A("v2", [128, 256], F32),
                           vn=A("vn", [128, 256], BF16), sg=g1_.rearrange("p (c t) -> p c t", c=2)))
        if l > 0:
            load_win(l, win)

        def hres_of(ti):
            return ["hq%d_%d" % (ti % 2, k) for k in range(8)]

        def tm_mm(ti, sub, need_q):
            hbuf, hres = hq[ti % 2], hres_of(ti)
            bq, bk = (4, 5) if sub % 2 == 0 else (2, 3)
            for k in range(8):
                lt = hbuf[:, k, sub * 128:(sub + 1) * 128]
                if need_q:
                    mm(PS(bq), lt, win[:, k, 0:512], k == 0, k == 7, [hres[k], "win"], ["ps%d" % bq])
                mm(PS(bk), lt, win[:, k, 512:1024], k == 0, k == 7, [hres[k], "win"], ["ps%d" % bk])

        def tm_post(ti, sub, full):
            t0, T = TILES[ti]
            is_ctx = ti == 4
            pq = sub % 2
            X = TS[pq]
            qn, rt, qb, g1, v2, vn, sg = X["qn"], X["rt"], X["qb"], X["g1"], X["v2"], X["vn"], X["sg"]
            sqq = qn.rearrange("p h d -> p (h d)")
            uT = uTs[ti % 2]
            P = lambda n: "%s_%d" % (n, pq)
            t128 = t0 // 128 + sub
            tok = slice(t128 * 128, (t128 + 1) * 128)
            bq, bk = (4, 5) if pq == 0 else (2, 3)
            psT6 = PS(6).bitcast(BF16)
            psT7 = PS(7).bitcast(BF16)
            qT = psT6[:, pq * 512:(pq + 1) * 512]
            kT = psT7[:, pq * 128:(pq + 1) * 128]
            psG = PS(bk)[:, 256:512]
            rq, rk = ["ps%d" % bq], ["ps%d" % bk]
            psQ, psK, psV, psGV = PS(bq), PS(bk)[:, 0:128], PS(bk)[:, 128:256], PS(bk)[:, 256:512]
            h0 = 0 if full else 8
            h1 = 11 if full else 10
            sc0 = 16 + pq * 40
            ss, ms, rs = small[:, sc0:sc0 + 11], small[:, sc0 + 11:sc0 + 22], small[:, sc0 + 22:sc0 + 33]
            cp("act", Vt[:, t128, :, :], psV.rearrange("p (g d) -> p g d", g=2), rk, ["V%d" % t128])
            if full:
                gelu2(psGV, v2, g1, rk, [P("v2")], P("g1"))
                act(g1, v2, AF.Square, [P("v2")], [P("g1"), P("ss")], accum=ss[:, 10:11])
                act(sqq[:, 0:512], psQ, AF.Square, rq, [P("qn_q")])
            act(sqq[:, 512:640], psK, AF.Square, rk, [P("qn_k")])
            emit_op("dve", lambda e: e.tensor_reduce(out=ss[:, h0:10], in_=qn[:, h0:10, :], axis=AX.X, op=ALU.add),
                    [P("qn_q"), P("qn_k")], [P("ss")], n=640)
            ts("dve", ms[:, h0:10], ss[:, h0:10], 1.0 / 64, EPS, ALU.mult, ALU.add, [P("ss")], [P("ms")])
            if full:
                ts("dve", ms[:, 10:11], ss[:, 10:11], 0.25 / 256, EPS, ALU.mult, ALU.add, [P("ss")], [P("ms")])
            tt("pool", rs[:, h0:h1], ms[:, h0:h1], mhalf[:, h0:h1], ALU.pow, [P("ms"), "mhalf"], [P("rs")])
            if full:
                stt(vn, v2, rs[:, 10:11], sgn_bc[:, l, :], ALU.mult, ALU.mult, [P("v2"), P("rs"), "sgn_bc"], [P("vn")])
                for h in range(4):
                    o0 = (h % 2) * 64
                    mm(psG[o0:o0 + 64, (h // 2) * 128:(h // 2 + 1) * 128], vn[:, h * 64:(h + 1) * 64], wsT[:, l, h, :],
                       True, True, [P("vn"), "wsT"], ["ps%d" % bk], tile_position=(0, o0))
                tt("dve", sg, psG.rearrange("p (c t) -> p c t", c=2), bs_bc[:, l, :, :], ALU.add,
                   ["ps%d" % bk, "bs_bc"], [P("g1")])
                stt(mixT[:, 4:6, tok], sg, 0.5, uT[:, :, sub * 128:(sub + 1) * 128], ALU.mult, ALU.mult,
                    [P("g1"), "uT0_0", "uT0_1"], ["mix4_%d" % ti, "mix5_%d" % ti])
                tt("dve", qn[:, 0:8, :], psQ.rearrange("p (h d) -> p h d", d=64),
                   rs[:, 0:8].unsqueeze(2).broadcast_to([128, 8, 64]), ALU.mult, rq + [P("rs")], [P("qn_q")])
                tt("pool", qn[:, 0:8, :], qn[:, 0:8, :], qg_bc[:, l, :].unsqueeze(1).broadcast_to([128, 8, 64]), ALU.mult,
                   [P("qn_q"), "qg_bc"], [P("qn_q")])
            tt("dve", qn[:, 8:10, :], psK.rearrange("p (h d) -> p h d", d=64),
               rs[:, 8:10].unsqueeze(2).broadcast_to([128, 2, 64]), ALU.mult, rk + [P("rs")], [P("qn_k")])
            tt("pool", qn[:, 8:10, :], qn[:, 8:10, :], kg_bc[:, l, :].unsqueeze(1).broadcast_to([128, 2, 64]), ALU.mult,
               [P("qn_k"), "kg_bc"], [P("qn_k")])
            nh = 10 - h0
            if not is_ctx:
                cs = cosT[:, t128, :].unsqueeze(1).broadcast_to([128, nh, 32])
                sn = sinT[:, t128, :].unsqueeze(1).broadcast_to([128, nh, 32])
                x1, x2 = qn[:, h0:10, 0:32], qn[:, h0:10, 32:64]
                rr = [P("qn_q"), P("qn_k"), "cosT", "sinT"]
                tt("pool", rt[0][:, h0:10, :], x1, cs, ALU.mult, rr, [P("rt0")])
                tt("pool", rt[1][:, h0:10, :], x2, sn, ALU.mult, rr, [P("rt1")])
                tt("dve", qb[:, h0:10, 0:32], rt[0][:, h0:10, :], rt[1][:, h0:10, :], ALU.subtract, [P("rt0"), P("rt1")], [P("qb")])
                tt("pool", rt[0][:, h0:10, :], x2, cs, ALU.mult, rr, [P("rt0")])
                tt("pool", rt[1][:, h0:10, :], x1, sn, ALU.mult, rr, [P("rt1")])
                tt("dve", qb[:, h0:10, 32:64], rt[0][:, h0:10, :], rt[1][:, h0:10, :], ALU.add, [P("rt0"), P("rt1")], [P("qb")])
            else:
                cp("dve", qb[:, h0:10, :], qn[:, h0:10, :], [P("qn_q"), P("qn_k")], [P("qb")])
            qbf = qb.rearrange("p h d -> p (h d)")
            if full:
                for c in range(4):
                    tr(qT[:, c * 128:(c + 1) * 128], qbf[:, c * 128:(c + 1) * 128], ident, [P("qb"), "ident"], ["ps6"])
            tr(kT, qbf[:, 512:640], ident, [P("qb"), "ident"], ["ps7"])
            if full:
                cp("act", mixT[:, 0:4, tok], qT.rearrange("p (c t) -> p c t", c=4), ["ps6"],
                   ["mix%d_%d" % (c, ti) for c in range(4)])
            cp("act", KA[:, tok], kT, ["ps7"], ["K%d" % t128])

        def norm_fm(ti):
            t0, T = TILES[ti]
            full = not (last and ti == 4)
            hbuf, hres = hq[ti % 2], hres_of(ti)
            uT = uTs[ti % 2]
            norm_mod(ti, hbuf, hres, a1s[l], 0, 0, 1, R, small[:, 0:8], tmp, ["tmp0", "tmp1"])
            if full:
                for fc in range(4):
                    b = 2 + fc % 2
                    for k in range(8):
                        mm(PS(b)[:, 0:T], win[:, k, 1024 + fc * 128:1024 + (fc + 1) * 128], hbuf[:, k, 0:T], k == 0, k == 7,
                           [hres[k], "win"], ["ps%d" % b])
                    if fc < 2:
                        gelu2(PS(b)[:, 0:T], uT[:, fc, 0:T], gsc[:, 0:T], ["ps%d" % b], ["uT0_%d" % fc], "gsc")
                    else:
                        cp("act", pT[:, fc - 2, t0:t0 + T], PS(b)[:, 0:T], ["ps%d" % b], ["pT%d_%d" % (fc - 2, ti)])

        def fullf(ti):
            return not (last and ti == 4)

        def zipped(lists, burst=None, at=0, pre=None):
            out = list(pre) if pre else []
            n_ = max(len(x) for x in lists)
            for i in range(max(n_, at + 1)):
                for x in lists:
                    if i < len(x) and x[i] is not None:
                        out.append(x[i])
                if burst is not None and i == at:
                    out.extend(burst)
            return out

        prog = record(norm_fm, 0) + record(tm_mm, 0, 0, fullf(0)) + record(tm_mm, 0, 1, fullf(0))
        for ti in range(5):
            nsub = TILES[ti][1] // 128
            full = fullf(ti)
            lists = [record(tm_post, ti, 0, full), record(tm_post, ti, 1, full)]
            burst = pre = nf_rest = None
            if nsub == 4:
                burst = record(tm_mm, ti, 2, full) + record(tm_mm, ti, 3, full)
                if ti + 1 < 5:
                    nf = record(norm_fm, ti + 1)
                    ns1 = TILES[ti + 1][1] // 128
                    n_norm = 1 + 10 * ns1 + 18
                    pre, nf_rest = nf[:n_norm], nf[n_norm:]
            prog += zipped(lists, burst, TM_SKEW, pre)
            if nsub == 4:
                lists = [record(tm_post, ti, 2, full), record(tm_post, ti, 3, full)]
                burst = None
                if ti + 1 < 5:
                    burst = (nf_rest + record(tm_mm, ti + 1, 0, fullf(ti + 1))
                             + record(tm_mm, ti + 1, 1, fullf(ti + 1)))
                prog += zipped(lists, burst, TM_SKEW)
        if ZIP:
            emit_scheduled(prog)
        else:
            emit_zip([prog])

    def pool_phase(l):
        return

    def pool_prep(l, A):
        last = l == DEPTH - 1
        P0 = A("P0", [128, 2064], F32)
        s2 = A("s2", [128, 2064], F32)
        s4 = A("s4", [128, 2064], F32)
        s8 = A("s8", [128, 2064], F32)
        s16 = s2
        fx = A("fx", [128, 16], F32)
        streams = [(0, 2048, [0, 1, 2, 3])] + ([] if last else [(2048, 256, [4])])
        for (t0, N, tis) in streams:
            for ch in range(2):
                pres = ["pT%d_%d" % (ch, ti) for ti in tis]
                memset("dve", P0[:, 0:8], 0.0, ["P0"])
                memset("dve", P0[:, 8 + N:16 + N], 0.0, ["P0"])
                cp("dve", P0[:, 8:8 + N], pT[:, ch, t0:t0 + N], pres, ["P0"])
                tt("dve", s2[:, 1:N + 16], P0[:, 0:N + 15], P0[:, 1:N + 16], ALU.add, ["P0"], ["s2"])
                tt("dve", s4[:, 2:N + 15], s2[:, 1:N + 14], s2[:, 3:N + 16], ALU.add, ["s2"], ["s4"])
                if ch == 0:
                    srcs = [(0, 64, s2, 8), (64, 128, s4, 8)]
                else:
                    tt("dve", s8[:, 4:N + 13], s4[:, 2:N + 11], s4[:, 6:N + 15], ALU.add, ["s4"], ["s8"])
                    tt("dve", s16[64:128, 0:N], s8[64:128, 4:N + 4], s8[64:128, 12:N + 12], ALU.add, ["s8"], ["s2"])
                    srcs = [(0, 64, s8, 8), (64, 128, s16, 0)]
                for (p0, p1, sw, o_) in srcs:
                    stt(pT[p0:p1, ch, t0:t0 + N], sw[p0:p1, o_:o_ + N], invw[p0:p1, ch:ch + 1], P0[p0:p1, 8:8 + N],
                        ALU.mult, ALU.subtract, ["s2", "s4", "s8", "P0", "invw"], pres)
                    for side, c0 in ((0, 0), (1, N - 8)):
                        tt("dve", fx[p0:p1, side * 8:side * 8 + 8], sw[p0:p1, o_ + c0:o_ + c0 + 8], rcb[p0:p1, ch, side, :],
                           ALU.mult, ["s2", "s4", "s8", "rcb"], ["fx"])
                        tt("dve", pT[p0:p1, ch, t0 + c0:t0 + c0 + 8], fx[p0:p1, side * 8:side * 8 + 8],
                           P0[p0:p1, 8 + c0:16 + c0], ALU.subtract, ["fx", "P0"] + pres, pres)

    def pool_finish(l):
        last = l == DEPTH - 1
        i = 0
        for ti in range(4 if last else 5):
            t0, T = TILES[ti]
            for ch in range(2):
                b = 6 + i % 2
                i += 1
                mm(PS(b)[:, 0:T], poolbd[:, l, ch, :], pT[:, ch, t0:t0 + T], True, True, ["pT%d_%d" % (ch, ti), "poolbd"],
                   ["ps%d" % b])
                act(mixT[:, 6 + ch, t0:t0 + T], PS(b)[:, 0:T], AF.Copy, ["ps%d" % b, "pscT"], ["mix%d_%d" % (6 + ch, ti)],
                    scale=pscT[:, l * 2 + ch:l * 2 + ch + 1])

    def attention(l):
        S.barrier()
        CUR_L[0] = l
        last = l == DEPTH - 1
        Lb = loc_alloc()
        PT = [Lb("PT%d" % i, [128, 1024], BF16) for i in range(3)]
        rec = [Lb("rec%d" % i, [128, 512], F32) for i in range(2)]
        wo = Lb("wo", [128, 8, 1024], BF16)
        WO[l] = (wo, Lb.state["v"])
        for e_ in range(2):
            dma("pool", wo[e_ * 64:(e_ + 1) * 64, 0:4, :],
                wout_d[l][e_ * 256:(e_ + 1) * 256, :].rearrange("(c d) n -> d c n", d=64), [], ["wo"], "wo")
        dma("pool", wo[:, 4:8, :], wout_d[l][512:1024, :].rearrange("(k p) n -> p k n", p=128), [], ["wo"], "wo")
        blocks = [(c, ti, list(range(18))) for c in range(4) for ti in range(4)]
        if not last:
            blocks += [(c, 4, [16, 17]) for c in range(4)]
        prep = record(pool_prep, l, Lb)
        prep_pos = [0]
        per_block = -(-len(prep) // 12)

        def drip(n):
            for eng, fn, r_, w_, d_, c_ in prep[prep_pos[0]:prep_pos[0] + n]:
                S.op(eng, fn, r_, w_, dma=d_)
            prep_pos[0] += n

        units = []
        for bi, (c, ti, kts) in enumerate(blocks):
            for ki, kt in enumerate(kts):
                units.append((bi, c, ti, kt, ki, len(kts)))

        def emit_s(j):
            bi, c, ti, kt, ki, nk = units[j]
            t0, T = TILES[ti]
            g = c // 2
            sb0 = (j % 2) * 2
            pt = PT[j % 3]
            qres = ["mix%d_%d" % (c, ti)]
            for e_ in range(2):
                mm(PS(sb0 + e_)[:, 0:T], KA[e_ * 64:(e_ + 1) * 64, kt * 128:(kt + 1) * 128],
                   mixT[e_ * 64:(e_ + 1) * 64, c, t0:t0 + T], True, True, ["K%d" % kt] + qres, ["ps%d" % (sb0 + e_)])
            S.op("act", (lambda pt=pt, sb0=sb0, T=T: lambda e: e.activation(
                out=pt.rearrange("p (e t) -> p e t", e=2)[:, :, 0:T],
                in_=PS(sb0, 2).rearrange("p (e t) -> p e t", e=2)[:, :, 0:T], func=AF.Exp))(),
                ["ps%d" % sb0, "ps%d" % (sb0 + 1)], ["PT%d" % (j % 3)])

        def emit_pv(j):
            bi, c, ti, kt, ki, nk = units[j]
            t0, T = TILES[ti]
            g = c // 2
            bo = 4 + bi % 2
            bd = 6 + bi % 2
            pt = PT[j % 3]
            ptr = "PT%d" % (j % 3)
            for e_ in range(2):
                o0 = e_ * 64
                mm(PS(bo)[o0:o0 + 64, 0:T], Vt[:, kt, e_, :], pt[:, e_ * 512:e_ * 512 + T], ki == 0, ki == nk - 1,
                   ["V%d" % kt, ptr], ["ps%d_%d" % (bo, e_)], tile_position=(0, o0))
            for e_ in range(2):
                o0 = e_ * 64
                mm(PS(bd)[o0:o0 + 64, 0:T], ones_bf[:, 0:64], pt[:, e_ * 512:e_ * 512 + T], ki == 0, ki == nk - 1,
                   ["ones_bf", ptr], ["ps%d_%d" % (bd, e_)], tile_position=(0, o0))
            if ki == nk - 1:
                rc_ = rec[bi % 2]
                S.op("dve", (lambda rc_=rc_, bd=bd, T=T: lambda e: e.reciprocal(out=rc_[:, 0:T], in_=PS(bd)[:, 0:T]))(),
                     ["ps%d_0" % bd, "ps%d_1" % bd], ["rec%d" % (bi % 2)])
                tt("dve", mixT[:, c, t0:t0 + T], PS(bo)[:, 0:T], rc_[:, 0:T], ALU.mult,
                   ["ps%d_0" % bo, "ps%d_1" % bo, "rec%d" % (bi % 2)], ["mix%d_%d" % (c, ti)])
                drip(per_block)

        for j in range(len(units)):
            emit_s(j)
            if j >= 1:
                emit_pv(j - 1)
        emit_pv(len(units) - 1)
        drip(len(prep))

    FCTX = {}
    KA0 = MIX0 + 8 * NT * 2

    def ffn_setup(l):
        def at(name, shape, dt, pos):
            nbytes = int(np.prod(shape[1:])) * (4 if dt == F32 else 2)
            end = (pos + nbytes + 31) // 32 * 32
            return nc.alloc_sbuf_tensor_at(nm(name), list(shape), dt, offset=pos).ap(), end
        p = MIX0
        th = []
        for i in range(2):
            t_, p = at("th%d" % i, [128, 512], F32, p)
            th.append(t_)
        wgs, wus, wds = [], [], []
        for i in range(3):
            t_, p = at("wg%d" % i, [128, 8, 128], BF16, p)
            wgs.append(t_)
        for i in range(3):
            t_, p = at("wu%d" % i, [128, 8, 128], BF16, p)
            wus.append(t_)
        for i in range(2):
            t_, p = at("wd%d" % i, [128, 22, 128], BF16, p)
            wds.append(t_)
        assert p <= KA0, (p, KA0)
        p = KA0
        h2, p = at("h2", [128, 8, 1280], BF16, p)
        R, p = at("R", [128, 4, 128], F32, p)
        tmp = []
        for i in range(2):
            t_, p = at("tmp%d" % i, [128, 512], F32, p)
            tmp.append(t_)
        assert p <= LOC0 + 10240, (p, LOC0 + 10240)
        aT, p = at("aT", [128, 22, 1280], BF16, p)
        assert p <= LOC_END, (p, LOC_END)
        FCTX[l] = (h2, aT, R, tmp, th, wgs, wus, wds)

    def ffn_goffs(tis):
        offs, o = {}, 0
        for ti in tis:
            offs[ti] = o
            o += TILES[ti][1]
        return offs

    def ffn_h2res(offs, ti):
        return ["h2_%d_%d" % (offs[ti], k) for k in range(8)]

    def ffn_norm(l, offs, ti, bS, bB):
        h2, aT, R, tmp = FCTX[l][0:4]
        T = TILES[ti][1]
        hb = h2[:, :, offs[ti]:offs[ti] + T]
        norm_mod(ti, hb, ffn_h2res(offs, ti), a2s[l], 3, bS, bB, R, small[:, 0:8], tmp, ["tmp0", "tmp1"])

    def split_norm(ops, ti):
        nsub = TILES[ti][1] // 128
        n1 = 1 + 8 * nsub + 2
        rb = ops[n1:n1 + 2 * nsub]
        return (ops[:1], ops[1:n1] + rb[0::2], rb[1::2] + ops[n1 + 2 * nsub:])

    def emit_list(ops):
        for eng, fn, r, w, d, c in ops:
            S.op(eng, fn, r, w, dma=d)

    def phase_c1(l):
        S.barrier()
        CUR_L[0] = l
        last = l == DEPTH - 1
        Lc = loc_alloc()
        wo, wo_end = WO[l]
        Lc.state["v"] = wo_end
        pool_finish(l)
        ffn_setup(l)
        g0 = [0, 1]
        offs0 = ffn_goffs(g0)
        pcs = [split_norm(record(ffn_norm, l, offs0, ti, 4, 6), ti) for ti in g0]
        ptres = ["pT%d_%d" % (c, ti) for c in range(2) for ti in range(5)]
        nxt = l + 1 if l + 1 < DEPTH else None
        if nxt is not None:
            wmn = [Lc("wm0", [128, 8, 512], BF16), Lc("wm1", [128, 8, 512], BF16)]
        i = 0
        for ti in range(4 if last else 5):
            t0, T = TILES[ti]
            s = 1 if ti == 4 else 0
            for dc in range(8):
                if nxt is not None and i % 3 == 0 and i // 3 < 12:
                    mod_chunk(nxt, i // 3, wmn)
                if i == 16:
                    for e_ in ("act", "dve"):
                        S.op(e_, None, (), ptres)
                    emit_list(pcs[0][0])
                    emit_list(pcs[1][0])
                elif i == 18:
                    emit_list(pcs[0][1])
                elif i == 21:
                    emit_list(pcs[0][2])
                    emit_list(pcs[1][1])
                elif i == 24:
                    emit_list(pcs[1][2])
                b = i % 4
                i += 1
                for k in range(8):
                    mm(PS(b)[:, 0:T], wo[:, k, dc * 128:(dc + 1) * 128], mixT[:, k, t0:t0 + T], k == 0, k == 7,
                       ["wo", "mix%d_%d" % (k, ti)], ["ps%d" % b])
                stt(xT[:, dc, t0:t0 + T], PS(b)[:, 0:T], MOD(2, dc, s), xT[:, dc, t0:t0 + T], ALU.mult, ALU.add,
                    ["ps%d" % b, MODR()] + xres(ti, dc), xres(ti, dc))
        if nxt is not None:
            mod_finish(nxt)

    def ffn_phase(l):
        S.barrier()
        CUR_L[0] = l
        last = l == DEPTH - 1
        groups = [[0, 1], [2, 3] if last else [2, 3, 4]]
        h2, aT, R, tmp, th, wgs, wus, wds = FCTX[l]
        allmix = ["mix%d_%d" % (c, ti) for c in range(8) for ti in range(5)]
        fence = allmix + ["K%d" % t for t in range(18)] + ["V%d" % t for t in range(18)] + \
            ["pT%d_%d" % (c, ti) for c in range(2) for ti in range(5)] + ["wo", "PT0", "PT1", "PT2", "rec0", "rec1"]
        fi = [0]
        wgv = wg_d[l].rearrange("(k p) n -> p k n", p=128)
        wuv = wu_d[l].rearrange("(k p) n -> p k n", p=128)
        wdv = wd_d[l].rearrange("(f p) n -> p f n", p=128)

        def load_gu(f):
            s_ = f % 3
            dma("pool", wgs[s_], wgv[:, :, f * 128:(f + 1) * 128], [], ["wg%d" % s_], "wg%d" % s_)
            dma("pool", wus[s_], wuv[:, :, f * 128:(f + 1) * 128], [], ["wu%d" % s_], "wu%d" % s_)

        def load_d(dc):
            s_ = dc % 2
            dma("pool", wds[s_], wdv[:, :, dc * 128:(dc + 1) * 128], [], ["wd%d" % s_], "wd%d" % s_)

        goffs, h2res = ffn_goffs, ffn_h2res

        def do_norm(offs, ti):
            ffn_norm(l, offs, ti, 0, 1)

        for gi, tis in enumerate(groups):
            offs = goffs(tis)
            load_gu(0)
            load_gu(1)
            if gi == 0:
                offs1 = goffs(groups[1])
                pieces = []
                for ti in groups[1]:
                    pieces.append(split_norm(record(do_norm, offs1, ti), ti))
            for f in range(22):
                if f + 2 < 22:
                    load_gu(f + 2)
                elif f == 20:
                    load_d(0)
                elif f == 21:
                    load_d(1)
                s_ = f % 3
                for ti in tis:
                    T = TILES[ti][1]
                    o = offs[ti]
                    j = fi[0] % 2
                    fi[0] += 1
                    bg, bu = 2 + j, 4 + j
                    hres = h2res(offs, ti)
                    for k in range(8):
                        mm(PS(bg)[:, 0:T], wgs[s_][:, k, :], h2[:, k, o:o + T], k == 0, k == 7, ["wg%d" % s_, hres[k]],
                           ["ps%d" % bg])
                    for k in range(8):
                        mm(PS(bu)[:, 0:T], wus[s_][:, k, :], h2[:, k, o:o + T], k == 0, k == 7, ["wu%d" % s_, hres[k]],
                           ["ps%d" % bu])
                    act(th[j][:, 0:T], PS(bg)[:, 0:T], AF.Tanh, ["ps%d" % bg], ["th%d" % j], scale=0.5)
                    stt(th[j][:, 0:T], th[j][:, 0:T], 1.0, PS(bg)[:, 0:T], ALU.add, ALU.mult, ["th%d" % j, "ps%d" % bg],
                        ["th%d" % j])
                    stt(aT[:, f, o:o + T], th[j][:, 0:T], 0.5, PS(bu)[:, 0:T], ALU.mult, ALU.mult, ["th%d" % j, "ps%d" % bu],
                        ["aT%d_%d" % (f, ti)])
            yi = 0
            if gi == 0:
                for pc in pieces:
                    emit_list(pc[0])
            for dc in range(8):
                if gi == 0:
                    if 2 <= dc <= len(pieces) + 1:
                        emit_list(pieces[dc - 2][2])
                    if 1 <= dc <= len(pieces):
                        emit_list(pieces[dc - 1][1])
                s_ = dc % 2
                for ti in tis:
                    t0, T = TILES[ti]
                    o = offs[ti]
                    sidx = 1 if ti == 4 else 0
                    b = 6 + yi % 2
                    yi += 1
                    for f in range(22):
                        mm(PS(b)[:, 0:T], wds[s_][:, f, :], aT[:, f, o:o + T], f == 0, f == 21,
                           ["wd%d" % s_, "aT%d_%d" % (f, ti)], ["ps%d" % b])
                    stt(xT[:, dc, t0:t0 + T], PS(b)[:, 0:T], MOD(5, dc, sidx), xT[:, dc, t0:t0 + T], ALU.mult, ALU.add,
                        ["ps%d" % b, MODR()] + xres(ti, dc), xres(ti, dc))
                if dc + 2 < 8:
                    load_d(dc + 2)

    def final_phase():
        S.barrier()
        st = {"v": MIX0}

        def Lz(name, shape, dt):
            nbytes = int(np.prod(shape[1:])) * (4 if dt == F32 else 2)
            at = st["v"]
            st["v"] = (at + nbytes + 31) // 32 * 32
            assert st["v"] <= LOC_END
            return nc.alloc_sbuf_tensor_at(nm(name), list(shape), dt, offset=at).ap()

        fn_bc = Lz("fn_bc", [128, 1024], F32)
        ob = [Lz("ob0", [128, 1024], F32), Lz("ob1", [128, 1024], F32)]
        junk = Lz("junk", [128, 1024], F32)
        fence = ["h2_%d_%d" % (ti, k) for ti in range(5) for k in range(8)] + \
            ["aT%d_%d" % (f, ti) for f in range(22) for ti in range(5)] + ["R0", "R1", "R2", "R3", "tmp0", "tmp1"]
        dma("sp", fn_bc, fn_d.partition_broadcast(128), [], ["fn_bc"], "fn_bc")
        for t128 in range(16):
            ti = t128 // 4
            slot = t128 % 2
            b0 = slot * 2
            for c in range(8):
                tr(PS(b0 + c // 4)[:, (c % 4) * 128:(c % 4 + 1) * 128], xT[:, c, t128 * 128:(t128 + 1) * 128], identf,
                   xres(ti, c) + ["identf"], ["ps%d" % (b0 + c // 4)])
            pr = ["ps%d" % b0, "ps%d" % (b0 + 1)]
            ss, ms, rs_ = small[:, 100 + slot * 3:101 + slot * 3], small[:, 101 + slot * 3:102 + slot * 3], small[:, 102 + slot * 3:103 + slot * 3]
            act(junk, PS(b0, 2), AF.Square, pr, ["junk", "fss%d" % slot], accum=ss)
            ts("dve", ms, ss, 1.0 / 1024, EPS, ALU.mult, ALU.add, ["fss%d" % slot], ["fms%d" % slot])
            tt("pool", rs_, ms, mhalf[:, 0:1], ALU.pow, ["fms%d" % slot, "mhalf"], ["frs%d" % slot])
            stt(ob[slot], PS(b0, 2), rs_, fn_bc, ALU.mult, ALU.mult, pr + ["frs%d" % slot, "fn_bc"], ["ob%d" % slot])
            dma("sp", out_d[t128 * 128:(t128 + 1) * 128, :], ob[slot], ["ob%d" % slot], ["out%d" % t128], "out%d" % slot)
        S.op("sp", None, ["out%d" % t for t in range(16)] + list(dump_aps.values()), ())

    stop = ""
    phases = []
    for l in range(DEPTH):
        phases += [("mod%d" % l, mod_phase, l), ("a%d" % l, phase_a, l), ("pool%d" % l, pool_phase, l),
                   ("attn%d" % l, attention, l), ("c1%d" % l, phase_c1, l), ("ffn%d" % l, ffn_phase, l)]
    for name, fn, l in phases:
        fn(l)
        if name == stop:
            break
    final_phase()
    S.build()
    return nc, list(dump_aps.values())


def _host_consts():
    t = np.arange(2048)
    rows = (t // 64).astype(np.float64)
    cols = (t % 64).astype(np.float64)
    inv = 10000.0 ** (-np.arange(16, dtype=np.float64) / 16)
    ang = np.concatenate([rows[:, None] * inv, cols[:, None] * inv], axis=-1)
    cosT = np.cos(ang).astype(np.float32).reshape(16, 128, 32).transpose(1, 0, 2).copy()
    sinT = np.sin(ang).astype(np.float32).reshape(16, 128, 32).transpose(1, 0, 2).copy()
    rcb = np.zeros((128, 2, 2, 8), np.float32)
    for ch in range(2):
        for half in range(2):
            w = 2 ** (2 * ch + half + 1)
            for i in range(8):
                cl = (i + w // 2) - max(i - w // 2, 0)
                tr_ = -8 + i
                cr = min(tr_ + w // 2, 0) - (tr_ - w // 2)
                rcb[half * 64:(half + 1) * 64, ch, 0, i] = 1.0 / cl
                rcb[half * 64:(half + 1) * 64, ch, 1, i] = 1.0 / cr
    return cosT, sinT, rcb


_CACHE = {}


def kernel(x, c, ctx, c_ctx, w_mod, b_mod, norm1, norm2, w_in, q_norm, k_norm, sgu_norm, w_s, b_s, pool_w,
           pool_scale, w_out, w_gate, w_up, w_down, final_norm, _dumps=()):
    f = lambda a: np.ascontiguousarray(np.asarray(a, dtype=np.float32))
    x, c, ctx, c_ctx = f(x), f(c), f(ctx), f(c_ctx)
    key = tuple(_dumps)
    if key not in _CACHE:
        _CACHE[key] = build_program(_dumps)
    nc, dnames = _CACHE[key]
    cosT, sinT, rcb = _host_consts()
    vecs = np.concatenate([np.concatenate([f(b_mod)[l].reshape(48, 128), f(norm1)[l].reshape(8, 128),
                                           f(norm2)[l].reshape(8, 128)], 0) for l in range(DEPTH)], 0)
    shared = {
        "w_mod": f(w_mod), "vecs": np.ascontiguousarray(vecs), "pool_scale": f(pool_scale).reshape(4, 128),
        "w_in": f(w_in), "q_norm": f(q_norm), "k_norm": f(k_norm), "sgu_norm": f(sgu_norm), "w_s": f(w_s),
        "b_s": f(b_s), "pool_w": f(pool_w), "w_out": f(w_out), "w_gate": f(w_gate), "w_up": f(w_up),
        "w_down": f(w_down), "final_norm": f(final_norm), "cosT": cosT, "sinT": sinT, "rcb": rcb,
    }
    in_maps = []
    for b in range(8):
        m = dict(shared)
        m["x"] = x[b]
        m["ctx"] = ctx[b]
        m["cvec"] = np.ascontiguousarray(np.concatenate([c[b].reshape(8, 128), c_ctx.reshape(8, 128)], 0))
        in_maps.append(m)
    res = run_bass_kernel_spmd(nc, in_maps, core_ids=list(range(8)))
    out = np.stack([np.asarray(r["out"], dtype=np.float32) for r in res.results], 0)
    if _dumps:
        return out, [{d: np.asarray(r[d]) for d in dnames} for r in res.results]
    return out
```

```python
import contextlib
import numpy as np
import concourse.bass as bass
import concourse.mybir as mybir
from concourse.bass_utils import run_bass_kernel_spmd

F32 = mybir.dt.float32
BF16 = mybir.dt.bfloat16
ALU = mybir.AluOpType
AF = mybir.ActivationFunctionType
AX = mybir.AxisListType

EPS = 1e-6
DEPTH = 2
NT = 2304
TILES = [(0, 512), (512, 512), (1024, 512), (1536, 512), (2048, 256)]
GELU_C = 0.7978845608028654
TM_SKEW = 24
ZIP = 1


class Sched:
    ENG = ("pe", "act", "dve", "pool", "sp")

    def __init__(self, nc, same_engine_sync=True):
        self.nc = nc
        self.ops = {e: [] for e in self.ENG}
        self.ncomp = {e: 0 for e in self.ENG}
        self.waited = {e: {} for e in self.ENG}
        self.last_w = {}
        self.readers = {}
        self.dma_cnt = {}
        self.same = same_engine_sync
        self.semkeys = []

    def _semkey(self, k):
        if k not in self.semkeys:
            self.semkeys.append(k)
        return k

    def op(self, eng, emit, reads=(), writes=(), dma=None):
        deps = []
        for r in reads:
            t = self.last_w.get(r)
            if t is not None:
                deps.append((t, "raw"))
        for w in writes:
            t = self.last_w.get(w)
            if t is not None:
                deps.append((t, "waw"))
            for t in self.readers.get(w, ()):
                deps.append((t, "war"))
        need = {}
        for (key, val, teng), kind in deps:
            if teng == eng:
                if eng == "pe" or not self.same or kind == "war":
                    continue
            if self.waited[eng].get(key, 0) >= val:
                continue
            if need.get(key, 0) < val:
                need[key] = val
        for key, val in need.items():
            self.waited[eng][key] = val
        if dma is not None:
            key = self._semkey("D:" + dma)
            self.dma_cnt[key] = self.dma_cnt.get(key, 0) + 1
            tok = (key, 16 * self.dma_cnt[key], None)
        elif emit is not None:
            key = self._semkey("E:" + eng)
            self.ncomp[eng] += 1
            tok = (key, self.ncomp[eng], eng)
        else:
            tok = None
        self.ops[eng].append((emit, sorted(need.items()), tok))
        if tok is not None:
            for w in writes:
                self.last_w[w] = tok
                self.readers[w] = []
            for r in reads:
                self.readers.setdefault(r, []).append(tok)
        return tok

    def barrier(self):
        state = {}
        for e in self.ENG:
            if self.ncomp[e]:
                state["E:" + e] = self.ncomp[e]
        for k, c in self.dma_cnt.items():
            state[k] = 16 * c
        for eng in self.ENG:
            need = {}
            for key, val in state.items():
                if self.waited[eng].get(key, 0) >= val:
                    continue
                need[key] = val
                self.waited[eng][key] = val
            if need:
                self.ops[eng].append((None, sorted(need.items()), None))

    def build(self):
        nc = self.nc
        with contextlib.ExitStack() as st:
            sems = {}
            for i, k in enumerate(self.semkeys):
                sems[k] = st.enter_context(nc.semaphore("s%d" % i))
            block = st.enter_context(nc.Block())

            def run(name):
                def f(e):
                    for emit, waits, tok in self.ops[name]:
                        for key, val in waits:
                            e.wait_ge(sems[key], val)
                        if emit is None:
                            continue
                        ins = emit(e)
                        if tok is not None:
                            ins.then_inc(sems[tok[0]], 16 if tok[2] is None else 1)
                return f

            block.tensor(run("pe"))
            block.scalar(run("act"))
            block.vector(run("dve"))
            block.gpsimd(run("pool"))
            block.sync(run("sp"))


def build_program(dumps=()):
    nc = bass.Bass("TRN2", target_bir_lowering=False)
    S = Sched(nc)

    def din(name, shape):
        return nc.dram_tensor(name, list(shape), F32, kind="ExternalInput").ap()

    x_d = din("x", [2048, 1024])
    ctx_d = din("ctx", [256, 1024])
    cvec_d = din("cvec", [16, 128])
    wmod_d = din("w_mod", [DEPTH, 1024, 6144])
    vecs_d = din("vecs", [128, 128])
    pscale_d = din("pool_scale", [4, 128])
    win_d = din("w_in", [DEPTH, 1024, 1536])
    qn_d = din("q_norm", [DEPTH, 64])
    kn_d = din("k_norm", [DEPTH, 64])
    sgn_d = din("sgu_norm", [DEPTH, 256])
    ws_d = din("w_s", [DEPTH, 4, 128, 128])
    bs_d = din("b_s", [DEPTH, 4, 128])
    pw_d = din("pool_w", [DEPTH, 4, 64, 64])
    wout_d = din("w_out", [DEPTH, 1024, 1024])
    wg_d = din("w_gate", [DEPTH, 1024, 2816])
    wu_d = din("w_up", [DEPTH, 1024, 2816])
    wd_d = din("w_down", [DEPTH, 2816, 1024])
    fn_d = din("final_norm", [1024])
    cos_d = din("cosT", [128, 16, 32])
    sin_d = din("sinT", [128, 16, 32])
    rcb_d = din("rcb", [128, 2, 2, 8])
    out_d = nc.dram_tensor("out", [2048, 1024], F32, kind="ExternalOutput").ap()

    KB_ = 1024
    uid = [0]

    def nm(s_):
        uid[0] += 1
        return "%s_%d" % (s_, uid[0])

    SB_BASE = 16512
    SB_END = 229376
    off = {"v": SB_BASE}

    def sb(name, shape, dt, at=None):
        nbytes = int(np.prod(shape[1:])) * (4 if dt == F32 else 2)
        if at is None:
            at = off["v"]
            off["v"] = (at + nbytes + 31) // 32 * 32
        return nc.alloc_sbuf_tensor_at(name, list(shape), dt, offset=at).ap(), at + nbytes

    xT, _ = sb("xT", [128, 8, NT], F32)
    MIX0 = off["v"]
    MIX67 = MIX0 + 6 * NT * 2
    mixT, _ = sb("mixT", [128, 8, NT], BF16)
    KA, _ = sb("KA", [128, NT], BF16)
    Vt, _ = sb("Vt", [128, 18, 2, 64], BF16)
    pT, _ = sb("pT", [128, 2, NT], BF16)
    LOC0 = off["v"]
    CONST_BYTES = 15616 + 256 + 640
    TOTAL = SB_END
    C0 = TOTAL - CONST_BYTES
    off["v"] = C0
    ident, _ = sb("ident", [128, 128], BF16)
    identf, _ = sb("identf", [128, 128], F32)
    ones_bf, _ = sb("ones_bf", [128, 128], BF16)
    ones_f, _ = sb("ones_f", [128, 128], F32)
    mhalf, _ = sb("mhalf", [128, 16], F32)
    vecT, _ = sb("vecT", [128, 128], F32)
    cT, _ = sb("cT", [128, 16], F32)
    pscT, _ = sb("pscT", [128, 4], F32)
    silT, _ = sb("silT", [128, 8, 2], BF16)
    modvs = [sb("modv%d" % i, [128, 48, 2], F32)[0] for i in range(DEPTH)]
    a1s = [sb("a1_%d" % i, [128, 8, 2], F32)[0] for i in range(DEPTH)]
    a2s = [sb("a2_%d" % i, [128, 8, 2], F32)[0] for i in range(DEPTH)]
    CUR_L = [0]
    WO = {}
    qg_bc, _ = sb("qg_bc", [128, DEPTH, 64], F32)
    kg_bc, _ = sb("kg_bc", [128, DEPTH, 64], F32)
    sgn_bc, _ = sb("sgn_bc", [128, DEPTH, 256], F32)
    bs_bc, _ = sb("bs_bc", [128, DEPTH, 2, 128], F32)
    wsT, _ = sb("wsT", [128, DEPTH, 4, 128], BF16)
    poolbd, _ = sb("poolbd", [128, DEPTH, 2, 128], BF16)
    invw, _ = sb("invw", [128, 2], F32)
    rcb, _ = sb("rcb_s", [128, 2, 2, 8], F32)
    cosT, _ = sb("cos_s", [128, 16, 32], F32)
    sinT, _ = sb("sin_s", [128, 16, 32], F32)
    small, _ = sb("small", [128, 128], F32)
    assert off["v"] <= TOTAL, off["v"]
    LOC_END = C0

    ps = nc.alloc_psum_tensor("ps", [128, 4096], F32).ap()

    def PS(b, n=1):
        return ps[:, b * 512:(b + n) * 512]

    def loc_alloc(skip=0):
        st = {"v": LOC0 + skip}

        def f(name, shape, dt, base=None):
            nbytes = int(np.prod(shape[1:])) * (4 if dt == F32 else 2)
            at = st["v"]
            st["v"] = (at + nbytes + 31) // 32 * 32
            assert st["v"] <= LOC_END, (name, st["v"], LOC_END)
            return nc.alloc_sbuf_tensor_at(nm(name), list(shape), dt, offset=at).ap()
        f.state = st
        return f

    REC = [None]

    def emit_op(eng, fn, r=(), w=(), dma=None, n=64, kind=""):
        if REC[0] is not None:
            if eng == "pe":
                cost = 0.04 + n / 1800.0
            elif eng == "act":
                cost = 0.2 + n * 0.00085
            elif eng == "dve":
                cost = 0.08 + n * 0.0016
            elif kind == "pow":
                cost = 0.35 + n * 0.16
            else:
                cost = 0.1 + n * 0.0026
            REC[0].append((eng, fn, tuple(r), tuple(w), dma, cost))
        else:
            S.op(eng, fn, r, w, dma=dma)

    def record(fn, *args):
        REC[0] = []
        fn(*args)
        ops = REC[0]
        REC[0] = None
        return ops

    def emit_scheduled(prog, lat=0.2, keep_pe_order=True):
        n_ = len(prog)
        lw, rd = {}, {}
        preds = [set() for _ in range(n_)]
        for i, (eng, fn, r, w, d, c) in enumerate(prog):
            for x in r:
                if x in lw:
                    preds[i].add(lw[x])
            for x in w:
                if x in lw:
                    preds[i].add(lw[x])
                preds[i].update(rd.get(x, ()))
            preds[i].discard(i)
            for x in w:
                lw[x] = i
                rd[x] = []
            for x in r:
                rd.setdefault(x, []).append(i)
        if keep_pe_order:
            prev = None
            for i in range(n_):
                if prog[i][0] == "pe":
                    if prev is not None:
                        preds[i].add(prev)
                    prev = i
        succs = [[] for _ in range(n_)]
        indeg = [len(p) for p in preds]
        for i, p in enumerate(preds):
            for j in p:
                succs[j].append(i)
        finish = [0.0] * n_
        efree = {e: 0.0 for e in Sched.ENG}
        ready = [i for i in range(n_) if indeg[i] == 0]
        order = []
        while ready:
            best, bstart = None, None
            for i in ready:
                st_ = efree[prog[i][0]]
                for j in preds[i]:
                    t_ = finish[j] + (0.0 if prog[j][0] == prog[i][0] == "pe" else lat)
                    if t_ > st_:
                        st_ = t_
                if best is None or st_ < bstart - 1e-9 or (abs(st_ - bstart) <= 1e-9 and i < best):
                    best, bstart = i, st_
            ready.remove(best)
            finish[best] = bstart + prog[best][5]
            efree[prog[best][0]] = finish[best]
            order.append(best)
            for k in succs[best]:
                indeg[k] -= 1
                if indeg[k] == 0:
                    ready.append(k)
        assert len(order) == n_
        for i in order:
            eng, fn, r, w, d, c = prog[i]
            S.op(eng, fn, r, w, dma=d)

    def emit_zip(lists):
        for x in lists:
            for it_ in x:
                if it_ is not None:
                    S.op(it_[0], it_[1], it_[2], it_[3], dma=it_[4])

    def fsz(ap):
        return int(np.prod(ap.shape[1:]))

    def mm(out, lhsT, rhs, start, stop, r, w, **kw):
        emit_op("pe", lambda e: e.matmul(out, lhsT=lhsT, rhs=rhs, start=start, stop=stop, **kw), r, w, n=fsz(rhs))

    def tr(out, in_, idn, r, w):
        emit_op("pe", lambda e: e.transpose(out=out, in_=in_, identity=idn), r, w, n=128)

    def act(out, in_, func, r, w, bias=None, scale=None, accum=None):
        kw = {}
        if bias is not None:
            kw["bias"] = bias
        if scale is not None:
            kw["scale"] = scale
        if accum is not None:
            kw["accum_out"] = accum
        emit_op("act", lambda e: e.activation(out=out, in_=in_, func=func, **kw), r, w, n=fsz(out))

    def tt(eng, out, in0, in1, op, r, w):
        emit_op(eng, lambda e: e.tensor_tensor(out=out, in0=in0, in1=in1, op=op), r, w, n=fsz(out),
                kind="pow" if op == ALU.pow else "")

    def ts(eng, out, in0, s1, s2, op0, op1, r, w):
        if s2 is None:
            emit_op(eng, lambda e: e.tensor_scalar(out=out, in0=in0, scalar1=s1, scalar2=None, op0=op0), r, w, n=fsz(out))
        else:
            emit_op(eng, lambda e: e.tensor_scalar(out=out, in0=in0, scalar1=s1, scalar2=s2, op0=op0, op1=op1), r, w, n=fsz(out))

    def stt(out, in0, scalar, in1, op0, op1, r, w):
        emit_op("dve", lambda e: e.scalar_tensor_tensor(out=out, in0=in0, scalar=scalar, in1=in1, op0=op0, op1=op1), r, w, n=fsz(out))

    def cp(eng, out, in_, r, w):
        if eng == "act":
            emit_op("act", lambda e: e.copy(out=out, in_=in_), r, w, n=fsz(out))
        else:
            emit_op(eng, lambda e: e.tensor_copy(out=out, in_=in_), r, w, n=fsz(out))

    def memset(eng, ap, val, w):
        emit_op(eng, lambda e: e.memset(ap, val), (), w)

    def dma(eng, out, in_, r, w, key):
        emit_op(eng, lambda e: e.dma_start(out=out, in_=in_), r, w, dma=key)

    dump_aps = {}

    def dump(name, ap, res):
        if name not in dumps:
            return
        d = nc.dram_tensor("dbg_" + name, list(ap.shape), ap.dtype, kind="ExternalOutput").ap()
        dma("sp", d, ap, res, ["dbg_" + name], "dbg_" + name)
        dump_aps[name] = "dbg_" + name

    def xres(ti, c=None):
        if c is None:
            return ["x%d_%d" % (ti, cc) for cc in range(8)]
        return ["x%d_%d" % (ti, c)]

    WIN_BYTES = 8 * 1536 * 2

    def load_win(l, win):
        wv = win_d[l].rearrange("(k p) n -> p k n", p=128)
        for g in range(2):
            for c in range(4):
                dma("pool", win[:, :, (c * 2 + g) * 64:(c * 2 + g + 1) * 64], wv[:, :, (g * 4 + c) * 64:(g * 4 + c + 1) * 64],
                    [], ["win"], "win")
        dma("pool", win[:, :, 512:768], wv[:, :, 512:768], [], ["win"], "win")
        dma("pool", win[:, :, 768:1024], wv[:, :, 1024:1280], [], ["win"], "win")
        dma("pool", win[:, :, 1024:1280], wv[:, :, 768:1024], [], ["win"], "win")
        dma("pool", win[:, :, 1280:1536], wv[:, :, 1280:1536], [], ["win"], "win")

    win0 = nc.alloc_sbuf_tensor_at(nm("win0"), [128, 8, 1536], BF16, offset=LOC0).ap()
    L = loc_alloc(WIN_BYTES)
    vst = L("vst", [128, 128], F32)
    vst2 = L("vst2", [32, 128], F32)
    wsst = L("wsst", [128, 8, 128], F32)
    xin = [L("xin0", [128, 1024], F32), L("xin1", [128, 1024], F32)]

    memset("dve", ones_bf, 1.0, ["ones_bf"])
    memset("dve", ones_f, 1.0, ["ones_f"])
    memset("dve", mhalf, -0.5, ["mhalf"])
    memset("dve", identf, 0.0, ["identf"])
    S.op("pool", lambda e: e.affine_select(out=identf, in_=identf, pattern=[[-1, 128]], compare_op=ALU.not_equal,
                                           fill=1.0, base=0, channel_multiplier=1), ["identf"], ["identf"])
    cp("dve", ident, identf, ["identf"], ["ident"])
    load_win(0, win0)
    memset("dve", invw[0:64, 0:1], 0.5, ["invw"])
    memset("dve", invw[64:128, 0:1], 0.25, ["invw"])
    memset("dve", invw[0:64, 1:2], 0.125, ["invw"])
    memset("dve", invw[64:128, 1:2], 0.0625, ["invw"])
    memset("dve", poolbd, 0.0, ["poolbd"])
    memset("dve", vst2, 0.0, ["vst2"])

    dma("sp", vst, vecs_d, [], ["vst"], "vst")
    dma("sp", vst2[0:16, :], cvec_d, ["vst2"], ["vst2"], "vst2")
    dma("sp", vst2[16:20, :], pscale_d, ["vst2"], ["vst2"], "vst2")
    dma("sp", cosT, cos_d, [], ["cosT"], "cosT")
    dma("sp", sinT, sin_d, [], ["sinT"], "sinT")
    dma("sp", rcb, rcb_d, [], ["rcb"], "rcb")
    for l in range(DEPTH):
        dma("sp", qg_bc[:, l, :], qn_d[l].partition_broadcast(128), [], ["qg_bc"], "qg")
        dma("sp", kg_bc[:, l, :], kn_d[l].partition_broadcast(128), [], ["kg_bc"], "kg")
        dma("sp", sgn_bc[:, l, :], sgn_d[l].partition_broadcast(128), [], ["sgn_bc"], "sgn")
        for g in range(4):
            h0 = (g % 2) * 64
            dma("sp", bs_bc[h0:h0 + 64, l, g // 2, :], bs_d[l, g].partition_broadcast(64), [], ["bs_bc"], "bs")
            dma("pool", poolbd[h0:h0 + 64, l, g // 2, h0:h0 + 64], pw_d[l, g], ["poolbd"], ["poolbd"], "poolbd")
        dma("sp", wsst[:, l * 4:(l + 1) * 4, :], ws_d[l].rearrange("h p q -> p h q"), [], ["wsst"], "wsst")
    ts("dve", qg_bc, qg_bc, 0.125, None, ALU.mult, None, ["qg_bc"], ["qg_bc"])
    ts("dve", sgn_bc, sgn_bc, 0.5, None, ALU.mult, None, ["sgn_bc"], ["sgn_bc"])

    tr(PS(4)[:, 0:128], vst, identf, ["vst", "identf"], ["ps4"])
    cp("dve", vecT, PS(4)[:, 0:128], ["ps4"], ["vecT"])
    tr(PS(5)[:, 0:32], vst2, identf[0:32, 0:32], ["vst2", "identf"], ["ps5"])
    cp("dve", cT, PS(5)[:, 0:16], ["ps5"], ["cT"])
    cp("dve", pscT, PS(5)[:, 16:20], ["ps5"], ["pscT"])
    sil_t = small[:, 0:16]
    act(sil_t, cT, AF.Tanh, ["cT"], ["small"], scale=0.5)
    stt(sil_t, sil_t, 1.0, cT, ALU.add, ALU.mult, ["small", "cT"], ["small"])
    ts("dve", silT.rearrange("p k s -> p s k"), sil_t.rearrange("p (s k) -> p s k", s=2), 0.5, None, ALU.mult, None,
       ["small"], ["silT"])
    for i in range(8):
        b = 6 + (i % 2)
        tr(PS(b)[:, 0:128], wsst[:, i, :], identf, ["wsst", "identf"], ["ps%d" % b])
        cp("dve" if i % 2 else "act", wsT[:, i // 4, i % 4, :], PS(b)[:, 0:128], ["ps%d" % b], ["wsT"])

    for t128 in range(18):
        ti = min(t128 // 4, 4)
        slot = t128 % 2
        src = x_d[t128 * 128:(t128 + 1) * 128, :] if t128 < 16 else ctx_d[(t128 - 16) * 128:(t128 - 15) * 128, :]
        dma("sp", xin[slot], src, [], ["xin%d" % slot], "xin%d" % slot)
        b0 = slot * 2
        for c in range(8):
            tr(PS(b0 + c // 4)[:, (c % 4) * 128:(c % 4 + 1) * 128], xin[slot][:, c * 128:(c + 1) * 128], identf,
               ["xin%d" % slot, "identf"], ["ps%d" % (b0 + c // 4)])
        for hh in range(2):
            cp("act" if hh else "dve", xT[:, hh * 4:(hh + 1) * 4, t128 * 128:(t128 + 1) * 128],
               PS(b0 + hh).rearrange("p (c t) -> p c t", c=4), ["ps%d" % (b0 + hh)],
               ["x%d_%d" % (ti, c) for c in range(hh * 4, hh * 4 + 4)])

    def mod_chunk(l, ch, wm):
        pm = PS(5)[:, 0:96]
        s = ch % 2
        dma("pool", wm[s], wmod_d[l].rearrange("(k p) n -> p k n", p=128)[:, :, ch * 512:(ch + 1) * 512],
            [], ["wm%d" % s], "wm%d" % s)
        for jj in range(4):
            jc = ch * 4 + jj
            for k in range(8):
                mm(pm[:, jc * 2:jc * 2 + 2], wm[s][:, k, jj * 128:(jj + 1) * 128], silT[:, k, :], k == 0, k == 7,
                   ["wm%d" % s, "silT"], ["ps5"])

    def mod_finish(l):
        pm = PS(5)[:, 0:96]
        modv, a1, a2 = modvs[l], a1s[l], a2s[l]
        tt("dve", modv, pm.rearrange("p (j s) -> p j s", s=2),
           vecT[:, l * 64:l * 64 + 48].unsqueeze(2).broadcast_to([128, 48, 2]), ALU.add, ["ps5", "vecT"], ["modv%d" % l])
        stt(a1, modv[:, 8:16, :], 1.0, vecT[:, l * 64 + 48:l * 64 + 56].unsqueeze(2).broadcast_to([128, 8, 2]),
            ALU.add, ALU.mult, ["modv%d" % l, "vecT"], ["a1_%d" % l])
        stt(a2, modv[:, 32:40, :], 1.0, vecT[:, l * 64 + 56:l * 64 + 64].unsqueeze(2).broadcast_to([128, 8, 2]),
            ALU.add, ALU.mult, ["modv%d" % l, "vecT"], ["a2_%d" % l])

    def mod_phase(l):
        if l > 0:
            return
        wm = [L("wm0", [128, 8, 512], BF16), L("wm1", [128, 8, 512], BF16)]
        for ch in range(12):
            mod_chunk(0, ch, wm)
        mod_finish(0)

    def MOD(j, c, s):
        return modvs[CUR_L[0]][:, j * 8 + c, s:s + 1]

    def MODR():
        return "modv%d" % CUR_L[0]

    def norm_mod(ti, hbuf, hres, aT, bj, bS, bB, R, rs, tmp, tmpres):
        t0, T = TILES[ti]
        s = 1 if ti == 4 else 0
        nsub = T // 128
        act(hbuf[:, :, 0:T], xT[:, :, t0:t0 + T], AF.Square, xres(ti), hres)
        for sub in range(nsub):
            for k in range(8):
                mm(PS(bS)[:, sub:sub + 1], hbuf[:, k, sub * 128:(sub + 1) * 128], ones_bf[:, 0:1], k == 0, k == 7,
                   [hres[k], "ones_bf"], ["ps%d" % bS])
        ts("dve", rs[:, 0:nsub], PS(bS)[:, 0:nsub], 1.0 / 1024, EPS, ALU.mult, ALU.add, ["ps%d" % bS], ["rs_ms"])
        tt("pool", rs[:, 4:4 + nsub], rs[:, 0:nsub], mhalf[:, 0:nsub], ALU.pow, ["rs_ms", "mhalf"], ["rs_r"])
        for sub in range(nsub):
            ts("dve", R[:, sub, :], ones_f, rs[:, 4 + sub:5 + sub], None, ALU.mult, None, ["rs_r", "ones_f"],
               ["R%d" % sub, "gsc"])
            mm(PS(bB)[:, sub * 128:(sub + 1) * 128], R[:, sub, :], identf, True, True, ["R%d" % sub, "gsc", "identf"],
               ["ps%d" % bB])
        for c in range(8):
            j = c % 2
            stt(tmp[j][:, 0:T], xT[:, c, t0:t0 + T], aT[:, c, s:s + 1], PS(bB)[:, 0:T], ALU.mult, ALU.mult,
                xres(ti, c) + ["ps%d" % bB, "a1_%d" % CUR_L[0], "a2_%d" % CUR_L[0]], [tmpres[j]])
            act(hbuf[:, c, 0:T], tmp[j][:, 0:T], AF.Identity, [tmpres[j], MODR()], [hres[c]], bias=MOD(bj, c, s))

    def gelu2(src, dst, g1, r_src, w_dst, g1res):
        act(g1, src, AF.Square, r_src, [g1res])
        ts("dve", g1, g1, 0.044715, 1.0, ALU.mult, ALU.add, [g1res], [g1res])
        tt("dve", g1, g1, src, ALU.mult, [g1res] + r_src, [g1res])
        act(g1, g1, AF.Tanh, [g1res], [g1res], scale=GELU_C)
        stt(dst, g1, 1.0, src, ALU.add, ALU.mult, [g1res] + r_src, w_dst)

    def phase_a(l):
        S.barrier()
        CUR_L[0] = l
        last = l == DEPTH - 1
        La = loc_alloc()
        m67 = {"v": MIX67}

        def Lm(name, shape, dt):
            nbytes = int(np.prod(shape[1:])) * (4 if dt == F32 else 2)
            at = m67["v"]
            end = (at + nbytes + 31) // 32 * 32
            if end <= MIX67 + 2 * NT * 2:
                m67["v"] = end
                return nc.alloc_sbuf_tensor_at(nm(name), list(shape), dt, offset=at).ap()
            return La(name, shape, dt)

        if l == 0:
            win = win0
            La.state["v"] = LOC0 + WIN_BYTES
        else:
            win = La("win", [128, 8, 1536], BF16)
        hq = [La("hq0", [128, 8, 512], BF16), La("hq1", [128, 8, 512], BF16)]
        R = La("R", [128, 4, 128], F32)
        tmp = [La("tmp0", [128, 512], F32), La("tmp1", [128, 512], F32)]
        uT0_ = La("uT0", [128, 2, 512], BF16)
        uTs = [uT0_, uT0_]
        gsc = R.rearrange("p s t -> p (s t)")
        TS = []
        for pq in range(2):
            A = La
            g1_ = A("g1", [128, 256], F32)
            TS.append(dict(qn=A("qn", [128, 10, 64], F32), rt=[A("rt0", [128, 10, 32], F32), A("rt1", [128, 10, 32], F32)],
                           qb=A("qb", [128, 10, 64], BF16), g1=g1_, v2=A("v2", [128, 256], F32),
                           vn=A("vn", [128, 256], BF16), sg=g1_.rearrange("p (c t) -> p c t", c=2)))
        if l > 0:
            load_win(l, win)

        def hres_of(ti):
            return ["hq%d_%d" % (ti % 2, k) for k in range(8)]

        def tm_mm(ti, sub, need_q):
            hbuf, hres = hq[ti % 2], hres_of(ti)
            bq, bk = (4, 5) if sub % 2 == 0 else (2, 3)
            for k in range(8):
                lt = hbuf[:, k, sub * 128:(sub + 1) * 128]
                if need_q:
                    mm(PS(bq), lt, win[:, k, 0:512], k == 0, k == 7, [hres[k], "win"], ["ps%d" % bq])
                mm(PS(bk), lt, win[:, k, 512:1024], k == 0, k == 7, [hres[k], "win"], ["ps%d" % bk])

        def tm_post(ti, sub, full):
            t0, T = TILES[ti]
            is_ctx = ti == 4
            pq = sub % 2
            X = TS[pq]
            qn, rt, qb, g1, v2, vn, sg = X["qn"], X["rt"], X["qb"], X["g1"], X["v2"], X["vn"], X["sg"]
            sqq = qn.rearrange("p h d -> p (h d)")
            uT = uTs[ti % 2]
            P = lambda n: "%s_%d" % (n, pq)
            t128 = t0 // 128 + sub
            tok = slice(t128 * 128, (t128 + 1) * 128)
            bq, bk = (4, 5) if pq == 0 else (2, 3)
            psT6 = PS(6).bitcast(BF16)
            psT7 = PS(7).bitcast(BF16)
            qT = psT6[:, pq * 512:(pq + 1) * 512]
            kT = psT7[:, pq * 128:(pq + 1) * 128]
            psG = PS(bk)[:, 256:512]
            rq, rk = ["ps%d" % bq], ["ps%d" % bk]
            psQ, psK, psV, psGV = PS(bq), PS(bk)[:, 0:128], PS(bk)[:, 128:256], PS(bk)[:, 256:512]
            h0 = 0 if full else 8
            h1 = 11 if full else 10
            sc0 = 16 + pq * 40
            ss, ms, rs = small[:, sc0:sc0 + 11], small[:, sc0 + 11:sc0 + 22], small[:, sc0 + 22:sc0 + 33]
            cp("act", Vt[:, t128, :, :], psV.rearrange("p (g d) -> p g d", g=2), rk, ["V%d" % t128])
            if full:
                gelu2(psGV, v2, g1, rk, [P("v2")], P("g1"))
                act(g1, v2, AF.Square, [P("v2")], [P("g1"), P("ss")], accum=ss[:, 10:11])
                act(sqq[:, 0:512], psQ, AF.Square, rq, [P("qn_q")])
            act(sqq[:, 512:640], psK, AF.Square, rk, [P("qn_k")])
            emit_op("dve", lambda e: e.tensor_reduce(out=ss[:, h0:10], in_=qn[:, h0:10, :], axis=AX.X, op=ALU.add),
                    [P("qn_q"), P("qn_k")], [P("ss")], n=640)
            ts("dve", ms[:, h0:10], ss[:, h0:10], 1.0 / 64, EPS, ALU.mult, ALU.add, [P("ss")], [P("ms")])
            if full:
                ts("dve", ms[:, 10:11], ss[:, 10:11], 0.25 / 256, EPS, ALU.mult, ALU.add, [P("ss")], [P("ms")])
            tt("pool", rs[:, h0:h1], ms[:, h0:h1], mhalf[:, h0:h1], ALU.pow, [P("ms"), "mhalf"], [P("rs")])
            if full:
                stt(vn, v2, rs[:, 10:11], sgn_bc[:, l, :], ALU.mult, ALU.mult, [P("v2"), P("rs"), "sgn_bc"], [P("vn")])
                for h in range(4):
                    o0 = (h % 2) * 64
                    mm(psG[o0:o0 + 64, (h // 2) * 128:(h // 2 + 1) * 128], vn[:, h * 64:(h + 1) * 64], wsT[:, l, h, :],
                       True, True, [P("vn"), "wsT"], ["ps%d" % bk], tile_position=(0, o0))
                tt("dve", sg, psG.rearrange("p (c t) -> p c t", c=2), bs_bc[:, l, :, :], ALU.add,
                   ["ps%d" % bk, "bs_bc"], [P("g1")])
                stt(mixT[:, 4:6, tok], sg, 0.5, uT[:, :, sub * 128:(sub + 1) * 128], ALU.mult, ALU.mult,
                    [P("g1"), "uT0_0", "uT0_1"], ["mix4_%d" % ti, "mix5_%d" % ti])
                tt("dve", qn[:, 0:8, :], psQ.rearrange("p (h d) -> p h d", d=64),
                   rs[:, 0:8].unsqueeze(2).broadcast_to([128, 8, 64]), ALU.mult, rq + [P("rs")], [P("qn_q")])
                tt("pool", qn[:, 0:8, :], qn[:, 0:8, :], qg_bc[:, l, :].unsqueeze(1).broadcast_to([128, 8, 64]), ALU.mult,
                   [P("qn_q"), "qg_bc"], [P("qn_q")])
            tt("dve", qn[:, 8:10, :], psK.rearrange("p (h d) -> p h d", d=64),
               rs[:, 8:10].unsqueeze(2).broadcast_to([128, 2, 64]), ALU.mult, rk + [P("rs")], [P("qn_k")])
            tt("pool", qn[:, 8:10, :], qn[:, 8:10, :], kg_bc[:, l, :].unsqueeze(1).broadcast_to([128, 2, 64]), ALU.mult,
               [P("qn_k"), "kg_bc"], [P("qn_k")])
            nh = 10 - h0
            if not is_ctx:
                cs = cosT[:, t128, :].unsqueeze(1).broadcast_to([128, nh, 32])
                sn = sinT[:, t128, :].unsqueeze(1).broadcast_to([128, nh, 32])
                x1, x2 = qn[:, h0:10, 0:32], qn[:, h0:10, 32:64]
                rr = [P("qn_q"), P("qn_k"), "cosT", "sinT"]
                tt("pool", rt[0][:, h0:10, :], x1, cs, ALU.mult, rr, [P("rt0")])
                tt("pool", rt[1][:, h0:10, :], x2, sn, ALU.mult, rr, [P("rt1")])
                tt("dve", qb[:, h0:10, 0:32], rt[0][:, h0:10, :], rt[1][:, h0:10, :], ALU.subtract, [P("rt0"), P("rt1")], [P("qb")])
                tt("pool", rt[0][:, h0:10, :], x2, cs, ALU.mult, rr, [P("rt0")])
                tt("pool", rt[1][:, h0:10, :], x1, sn, ALU.mult, rr, [P("rt1")])
                tt("dve", qb[:, h0:10, 32:64], rt[0][:, h0:10, :], rt[1][:, h0:10, :], ALU.add, [P("rt0"), P("rt1")], [P("qb")])
            else:
                cp("dve", qb[:, h0:10, :], qn[:, h0:10, :], [P("qn_q"), P("qn_k")], [P("qb")])
            qbf = qb.rearrange("p h d -> p (h d)")
            if full:
                for c in range(4):
                    tr(qT[:, c * 128:(c + 1) * 128], qbf[:, c * 128:(c + 1) * 128], ident, [P("qb"), "ident"], ["ps6"])
            tr(kT, qbf[:, 512:640], ident, [P("qb"), "ident"], ["ps7"])
            if full:
                cp("act", mixT[:, 0:4, tok], qT.rearrange("p (c t) -> p c t", c=4), ["ps6"],
                   ["mix%d_%d" % (c, ti) for c in range(4)])
            cp("act", KA[:, tok], kT, ["ps7"], ["K%d" % t128])

        def norm_fm(ti):
            t0, T = TILES[ti]
            full = not (last and ti == 4)
            hbuf, hres = hq[ti % 2], hres_of(ti)
            uT = uTs[ti % 2]
            norm_mod(ti, hbuf, hres, a1s[l], 0, 0, 1, R, small[:, 0:8], tmp, ["tmp0", "tmp1"])
            if full:
                for fc in range(4):
                    b = 2 + fc % 2
                    for k in range(8):
                        mm(PS(b)[:, 0:T], win[:, k, 1024 + fc * 128:1024 + (fc + 1) * 128], hbuf[:, k, 0:T], k == 0, k == 7,
                           [hres[k], "win"], ["ps%d" % b])
                    if fc < 2:
                        gelu2(PS(b)[:, 0:T], uT[:, fc, 0:T], gsc[:, 0:T], ["ps%d" % b], ["uT0_%d" % fc], "gsc")
                    else:
                        cp("act", pT[:, fc - 2, t0:t0 + T], PS(b)[:, 0:T], ["ps%d" % b], ["pT%d_%d" % (fc - 2, ti)])

        def fullf(ti):
            return not (last and ti == 4)

        def zipped(lists, burst=None, at=0, pre=None):
            out = list(pre) if pre else []
            n_ = max(len(x) for x in lists)
            for i in range(max(n_, at + 1)):
                for x in lists:
                    if i < len(x) and x[i] is not None:
                        out.append(x[i])
                if burst is not None and i == at:
                    out.extend(burst)
            return out

        prog = record(norm_fm, 0) + record(tm_mm, 0, 0, fullf(0)) + record(tm_mm, 0, 1, fullf(0))
        for ti in range(5):
            nsub = TILES[ti][1] // 128
            full = fullf(ti)
            lists = [record(tm_post, ti, 0, full), record(tm_post, ti, 1, full)]
            burst = pre = nf_rest = None
            if nsub == 4:
                burst = record(tm_mm, ti, 2, full) + record(tm_mm, ti, 3, full)
                if ti + 1 < 5:
                    nf = record(norm_fm, ti + 1)
                    ns1 = TILES[ti + 1][1] // 128
                    n_norm = 1 + 10 * ns1 + 18
                    pre, nf_rest = nf[:n_norm], nf[n_norm:]
            prog += zipped(lists, burst, TM_SKEW, pre)
            if nsub == 4:
                lists = [record(tm_post, ti, 2, full), record(tm_post, ti, 3, full)]
                burst = None
                if ti + 1 < 5:
                    burst = (nf_rest + record(tm_mm, ti + 1, 0, fullf(ti + 1))
                             + record(tm_mm, ti + 1, 1, fullf(ti + 1)))
                prog += zipped(lists, burst, TM_SKEW)
        if ZIP:
            emit_scheduled(prog)
        else:
            emit_zip([prog])

    def pool_phase(l):
        return

    def pool_prep(l, A):
        last = l == DEPTH - 1
        P0 = A("P0", [128, 2064], F32)
        s2 = A("s2", [128, 2064], F32)
        s4 = A("s4", [128, 2064], F32)
        s8 = A("s8", [128, 2064], F32)
        s16 = s2
        fx = A("fx", [128, 16], F32)
        streams = [(0, 2048, [0, 1, 2, 3])] + ([] if last else [(2048, 256, [4])])
        for (t0, N, tis) in streams:
            for ch in range(2):
                pres = ["pT%d_%d" % (ch, ti) for ti in tis]
                memset("dve", P0[:, 0:8], 0.0, ["P0"])
                memset("dve", P0[:, 8 + N:16 + N], 0.0, ["P0"])
                cp("dve", P0[:, 8:8 + N], pT[:, ch, t0:t0 + N], pres, ["P0"])
                tt("dve", s2[:, 1:N + 16], P0[:, 0:N + 15], P0[:, 1:N + 16], ALU.add, ["P0"], ["s2"])
                tt("dve", s4[:, 2:N + 15], s2[:, 1:N + 14], s2[:, 3:N + 16], ALU.add, ["s2"], ["s4"])
                if ch == 0:
                    srcs = [(0, 64, s2, 8), (64, 128, s4, 8)]
                else:
                    tt("dve", s8[:, 4:N + 13], s4[:, 2:N + 11], s4[:, 6:N + 15], ALU.add, ["s4"], ["s8"])
                    tt("dve", s16[64:128, 0:N], s8[64:128, 4:N + 4], s8[64:128, 12:N + 12], ALU.add, ["s8"], ["s2"])
                    srcs = [(0, 64, s8, 8), (64, 128, s16, 0)]
                for (p0, p1, sw, o_) in srcs:
                    stt(pT[p0:p1, ch, t0:t0 + N], sw[p0:p1, o_:o_ + N], invw[p0:p1, ch:ch + 1], P0[p0:p1, 8:8 + N],
                        ALU.mult, ALU.subtract, ["s2", "s4", "s8", "P0", "invw"], pres)
                    for side, c0 in ((0, 0), (1, N - 8)):
                        tt("dve", fx[p0:p1, side * 8:side * 8 + 8], sw[p0:p1, o_ + c0:o_ + c0 + 8], rcb[p0:p1, ch, side, :],
                           ALU.mult, ["s2", "s4", "s8", "rcb"], ["fx"])
                        tt("dve", pT[p0:p1, ch, t0 + c0:t0 + c0 + 8], fx[p0:p1, side * 8:side * 8 + 8],
                           P0[p0:p1, 8 + c0:16 + c0], ALU.subtract, ["fx", "P0"] + pres, pres)

    def pool_finish(l):
        last = l == DEPTH - 1
        i = 0
        for ti in range(4 if last else 5):
            t0, T = TILES[ti]
            for ch in range(2):
                b = 6 + i % 2
                i += 1
                mm(PS(b)[:, 0:T], poolbd[:, l, ch, :], pT[:, ch, t0:t0 + T], True, True, ["pT%d_%d" % (ch, ti), "poolbd"],
                   ["ps%d" % b])
                act(mixT[:, 6 + ch, t0:t0 + T], PS(b)[:, 0:T], AF.Copy, ["ps%d" % b, "pscT"], ["mix%d_%d" % (6 + ch, ti)],
                    scale=pscT[:, l * 2 + ch:l * 2 + ch + 1])

    def attention(l):
        S.barrier()
        CUR_L[0] = l
        last = l == DEPTH - 1
        Lb = loc_alloc()
        PT = [Lb("PT%d" % i, [128, 1024], BF16) for i in range(3)]
        rec = [Lb("rec%d" % i, [128, 512], F32) for i in range(2)]
        wo = Lb("wo", [128, 8, 1024], BF16)
        WO[l] = (wo, Lb.state["v"])
        for e_ in range(2):
            dma("pool", wo[e_ * 64:(e_ + 1) * 64, 0:4, :],
                wout_d[l][e_ * 256:(e_ + 1) * 256, :].rearrange("(c d) n -> d c n", d=64), [], ["wo"], "wo")
        dma("pool", wo[:, 4:8, :], wout_d[l][512:1024, :].rearrange("(k p) n -> p k n", p=128), [], ["wo"], "wo")
        blocks = [(c, ti, list(range(18))) for c in range(4) for ti in range(4)]
        if not last:
            blocks += [(c, 4, [16, 17]) for c in range(4)]
        prep = record(pool_prep, l, Lb)
        prep_pos = [0]
        per_block = -(-len(prep) // 18)

        def drip(n):
            for eng, fn, r_, w_, d_, c_ in prep[prep_pos[0]:prep_pos[0] + n]:
                S.op(eng, fn, r_, w_, dma=d_)
            prep_pos[0] += n

        units = []
        for bi, (c, ti, kts) in enumerate(blocks):
            for ki, kt in enumerate(kts):
                units.append((bi, c, ti, kt, ki, len(kts)))

        def emit_s(j):
            bi, c, ti, kt, ki, nk = units[j]
            t0, T = TILES[ti]
            g = c // 2
            sb0 = (j % 2) * 2
            pt = PT[j % 3]
            qres = ["mix%d_%d" % (c, ti)]
            for e_ in range(2):
                mm(PS(sb0 + e_)[:, 0:T], KA[e_ * 64:(e_ + 1) * 64, kt * 128:(kt + 1) * 128],
                   mixT[e_ * 64:(e_ + 1) * 64, c, t0:t0 + T], True, True, ["K%d" % kt] + qres, ["ps%d" % (sb0 + e_)])
            S.op("act", (lambda pt=pt, sb0=sb0, T=T: lambda e: e.activation(
                out=pt.rearrange("p (e t) -> p e t", e=2)[:, :, 0:T],
                in_=PS(sb0, 2).rearrange("p (e t) -> p e t", e=2)[:, :, 0:T], func=AF.Exp))(),
                ["ps%d" % sb0, "ps%d" % (sb0 + 1)], ["PT%d" % (j % 3)])

        def emit_pv(j):
            bi, c, ti, kt, ki, nk = units[j]
            t0, T = TILES[ti]
            g = c // 2
            bo = 4 + bi % 2
            bd = 6 + bi % 2
            pt = PT[j % 3]
            ptr = "PT%d" % (j % 3)
            for e_ in range(2):
                o0 = e_ * 64
                mm(PS(bo)[o0:o0 + 64, 0:T], Vt[:, kt, e_, :], pt[:, e_ * 512:e_ * 512 + T], ki == 0, ki == nk - 1,
                   ["V%d" % kt, ptr], ["ps%d_%d" % (bo, e_)], tile_position=(0, o0))
            for e_ in range(2):
                o0 = e_ * 64
                mm(PS(bd)[o0:o0 + 64, 0:T], ones_bf[:, 0:64], pt[:, e_ * 512:e_ * 512 + T], ki == 0, ki == nk - 1,
                   ["ones_bf", ptr], ["ps%d_%d" % (bd, e_)], tile_position=(0, o0))
            if ki == nk - 1:
                rc_ = rec[bi % 2]
                S.op("dve", (lambda rc_=rc_, bd=bd, T=T: lambda e: e.reciprocal(out=rc_[:, 0:T], in_=PS(bd)[:, 0:T]))(),
                     ["ps%d_0" % bd, "ps%d_1" % bd], ["rec%d" % (bi % 2)])
                tt("dve", mixT[:, c, t0:t0 + T], PS(bo)[:, 0:T], rc_[:, 0:T], ALU.mult,
                   ["ps%d_0" % bo, "ps%d_1" % bo, "rec%d" % (bi % 2)], ["mix%d_%d" % (c, ti)])
                drip(per_block)

        for j in range(len(units)):
            emit_s(j)
            if j >= 1:
                emit_pv(j - 1)
        emit_pv(len(units) - 1)
        drip(len(prep))

    FCTX = {}
    KA0 = MIX0 + 8 * NT * 2

    def ffn_setup(l):
        def at(name, shape, dt, pos):
            nbytes = int(np.prod(shape[1:])) * (4 if dt == F32 else 2)
            end = (pos + nbytes + 31) // 32 * 32
            return nc.alloc_sbuf_tensor_at(nm(name), list(shape), dt, offset=pos).ap(), end
        p = MIX0
        th = []
        for i in range(2):
            t_, p = at("th%d" % i, [128, 512], F32, p)
            th.append(t_)
        wgs, wus, wds = [], [], []
        for i in range(3):
            t_, p = at("wg%d" % i, [128, 8, 128], BF16, p)
            wgs.append(t_)
        for i in range(3):
            t_, p = at("wu%d" % i, [128, 8, 128], BF16, p)
            wus.append(t_)
        for i in range(2):
            t_, p = at("wd%d" % i, [128, 22, 128], BF16, p)
            wds.append(t_)
        assert p <= KA0, (p, KA0)
        p = KA0
        h2, p = at("h2", [128, 8, 1280], BF16, p)
        R, p = at("R", [128, 4, 128], F32, p)
        tmp = []
        for i in range(2):
            t_, p = at("tmp%d" % i, [128, 512], F32, p)
            tmp.append(t_)
        assert p <= LOC0 + 10240, (p, LOC0 + 10240)
        aT, p = at("aT", [128, 22, 1280], BF16, p)
        assert p <= LOC_END, (p, LOC_END)
        FCTX[l] = (h2, aT, R, tmp, th, wgs, wus, wds)

    def ffn_goffs(tis):
        offs, o = {}, 0
        for ti in tis:
            offs[ti] = o
            o += TILES[ti][1]
        return offs

    def ffn_h2res(offs, ti):
        return ["h2_%d_%d" % (offs[ti], k) for k in range(8)]

    def ffn_norm(l, offs, ti, bS, bB):
        h2, aT, R, tmp = FCTX[l][0:4]
        T = TILES[ti][1]
        hb = h2[:, :, offs[ti]:offs[ti] + T]
        norm_mod(ti, hb, ffn_h2res(offs, ti), a2s[l], 3, bS, bB, R, small[:, 0:8], tmp, ["tmp0", "tmp1"])

    def split_norm(ops, ti):
        nsub = TILES[ti][1] // 128
        n1 = 1 + 8 * nsub + 2
        rb = ops[n1:n1 + 2 * nsub]
        return (ops[:1], ops[1:n1] + rb[0::2], rb[1::2] + ops[n1 + 2 * nsub:])

    def emit_list(ops):
        for eng, fn, r, w, d, c in ops:
            S.op(eng, fn, r, w, dma=d)

    def phase_c1(l):
        S.barrier()
        CUR_L[0] = l
        last = l == DEPTH - 1
        Lc = loc_alloc()
        wo, wo_end = WO[l]
        Lc.state["v"] = wo_end
        pool_finish(l)
        ffn_setup(l)
        g0 = [0, 1]
        offs0 = ffn_goffs(g0)
        pcs = [split_norm(record(ffn_norm, l, offs0, ti, 4, 6), ti) for ti in g0]
        ptres = ["pT%d_%d" % (c, ti) for c in range(2) for ti in range(5)]
        nxt = l + 1 if l + 1 < DEPTH else None
        if nxt is not None:
            wmn = [Lc("wm0", [128, 8, 512], BF16), Lc("wm1", [128, 8, 512], BF16)]
        i = 0
        for ti in range(4 if last else 5):
            t0, T = TILES[ti]
            s = 1 if ti == 4 else 0
            for dc in range(8):
                if nxt is not None and i % 3 == 0 and i // 3 < 12:
                    mod_chunk(nxt, i // 3, wmn)
                if i == 16:
                    for e_ in ("act", "dve"):
                        S.op(e_, None, (), ptres)
                    emit_list(pcs[0][0])
                    emit_list(pcs[1][0])
                elif i == 18:
                    emit_list(pcs[0][1])
                elif i == 21:
                    emit_list(pcs[0][2])
                    emit_list(pcs[1][1])
                elif i == 24:
                    emit_list(pcs[1][2])
                b = i % 4
                i += 1
                for k in range(8):
                    mm(PS(b)[:, 0:T], wo[:, k, dc * 128:(dc + 1) * 128], mixT[:, k, t0:t0 + T], k == 0, k == 7,
                       ["wo", "mix%d_%d" % (k, ti)], ["ps%d" % b])
                stt(xT[:, dc, t0:t0 + T], PS(b)[:, 0:T], MOD(2, dc, s), xT[:, dc, t0:t0 + T], ALU.mult, ALU.add,
                    ["ps%d" % b, MODR()] + xres(ti, dc), xres(ti, dc))
        if nxt is not None:
            mod_finish(nxt)

    def ffn_phase(l):
        S.barrier()
        CUR_L[0] = l
        last = l == DEPTH - 1
        groups = [[0, 1], [2, 3] if last else [2, 3, 4]]
        h2, aT, R, tmp, th, wgs, wus, wds = FCTX[l]
        allmix = ["mix%d_%d" % (c, ti) for c in range(8) for ti in range(5)]
        fence = allmix + ["K%d" % t for t in range(18)] + ["V%d" % t for t in range(18)] + \
            ["pT%d_%d" % (c, ti) for c in range(2) for ti in range(5)] + ["wo", "PT0", "PT1", "PT2", "rec0", "rec1"]
        fi = [0]
        wgv = wg_d[l].rearrange("(k p) n -> p k n", p=128)
        wuv = wu_d[l].rearrange("(k p) n -> p k n", p=128)
        wdv = wd_d[l].rearrange("(f p) n -> p f n", p=128)

        def load_gu(f):
            s_ = f % 3
            dma("pool", wgs[s_], wgv[:, :, f * 128:(f + 1) * 128], [], ["wg%d" % s_], "wg%d" % s_)
            dma("pool", wus[s_], wuv[:, :, f * 128:(f + 1) * 128], [], ["wu%d" % s_], "wu%d" % s_)

        def load_d(dc):
            s_ = dc % 2
            dma("pool", wds[s_], wdv[:, :, dc * 128:(dc + 1) * 128], [], ["wd%d" % s_], "wd%d" % s_)

        goffs, h2res = ffn_goffs, ffn_h2res

        def do_norm(offs, ti):
            ffn_norm(l, offs, ti, 0, 1)

        for gi, tis in enumerate(groups):
            offs = goffs(tis)
            load_gu(0)
            load_gu(1)
            if gi == 0:
                offs1 = goffs(groups[1])
                pieces = []
                for ti in groups[1]:
                    pieces.append(split_norm(record(do_norm, offs1, ti), ti))
            for f in range(22):
                if f + 2 < 22:
                    load_gu(f + 2)
                elif f == 20:
                    load_d(0)
                elif f == 21:
                    load_d(1)
                s_ = f % 3
                for ti in tis:
                    T = TILES[ti][1]
                    o = offs[ti]
                    j = fi[0] % 2
                    fi[0] += 1
                    bg, bu = 2 + j, 4 + j
                    hres = h2res(offs, ti)
                    for k in range(8):
                        mm(PS(bg)[:, 0:T], wgs[s_][:, k, :], h2[:, k, o:o + T], k == 0, k == 7, ["wg%d" % s_, hres[k]],
                           ["ps%d" % bg])
                    for k in range(8):
                        mm(PS(bu)[:, 0:T], wus[s_][:, k, :], h2[:, k, o:o + T], k == 0, k == 7, ["wu%d" % s_, hres[k]],
                           ["ps%d" % bu])
                    act(th[j][:, 0:T], PS(bg)[:, 0:T], AF.Tanh, ["ps%d" % bg], ["th%d" % j], scale=0.5)
                    stt(th[j][:, 0:T], th[j][:, 0:T], 1.0, PS(bg)[:, 0:T], ALU.add, ALU.mult, ["th%d" % j, "ps%d" % bg],
                        ["th%d" % j])
                    stt(aT[:, f, o:o + T], th[j][:, 0:T], 0.5, PS(bu)[:, 0:T], ALU.mult, ALU.mult, ["th%d" % j, "ps%d" % bu],
                        ["aT%d_%d" % (f, ti)])
            yi = 0
            if gi == 0:
                for pc in pieces:
                    emit_list(pc[0])
            for dc in range(8):
                if gi == 0:
                    if 2 <= dc <= len(pieces) + 1:
                        emit_list(pieces[dc - 2][2])
                    if 1 <= dc <= len(pieces):
                        emit_list(pieces[dc - 1][1])
                s_ = dc % 2
                for ti in tis:
                    t0, T = TILES[ti]
                    o = offs[ti]
                    sidx = 1 if ti == 4 else 0
                    b = 6 + yi % 2
                    yi += 1
                    for f in range(22):
                        mm(PS(b)[:, 0:T], wds[s_][:, f, :], aT[:, f, o:o + T], f == 0, f == 21,
                           ["wd%d" % s_, "aT%d_%d" % (f, ti)], ["ps%d" % b])
                    stt(xT[:, dc, t0:t0 + T], PS(b)[:, 0:T], MOD(5, dc, sidx), xT[:, dc, t0:t0 + T], ALU.mult, ALU.add,
                        ["ps%d" % b, MODR()] + xres(ti, dc), xres(ti, dc))
                if dc + 2 < 8:
                    load_d(dc + 2)

    def final_phase():
        S.barrier()
        st = {"v": MIX0}

        def Lz(name, shape, dt):
            nbytes = int(np.prod(shape[1:])) * (4 if dt == F32 else 2)
            at = st["v"]
            st["v"] = (at + nbytes + 31) // 32 * 32
            assert st["v"] <= LOC_END
            return nc.alloc_sbuf_tensor_at(nm(name), list(shape), dt, offset=at).ap()

        fn_bc = Lz("fn_bc", [128, 1024], F32)
        ob = [Lz("ob0", [128, 1024], F32), Lz("ob1", [128, 1024], F32)]
        junk = Lz("junk", [128, 1024], F32)
        fence = ["h2_%d_%d" % (ti, k) for ti in range(5) for k in range(8)] + \
            ["aT%d_%d" % (f, ti) for f in range(22) for ti in range(5)] + ["R0", "R1", "R2", "R3", "tmp0", "tmp1"]
        dma("sp", fn_bc, fn_d.partition_broadcast(128), [], ["fn_bc"], "fn_bc")
        for t128 in range(16):
            ti = t128 // 4
            slot = t128 % 2
            b0 = slot * 2
            for c in range(8):
                tr(PS(b0 + c // 4)[:, (c % 4) * 128:(c % 4 + 1) * 128], xT[:, c, t128 * 128:(t128 + 1) * 128], identf,
                   xres(ti, c) + ["identf"], ["ps%d" % (b0 + c // 4)])
            pr = ["ps%d" % b0, "ps%d" % (b0 + 1)]
            ss, ms, rs_ = small[:, 100 + slot * 3:101 + slot * 3], small[:, 101 + slot * 3:102 + slot * 3], small[:, 102 + slot * 3:103 + slot * 3]
            act(junk, PS(b0, 2), AF.Square, pr, ["junk", "fss%d" % slot], accum=ss)
            ts("dve", ms, ss, 1.0 / 1024, EPS, ALU.mult, ALU.add, ["fss%d" % slot], ["fms%d" % slot])
            tt("pool", rs_, ms, mhalf[:, 0:1], ALU.pow, ["fms%d" % slot, "mhalf"], ["frs%d" % slot])
            stt(ob[slot], PS(b0, 2), rs_, fn_bc, ALU.mult, ALU.mult, pr + ["frs%d" % slot, "fn_bc"], ["ob%d" % slot])
            dma("sp", out_d[t128 * 128:(t128 + 1) * 128, :], ob[slot], ["ob%d" % slot], ["out%d" % t128], "out%d" % slot)
        S.op("sp", None, ["out%d" % t for t in range(16)] + list(dump_aps.values()), ())

    stop = ""
    phases = []
    for l in range(DEPTH):
        phases += [("mod%d" % l, mod_phase, l), ("a%d" % l, phase_a, l), ("pool%d" % l, pool_phase, l),
                   ("attn%d" % l, attention, l), ("c1%d" % l, phase_c1, l), ("ffn%d" % l, ffn_phase, l)]
    for name, fn, l in phases:
        fn(l)
        if name == stop:
            break
    final_phase()
    S.build()
    return nc, list(dump_aps.values())


def _host_consts():
    t = np.arange(2048)
    rows = (t // 64).astype(np.float64)
    cols = (t % 64).astype(np.float64)
    inv = 10000.0 ** (-np.arange(16, dtype=np.float64) / 16)
    ang = np.concatenate([rows[:, None] * inv, cols[:, None] * inv], axis=-1)
    cosT = np.cos(ang).astype(np.float32).reshape(16, 128, 32).transpose(1, 0, 2).copy()
    sinT = np.sin(ang).astype(np.float32).reshape(16, 128, 32).transpose(1, 0, 2).copy()
    rcb = np.zeros((128, 2, 2, 8), np.float32)
    for ch in range(2):
        for half in range(2):
            w = 2 ** (2 * ch + half + 1)
            for i in range(8):
                cl = (i + w // 2) - max(i - w // 2, 0)
                tr_ = -8 + i
                cr = min(tr_ + w // 2, 0) - (tr_ - w // 2)
                rcb[half * 64:(half + 1) * 64, ch, 0, i] = 1.0 / cl
                rcb[half * 64:(half + 1) * 64, ch, 1, i] = 1.0 / cr
    return cosT, sinT, rcb


_CACHE = {}


def kernel(x, c, ctx, c_ctx, w_mod, b_mod, norm1, norm2, w_in, q_norm, k_norm, sgu_norm, w_s, b_s, pool_w,
           pool_scale, w_out, w_gate, w_up, w_down, final_norm, _dumps=()):
    f = lambda a: np.ascontiguousarray(np.asarray(a, dtype=np.float32))
    x, c, ctx, c_ctx = f(x), f(c), f(ctx), f(c_ctx)
    key = tuple(_dumps)
    if key not in _CACHE:
        _CACHE[key] = build_program(_dumps)
    nc, dnames = _CACHE[key]
    cosT, sinT, rcb = _host_consts()
    vecs = np.concatenate([np.concatenate([f(b_mod)[l].reshape(48, 128), f(norm1)[l].reshape(8, 128),
                                           f(norm2)[l].reshape(8, 128)], 0) for l in range(DEPTH)], 0)
    shared = {
        "w_mod": f(w_mod), "vecs": np.ascontiguousarray(vecs), "pool_scale": f(pool_scale).reshape(4, 128),
        "w_in": f(w_in), "q_norm": f(q_norm), "k_norm": f(k_norm), "sgu_norm": f(sgu_norm), "w_s": f(w_s),
        "b_s": f(b_s), "pool_w": f(pool_w), "w_out": f(w_out), "w_gate": f(w_gate), "w_up": f(w_up),
        "w_down": f(w_down), "final_norm": f(final_norm), "cosT": cosT, "sinT": sinT, "rcb": rcb,
    }
    in_maps = []
    for b in range(8):
        m = dict(shared)
        m["x"] = x[b]
        m["ctx"] = ctx[b]
        m["cvec"] = np.ascontiguousarray(np.concatenate([c[b].reshape(8, 128), c_ctx.reshape(8, 128)], 0))
        in_maps.append(m)
    res = run_bass_kernel_spmd(nc, in_maps, core_ids=list(range(8)))
    out = np.stack([np.asarray(r["out"], dtype=np.float32) for r in res.results], 0)
    if _dumps:
        return out, [{d: np.asarray(r[d]) for d in dnames} for r in res.results]
    return out
```
